# Optimizing a Trainium2 kernel written in Bass

```python
import math
import jax
import jax.numpy as jnp
from jax import lax
import numpy as np

D_MODEL = 1024
BATCH = 16
SEQ = 4096
DEPTH = 1
DEC_BATCH = 16
DEC_SEQ = 64
PAST_LEN = 1024

CHUNK = 64
RMS_EPS = 1e-6
RW_HEADS = 8
RW_HEAD_DIM = 64
RW_WIDTH = RW_HEADS * RW_HEAD_DIM
LORA_W = 64
LORA_A = 64
LORA_G = 128
RW_COLS = 3 * RW_WIDTH + LORA_W + LORA_A + LORA_G
GN_EPS = 64e-5
ATT_Q_HEADS = 8
ATT_KV_HEADS = 2
ATT_HEAD_DIM = 64
ATT_GROUP = ATT_Q_HEADS // ATT_KV_HEADS
ATT_Q_WIDTH = ATT_Q_HEADS * ATT_HEAD_DIM
ATT_KV_WIDTH = ATT_KV_HEADS * ATT_HEAD_DIM
ATT_COLS = ATT_Q_WIDTH + 2 * ATT_KV_WIDTH
WINDOW = 128
N_PREV_CHUNKS = WINDOW // CHUNK
N_BRANCH = 2
BRANCH_WIDTH = 512
GATE_COLS = N_BRANCH * D_MODEL
IN_COLS = RW_COLS + ATT_COLS + GATE_COLS
O_R = 0
O_K = O_R + RW_WIDTH
O_V = O_K + RW_WIDTH
O_W = O_V + RW_WIDTH
O_A = O_W + LORA_W
O_G = O_A + LORA_A
O_GATE = RW_COLS + ATT_COLS
PEER_HEADS = 8
PEER_NKEYS = 128
PEER_EXPERTS = PEER_NKEYS * PEER_NKEYS
PEER_KEY_DIM = 128
PEER_HALF = PEER_KEY_DIM // 2
PEER_TOPK = 16
PEER_BLOCK = 256

kernel_name = "rwkv7_swa_peer_streaming_step"


def rms_norm(x, g):
    xf = x.astype(jnp.float32)
    y = xf * lax.rsqrt(jnp.mean(xf * xf, axis=-1, keepdims=True) + RMS_EPS)
    return (y * g.astype(jnp.float32)).astype(x.dtype)


def project_inputs(h, h_prev, w_in, mu_shift):
    p = jnp.einsum('btd,dc->btc', h, w_in)
    p_rw = p[..., :RW_COLS]
    prev_row = jnp.einsum('bd,dc->bc', h_prev.astype(h.dtype), w_in[:, :RW_COLS])
    p_shift = jnp.concatenate([prev_row[:, None], p_rw[:, :-1]], axis=1)
    p_rw = p_rw + (p_shift - p_rw) * mu_shift
    return p_rw, p[..., RW_COLS:O_GATE], p[..., O_GATE:]


def rwkv7_branch(p_rw, wkv0, w0, w2, a0, a2, g2, k_k, k_a, r_k, lnx_w, lnx_b):
    B, T, _ = p_rw.shape
    f = p_rw.astype(jnp.float32)
    r = f[..., O_R:O_K]
    k = f[..., O_K:O_V]
    v = f[..., O_V:O_W]
    xw = f[..., O_W:O_A]
    xa = f[..., O_A:O_G]
    xg = f[..., O_G:RW_COLS]
    w = -jax.nn.softplus(-(w0 + jnp.tanh(xw) @ w2)) - 0.5
    decay = jnp.exp(-jnp.exp(w))
    a = jax.nn.sigmoid(a0 + xa @ a2)
    g = jax.nn.sigmoid(xg) @ g2
    heads = lambda t: t.reshape(B, T, RW_HEADS, RW_HEAD_DIM)
    kk = heads(k * k_k)
    kk = kk / jnp.maximum(jnp.sqrt(jnp.sum(kk * kk, axis=-1, keepdims=True)), 1e-12)
    k = k * (1.0 + (a - 1.0) * k_a)
    r_h, k_h, v_h, d_h, a_h = heads(r), heads(k), heads(v), heads(decay), heads(a)
    b_h = kk * a_h

    def step(S, inp):
        r_t, d_t, k_t, v_t, kk_t, b_t = inp
        sa = jnp.einsum('bhij,bhj->bhi', S, -kk_t)
        S = S * d_t[:, :, None, :] + sa[..., None] * b_t[:, :, None, :] + v_t[..., None] * k_t[:, :, None, :]
        y_t = jnp.einsum('bhij,bhj->bhi', S, r_t)
        return S, y_t

    xs = tuple(jnp.moveaxis(t, 1, 0) for t in (r_h, d_h, k_h, v_h, kk, b_h))
    S_fin, ys = lax.scan(step, wkv0.astype(jnp.float32), xs)
    y = jnp.moveaxis(ys, 0, 1)
    mean = jnp.mean(y, axis=-1, keepdims=True)
    var = jnp.mean(jnp.square(y - mean), axis=-1, keepdims=True)
    y = (y - mean) * lax.rsqrt(var + GN_EPS)
    y = y * lnx_w.astype(jnp.float32).reshape(RW_HEADS, RW_HEAD_DIM) + lnx_b.astype(jnp.float32).reshape(RW_HEADS, RW_HEAD_DIM)
    y = y + jnp.sum(r_h * k_h * r_k.astype(jnp.float32), axis=-1, keepdims=True) * v_h
    out = y.reshape(B, T, RW_WIDTH) * g
    return out.astype(p_rw.dtype), S_fin


def attn_core(q, k, v, q_pos, k_pos, k_valid, sinks):
    scale = ATT_HEAD_DIM ** -0.5
    s = jnp.einsum('bnqhgd,bnkhd->bnhgqk', q.astype(jnp.float32), k.astype(jnp.float32)) * scale
    slopes = jnp.exp2(-8.0 * jnp.arange(1, ATT_Q_HEADS + 1, dtype=jnp.float32) / ATT_Q_HEADS)
    slopes = slopes.reshape(ATT_KV_HEADS, ATT_GROUP)[None, None, :, :, None, None]
    dist = jnp.abs(q_pos[:, :, None] - k_pos[:, None, :]).astype(jnp.float32)
    s = s - slopes * dist[None, :, None, None, :, :]
    s = jnp.where(k_valid[None, :, None, None, None, :], s, -1e30)
    sink = sinks.astype(jnp.float32).reshape(ATT_KV_HEADS, ATT_GROUP)[None, None, :, :, None, None]
    m = jnp.maximum(jnp.max(s, axis=-1, keepdims=True), sink)
    p = jnp.exp(s - m)
    denom = jnp.sum(p, axis=-1, keepdims=True) + jnp.exp(sink - m)
    return jnp.einsum('bnhgqk,bnkhd->bnqhgd', p / denom, v.astype(jnp.float32))


def split_attn_cols(p_att):
    B, T, _ = p_att.shape
    q = p_att[..., :ATT_Q_WIDTH].reshape(B, T, ATT_KV_HEADS, ATT_GROUP, ATT_HEAD_DIM)
    k = p_att[..., ATT_Q_WIDTH:ATT_Q_WIDTH + ATT_KV_WIDTH].reshape(B, T, ATT_KV_HEADS, ATT_HEAD_DIM)
    v = p_att[..., ATT_Q_WIDTH + ATT_KV_WIDTH:].reshape(B, T, ATT_KV_HEADS, ATT_HEAD_DIM)
    return q, k, v


def attn_prompt(p_att, sinks):
    B, T, _ = p_att.shape
    nC = T // CHUNK
    q, k, v = split_attn_cols(p_att)
    qb = q.reshape(B, nC, CHUNK, ATT_KV_HEADS, ATT_GROUP, ATT_HEAD_DIM)
    pad = ((0, 0), (N_PREV_CHUNKS, 0), (0, 0), (0, 0), (0, 0))
    kp = jnp.pad(k.reshape(B, nC, CHUNK, ATT_KV_HEADS, ATT_HEAD_DIM), pad)
    vp = jnp.pad(v.reshape(B, nC, CHUNK, ATT_KV_HEADS, ATT_HEAD_DIM), pad)
    kband = jnp.concatenate([kp[:, j:j + nC] for j in range(N_PREV_CHUNKS + 1)], axis=2)
    vband = jnp.concatenate([vp[:, j:j + nC] for j in range(N_PREV_CHUNKS + 1)], axis=2)
    c = jnp.arange(nC)
    q_pos = c[:, None] * CHUNK + jnp.arange(CHUNK)[None, :]
    k_pos = (c[:, None] - N_PREV_CHUNKS) * CHUNK + jnp.arange((N_PREV_CHUNKS + 1) * CHUNK)[None, :]
    o = attn_core(qb, kband, vband, q_pos, k_pos, k_pos >= 0, sinks)
    keep = min(WINDOW, T)
    return o.reshape(B, T, ATT_Q_WIDTH).astype(p_att.dtype), k[:, T - keep:], v[:, T - keep:]


def attn_sample(p_att, cache_k, cache_v, sinks):
    B, S, _ = p_att.shape
    L = cache_k.shape[1]
    q, k, v = split_attn_cols(p_att)
    k_all = jnp.concatenate([cache_k.astype(k.dtype), k], axis=1)[:, None]
    v_all = jnp.concatenate([cache_v.astype(v.dtype), v], axis=1)[:, None]
    q_pos = (PAST_LEN + jnp.arange(S))[None, :]
    k_pos = (PAST_LEN - L + jnp.arange(L + S))[None, :]
    valid = jnp.ones((1, L + S), dtype=bool)
    o = attn_core(q[:, None], k_all, v_all, q_pos, k_pos, valid, sinks)
    return o.reshape(B, S, ATT_Q_WIDTH).astype(p_att.dtype), k, v


def merge_branches(o_rw, o_att, p_gate, w_branch, w_out):
    B, T, _ = p_gate.shape
    gates = jax.nn.sigmoid(p_gate.astype(jnp.float32)).astype(o_rw.dtype).reshape(B, T, N_BRANCH, D_MODEL)
    merged = gates[:, :, 0] * (o_rw @ w_branch[0]) + gates[:, :, 1] * (o_att @ w_branch[1])
    return merged @ w_out


def peer(h, w_q, sub_keys, u, v):
    B, T, D = h.shape
    n = B * T
    n_pad = (-n) % PEER_BLOCK
    blocks = jnp.pad(h.reshape(n, D), ((0, n_pad), (0, 0))).reshape(-1, PEER_BLOCK, D)

    def one_block(hb):
        q = (hb @ w_q).reshape(PEER_BLOCK, PEER_HEADS, 2, PEER_HALF).astype(jnp.float32)
        s = jnp.einsum('bhpk,pnk->bhpn', q, sub_keys.astype(jnp.float32))
        s1, i1 = lax.top_k(s[:, :, 0], PEER_TOPK)
        s2, i2 = lax.top_k(s[:, :, 1], PEER_TOPK)
        cand = (s1[..., :, None] + s2[..., None, :]).reshape(PEER_BLOCK, PEER_HEADS, PEER_TOPK * PEER_TOPK)
        cidx = (i1[..., :, None] * PEER_NKEYS + i2[..., None, :]).reshape(PEER_BLOCK, PEER_HEADS, PEER_TOPK * PEER_TOPK)
        best, pos = lax.top_k(cand, PEER_TOPK)
        idx = jnp.take_along_axis(cidx, pos, axis=-1)
        gate = jax.nn.softmax(best, axis=-1)
        ue = u[idx]
        ve = v[idx]
        act = jax.nn.gelu(jnp.einsum('bhkd,bd->bhk', ue, hb).astype(jnp.float32), approximate=False) * gate
        return jnp.einsum('bhk,bhkd->bd', act.astype(hb.dtype), ve)

    out = lax.map(one_block, blocks).reshape(-1, D)[:n]
    return out.reshape(B, T, D)


def trunk_layer(x, h_prev, wkv0, cache_k, cache_v, norm1_g, w_in, mu_shift, w_decay0, w_decay_lora,
                a_icl0, a_icl_lora, g_lora, k_k, k_a, r_k, lnx_w, lnx_b, attn_sinks, w_branch, w_out,
                norm2_g, peer_wq, peer_sub_keys, peer_u, peer_v):
    h = rms_norm(x, norm1_g)
    p_rw, p_att, p_gate = project_inputs(h, h_prev, w_in, mu_shift)
    o_rw, wkv_new = rwkv7_branch(p_rw, wkv0, w_decay0, w_decay_lora, a_icl0, a_icl_lora, g_lora,
                                 k_k, k_a, r_k, lnx_w, lnx_b)
    if cache_k is None:
        o_att, k_new, v_new = attn_prompt(p_att, attn_sinks)
    else:
        o_att, k_new, v_new = attn_sample(p_att, cache_k, cache_v, attn_sinks)
    x = x + merge_branches(o_rw, o_att, p_gate, w_branch, w_out)
    x = x + peer(rms_norm(x, norm2_g), peer_wq, peer_sub_keys, peer_u, peer_v)
    return x, h[:, -1], wkv_new.astype(x.dtype), k_new, v_new


def setup_inputs(seed: int = 0) -> dict:
    key = jax.random.key(seed)
    ks = iter(jax.random.split(key, 40))
    nrm = lambda shape, scale: jax.random.normal(next(ks), shape, jnp.float32) * scale
    att_cache = min(WINDOW, PAST_LEN)
    D = D_MODEL
    return {
        "x_prompt": nrm((BATCH, SEQ, D), 1.0),
        "x_sample": nrm((DEC_BATCH, DEC_SEQ, D), 1.0),
        "state_shift": nrm((DEPTH, DEC_BATCH, D), 1.0),
        "state_wkv": nrm((DEPTH, DEC_BATCH, RW_HEADS, RW_HEAD_DIM, RW_HEAD_DIM), 0.5),
        "cache_k": nrm((DEPTH, DEC_BATCH, att_cache, ATT_KV_HEADS, ATT_HEAD_DIM), 1.0),
        "cache_v": nrm((DEPTH, DEC_BATCH, att_cache, ATT_KV_HEADS, ATT_HEAD_DIM), 1.0),
        "norm1_g": 1.0 + nrm((DEPTH, D), 0.02),
        "w_in": nrm((DEPTH, D, IN_COLS), D ** -0.5),
        "mu_shift": jax.random.uniform(next(ks), (DEPTH, RW_COLS), jnp.float32),
        "w_decay0": jax.random.uniform(next(ks), (DEPTH, RW_WIDTH), jnp.float32, minval=-6.0, maxval=1.0),
        "w_decay_lora": nrm((DEPTH, LORA_W, RW_WIDTH), 0.1),
        "a_icl0": nrm((DEPTH, RW_WIDTH), 0.1),
        "a_icl_lora": nrm((DEPTH, LORA_A, RW_WIDTH), 0.1),
        "g_lora": nrm((DEPTH, LORA_G, RW_WIDTH), LORA_G ** -0.5),
        "k_k": 0.85 + nrm((DEPTH, RW_WIDTH), 0.02),
        "k_a": 1.0 + nrm((DEPTH, RW_WIDTH), 0.02),
        "r_k": nrm((DEPTH, RW_HEADS, RW_HEAD_DIM), 0.1),
        "lnx_w": 1.0 + nrm((DEPTH, RW_WIDTH), 0.02),
        "lnx_b": nrm((DEPTH, RW_WIDTH), 0.02),
        "attn_sinks": nrm((DEPTH, ATT_Q_HEADS), 0.5),
        "w_branch": nrm((DEPTH, N_BRANCH, BRANCH_WIDTH, D), BRANCH_WIDTH ** -0.5),
        "w_out": nrm((DEPTH, D, D), D ** -0.5),
        "norm2_g": 1.0 + nrm((DEPTH, D), 0.02),
        "peer_wq": nrm((DEPTH, D, PEER_HEADS * PEER_KEY_DIM), D ** -0.5),
        "peer_sub_keys": nrm((DEPTH, 2, PEER_NKEYS, PEER_HALF), PEER_HALF ** -0.5),
        "peer_u": nrm((DEPTH, PEER_EXPERTS, D), D ** -0.5),
        "peer_v": nrm((DEPTH, PEER_EXPERTS, D), (PEER_HEADS * PEER_TOPK) ** -0.5),
        "final_g": 1.0 + nrm((D,), 0.02),
    }


def reference(x_prompt, x_sample, state_shift, state_wkv, cache_k, cache_v, norm1_g, w_in, mu_shift,
              w_decay0, w_decay_lora, a_icl0, a_icl_lora, g_lora, k_k, k_a, r_k, lnx_w, lnx_b,
              attn_sinks, w_branch, w_out, norm2_g, peer_wq, peer_sub_keys, peer_u, peer_v, final_g):
    yp = x_prompt
    ys = x_sample
    shift_p, wkv_p, k_p, v_p = [], [], [], []
    shift_s, wkv_s, k_s, v_s = [], [], [], []
    for l in range(DEPTH):
        lp = (norm1_g[l], w_in[l], mu_shift[l], w_decay0[l], w_decay_lora[l], a_icl0[l], a_icl_lora[l],
              g_lora[l], k_k[l], k_a[l], r_k[l], lnx_w[l], lnx_b[l], attn_sinks[l], w_branch[l], w_out[l],
              norm2_g[l], peer_wq[l], peer_sub_keys[l], peer_u[l], peer_v[l])
        h0 = jnp.zeros((yp.shape[0], D_MODEL), yp.dtype)
        s0 = jnp.zeros((yp.shape[0], RW_HEADS, RW_HEAD_DIM, RW_HEAD_DIM), jnp.float32)
        yp, a1, a2, a3, a4 = trunk_layer(yp, h0, s0, None, None, *lp)
        shift_p.append(a1)
        wkv_p.append(a2)
        k_p.append(a3)
        v_p.append(a4)
        ys, b1, b2, b3, b4 = trunk_layer(ys, state_shift[l], state_wkv[l], cache_k[l], cache_v[l], *lp)
        shift_s.append(b1)
        wkv_s.append(b2)
        k_s.append(b3)
        v_s.append(b4)
    y_prompt = rms_norm(yp, final_g)
    y_sample = rms_norm(ys, final_g)
    return (y_prompt, y_sample, jnp.stack(shift_p), jnp.stack(wkv_p), jnp.stack(k_p), jnp.stack(v_p),
            jnp.stack(shift_s), jnp.stack(wkv_s), jnp.stack(k_s), jnp.stack(v_s))
```

```python
import contextlib
import math
import numpy as np
import concourse.bass as bass
import concourse.mybir as mybir
from concourse.bass_utils import run_bass_kernel_spmd

F32 = mybir.dt.float32
BF16 = mybir.dt.bfloat16
U32 = mybir.dt.uint32
AF = mybir.ActivationFunctionType
ALU = mybir.AluOpType
AX = mybir.AxisListType

COMPUTE = ("tensor", "vector", "scalar", "gpsimd")
D = 1024
RW_COLS = 1792
EXPM05 = math.exp(-0.5)


class P64:
    def __init__(self, t):
        self.t = t

    def __getitem__(self, idx):
        if not isinstance(idx, tuple):
            idx = (idx,)
        assert idx[0] == slice(None)
        return self.t[(slice(0, 64),) + tuple(idx[1:])]


class P3:
    def __init__(self, t):
        self.t = t

    def __getitem__(self, idx):
        v = self.t[:].rearrange("p (a t) -> p a t", a=8)
        return v[idx]


class VW:
    def __init__(self, ap):
        self.ap = ap

    def __getitem__(self, idx):
        return self.ap[idx]


class Buf:
    __slots__ = ("name", "w", "r", "excl")

    def __init__(self, name="", excl=False):
        self.name = name
        self.w = None
        self.r = []
        self.excl = excl


class Op:
    __slots__ = ("eng", "fn", "deps", "isdma", "sem", "semval", "needinc")

    def __init__(self, eng, fn, isdma):
        self.eng = eng
        self.fn = fn
        self.isdma = isdma
        self.deps = set()
        self.needinc = False
        self.sem = None
        self.semval = None


class Sched:
    NDMASEM = 16

    def __init__(self, nc):
        self.nc = nc
        self.ops = []

    def op(self, eng, fn, reads=(), writes=(), dma=False):
        idx = len(self.ops)
        o = Op(eng, fn, dma)
        reads = list(reads)
        writes = list(writes)
        for b in list(reads):
            if b.excl:
                reads.remove(b)
                if b not in writes:
                    writes.append(b)
        for b in reads:
            if b.w is not None:
                o.deps.add(b.w)
        for b in writes:
            if b.w is not None:
                o.deps.add(b.w)
            for r in b.r:
                o.deps.add(r)
        for b in reads:
            b.r.append(idx)
        for b in writes:
            b.w = idx
            b.r = []
        o.deps.discard(idx)
        self.ops.append(o)
        return idx

    def emit(self):
        nc = self.nc
        ops = self.ops
        engs = []
        for o in ops:
            if o.eng not in engs:
                engs.append(o.eng)
        for o in ops:
            for d in list(o.deps):
                po = ops[d]
                if (not po.isdma) and (not o.isdma) and po.eng == "tensor" and o.eng == "tensor":
                    o.deps.discard(d)
        with contextlib.ExitStack() as st:
            csem = {e: st.enter_context(nc.semaphore("cs_" + e)) for e in COMPUTE}
            dengs = set(o.eng for o in ops if o.isdma)
            dsems = {e: ([st.enter_context(nc.semaphore("ds_%s_%d" % (e, k))) for k in range(self.NDMASEM)] if e in dengs else [])
                     for e in engs}
            dcount = {e: 0 for e in engs}
            dhist = {e: [[] for _ in range(self.NDMASEM)] for e in engs}
            for i, o in enumerate(ops):
                if o.isdma:
                    k = dcount[o.eng] % self.NDMASEM
                    dcount[o.eng] += 1
                    o.sem = dsems[o.eng][k]
                    h = dhist[o.eng][k]
                    if h:
                        o.deps.add(h[-1])
                    h.append(i)
                    o.semval = 16 * len(h)
            for o in ops:
                for d in o.deps:
                    ops[d].needinc = True
            ccount = {e: 0 for e in COMPUTE}
            for o in ops:
                if (not o.isdma) and o.needinc:
                    ccount[o.eng] += 1
                    o.sem = csem[o.eng]
                    o.semval = ccount[o.eng]
            per_eng = {e: [] for e in engs}
            for i, o in enumerate(ops):
                per_eng[o.eng].append(i)
            self.stats = {e: len(per_eng[e]) for e in engs}
            block = st.enter_context(nc.Block())

            def make(e):
                def body(engine):
                    seen = {}
                    for i in per_eng[e]:
                        o = ops[i]
                        need = {}
                        for d in o.deps:
                            po = ops[d]
                            if po.sem is None:
                                continue
                            key = id(po.sem)
                            if key not in need or need[key][1] < po.semval:
                                need[key] = (po.sem, po.semval)
                        for key, (sem, val) in need.items():
                            if seen.get(key, 0) >= val:
                                continue
                            engine.wait_ge(sem, val)
                            seen[key] = val
                        ins = o.fn(engine)
                        if o.isdma:
                            ins.then_inc(o.sem, 16)
                        elif o.needinc:
                            ins.then_inc(o.sem, 1)
                    for k in range(len(dsems[e])):
                        h = dhist[e][k]
                        if h and seen.get(id(dsems[e][k]), 0) < 16 * len(h):
                            engine.wait_ge(dsems[e][k], 16 * len(h))
                return body

            for e in engs:
                getattr(block, e)(make(e))


NCST = 448 + 1536


def make_consts():
    c = np.zeros((128, NCST), np.float32)
    c[:, 0:128] = np.eye(128, dtype=np.float32)
    c[:, 128:256] = np.arange(128, dtype=np.float32)[None, :]
    s = np.arange(64)[:, None]
    t = np.arange(64)[None, :]
    c[0:64, 256:320] = (s < t)
    c[0:64, 320:384] = (s <= t)
    c[0:64, 384:448] = (t < s)
    slopes = np.exp2(-8.0 * np.arange(1, 9, dtype=np.float64) / 8.0)
    al = np.zeros((64, 2, 3, 4, 64), np.float64)
    tk = np.arange(64)[:, None]
    tq = np.arange(64)[None, :]
    for kvh in range(2):
        for kc in range(3):
            for g in range(4):
                dist = np.abs(128 - 64 * kc + tq - tk)
                al[:, kvh, kc, g, :] = np.exp(-slopes[kvh * 4 + g] * dist)
    c[0:64, 448:448 + 1536] = al.reshape(64, 1536)
    return c


NPAR = 67
DBG_STOP = None
DBG_SEQS = (0, 1)


class _Stop(Exception):
    pass


_HITS = {}


def ckpt(n):
    if DBG_STOP is None:
        return
    _HITS[n] = _HITS.get(n, 0) + 1
    if DBG_STOP % 100 == n and _HITS[n] - 1 == DBG_STOP // 100:
        raise _Stop()


def build(NTP, do_peer=True):
    SEQ = NTP * 128
    nc = bass.Bass("TRN2", target_bir_lowering=False)
    S = Sched(nc)

    def din(name, shape, dt=F32):
        return nc.dram_tensor(name, list(shape), dt, kind="ExternalInput").ap()

    def dout(name, shape):
        return nc.dram_tensor(name, list(shape), F32, kind="ExternalOutput").ap()

    def dscr(name, shape, dt=BF16):
        return nc.dram_tensor(name, list(shape), dt, kind="Internal").ap()

    xp = din("xp", [2, SEQ, D]); xs = din("xs", [2, 64, D])
    st_shift = din("st_shift", [2, D]); st_wkv = din("st_wkv", [2, 8, 64, 64])
    ck = din("ck", [2, 128, 128]); cv = din("cv", [2, 128, 128])
    cst = din("cst", [128, NCST]); par = din("par", [128, NPAR])
    n1g = din("n1g", [D]); n2g = din("n2g", [D]); fg = din("fg", [D])
    mu_row = din("mu_row", [14 * 128])
    win_f = din("win_f", [128, 36, 1024])
    w0row = din("w0row", [1, 512]); a0row = din("a0row", [1, 512])
    lw = din("lw", [64, 512]); la = din("la", [64, 512]); gl = din("gl", [128, 512])
    lnw = din("lnw", [512]); lnb = din("lnb", [512]); sinks = din("sinks", [8])
    wb0_f = din("wb0_f", [128, 4, 1024]); wb1_f = din("wb1_f", [64, 8, 1024])
    wo_f = din("wo_f", [128, 8, 1024])
    if do_peer:
        wq_f = din("wq_f", [128, 8, 1024]); keys_f = din("keys_f", [128, 256])
        ut_f = din("ut_f", [128, 128, 1024]); pv_f = din("pv_f", [128, 128, 1024])

    y_p = dout("y_p", [2, SEQ, D]); y_s = dout("y_s", [2, 64, D])
    shift_p = dout("shift_p", [2, D]); wkv_p = dout("wkv_p", [2, 8, 64, 64])
    k_p = dout("k_p", [2, 128, 128]); v_p = dout("v_p", [2, 128, 128])
    shift_s = dout("shift_s", [2, D]); wkv_s = dout("wkv_s", [2, 8, 64, 64])
    k_s = dout("k_s", [2, 64, 128]); v_s = dout("v_s", [2, 64, 128])

    win_s = dscr("win_s", [128, 50, 1024])
    wb0_s = dscr("wb0_s", [128, 4, 1024]); wb1_s = dscr("wb1_s", [64, 8, 1024]); wo_s = dscr("wo_s", [128, 8, 1024])
    if do_peer:
        wq_s = dscr("wq_s", [128, 8, 1024]); ut_s = dscr("ut_s", [128, 128, 1024]); pv_s = dscr("pv_s", [128, 128, 1024])

    with contextlib.ExitStack() as es:
        def sb(name, shape, dt=F32):
            return es.enter_context(nc.sbuf_tensor(name, list(shape), dt))

        pbank = [es.enter_context(nc.psum_tensor("pb%d" % k, [128, 512], F32)) for k in range(8)]
        PB = [Buf("pb%d" % k, excl=True) for k in range(8)]

        def V(fn, r=(), w=()): S.op("vector", fn, r, w)
        def A(fn, r=(), w=()): S.op("scalar", fn, r, w)
        def G(fn, r=(), w=()): S.op("gpsimd", fn, r, w)
        def PE(fn, r=(), w=()): S.op("tensor", fn, r, w)
        def DMA(fn, r=(), w=(), q="sync"): S.op(q, fn, r, w, dma=True)

        CST = sb("CST", [128, NCST]); bCST = Buf()
        PAR = sb("PAR", [128, NPAR]); bPAR = Buf()
        DMA(lambda e: e.dma_start(out=CST[:], in_=cst), w=[bCST])
        DMA(lambda e: e.dma_start(out=PAR[:], in_=par), w=[bPAR])
        IDF = CST[:, 0:128]
        IOTA = CST[:, 128:256]
        M1 = CST[0:64, 256:384]
        ML = CST[0:64, 384:448]
        ALIBI = CST[0:64, 448:448 + 1536]
        IDB = sb("IDB", [128, 128], BF16); bIDB = Buf()
        V(lambda e: e.tensor_copy(out=IDB[:], in_=IDF), r=[bCST], w=[bIDB])
        ONESF = sb("ONESF", [128, 128]); bONES = Buf()
        ONESB = sb("ONESB", [64, 64], BF16)
        V(lambda e: e.memset(ONESF[:], 1.0), w=[bONES])
        V(lambda e: e.memset(ONESB[:], 1.0), w=[bONES])
        G1B = sb("G1B", [128, D]); G2B = sb("G2B", [128, D]); GFB = sb("GFB", [128, D]); bGB = Buf()
        DMA(lambda e: e.dma_start(out=G1B[:], in_=n1g.partition_broadcast(128)), w=[bGB])
        DMA(lambda e: e.dma_start(out=G2B[:], in_=n2g.partition_broadcast(128)), w=[bGB])
        DMA(lambda e: e.dma_start(out=GFB[:], in_=fg.partition_broadcast(128)), w=[bGB])
        LNW = sb("LNW", [64, 512]); LNB = sb("LNB", [64, 512]); SNK = sb("SNK", [64, 8]); bLN = Buf()
        DMA(lambda e: e.dma_start(out=LNW[:], in_=lnw.partition_broadcast(64)), w=[bLN])
        DMA(lambda e: e.dma_start(out=LNB[:], in_=lnb.partition_broadcast(64)), w=[bLN])
        DMA(lambda e: e.dma_start(out=SNK[:], in_=sinks.partition_broadcast(64)), w=[bLN])
        SNKE = sb("SNKE", [64, 8]); bSNK = Buf()
        A(lambda e: e.activation(out=SNKE[:], in_=SNK[:], func=AF.Exp), r=[bLN], w=[bSNK])
        LWX = sb("LWX", [128, 1024]); GL = sb("GL", [128, 512]); bLO = Buf()
        DMA(lambda e: e.dma_start(out=LWX[64:65, 0:512], in_=w0row), w=[bLO])
        DMA(lambda e: e.dma_start(out=LWX[64:65, 512:1024], in_=a0row), w=[bLO])
        DMA(lambda e: e.dma_start(out=LWX[0:64, 0:512], in_=lw), w=[bLO])
        DMA(lambda e: e.dma_start(out=LWX[0:64, 512:1024], in_=la), w=[bLO])
        DMA(lambda e: e.dma_start(out=GL[:], in_=gl), w=[bLO])
        P_KK = PAR[0:64, 27:35]; P_KA = PAR[0:64, 35:43]; P_RK = PAR[0:64, 59:67]

        bWIN = [Buf() for _ in range(50)]
        X = sb("X", [128, D]); bX = Buf()
        HB = sb("HB", [128, D], BF16); bHB = Buf()
        JNK = HB; bJNK = bHB
        MRG = sb("MRG", [128, 8, 128], BF16); bMRG = Buf()
        MRGF = VW(MRG[:].rearrange("p a t -> p (a t)"))
        KRf = sb("KRf", [128, 8, 2, 2, 64]); BKf = sb("BKf", [128, 8, 2, 2, 64]); bKR = Buf(); bBK = Buf()
        KR = P64(KRf); BK = P64(BKf)
        MUB = KRf[:].rearrange("p a b c d -> p (a b c d)"); OMUB = BKf[:].rearrange("p a b c d -> p (a b c d)")
        bMU = Buf()
        DMA(lambda e: e.dma_start(out=MUB[:, 0:1792], in_=mu_row.partition_broadcast(128)), w=[bMU, bKR, bBK])
        V(lambda e: e.tensor_scalar(out=OMUB[:, 0:1792], in0=MUB[:, 0:1792], scalar1=-1.0, scalar2=1.0, op0=ALU.mult, op1=ALU.add),
          r=[bMU], w=[bMU])
        STGl = [X]; bSTG = [bX]
        STOl = [HB, MRGF]; bSTO = [bHB, bMRG]
        nst = [0]
        for s in range(14):
            k = 0
            DMA((lambda s, k: lambda e: e.dma_start(out=STGl[k][:], in_=win_f[:, s, :]))(s, k), w=[bSTG[k]])
            for half, (MM, dst) in enumerate(((OMUB, s), (MUB, 36 + s))):
                o = nst[0] % 2
                nst[0] += 1
                eng = V if half == 0 else G
                eng((lambda s, k, o, MM: lambda e: e.tensor_tensor(
                    out=STOl[o][:].rearrange("p (a m) -> p a m", a=8),
                    in0=STGl[k][:].rearrange("p (a m) -> p a m", a=8),
                    in1=MM[:, s * 128:(s + 1) * 128].unsqueeze(1).to_broadcast([128, 8, 128]),
                    op=ALU.mult))(s, k, o, MM), r=[bSTG[k], bMU, bKR, bBK], w=[bSTO[o]])
                DMA((lambda o, dst: lambda e: e.dma_start(out=win_s[:, dst, :], in_=STOl[o][:]))(o, dst),
                    r=[bSTO[o]], w=[bWIN[dst]])
        for s0 in range(14, 36, 11):
            DMA((lambda s0: lambda e: e.dma_start(out=win_s[:, s0:s0 + 11, :], in_=win_f[:, s0:s0 + 11, :]))(s0),
                w=[bWIN[s] for s in range(s0, s0 + 11)], q="gpsimd")
        bWB0 = Buf(); bWB1 = Buf(); bWO = Buf()
        DMA(lambda e: e.dma_start(out=wb0_s, in_=wb0_f), w=[bWB0], q="gpsimd")
        DMA(lambda e: e.dma_start(out=wb1_s, in_=wb1_f), w=[bWB1], q="gpsimd")
        DMA(lambda e: e.dma_start(out=wo_s, in_=wo_f), w=[bWO], q="gpsimd")
        if do_peer:
            bWQ = Buf()
            DMA(lambda e: e.dma_start(out=wq_s, in_=wq_f), w=[bWQ], q="gpsimd")
            bUT = [Buf() for _ in range(8)]; bPV = [Buf() for _ in range(8)]
            for k in range(8):
                DMA((lambda k: lambda e: e.dma_start(out=ut_s[:, 16 * k:16 * k + 16, :], in_=ut_f[:, 16 * k:16 * k + 16, :]))(k),
                    w=[bUT[k]], q="gpsimd")
                DMA((lambda k: lambda e: e.dma_start(out=pv_s[:, 16 * k:16 * k + 16, :], in_=pv_f[:, 16 * k:16 * k + 16, :]))(k),
                    w=[bPV[k]], q="gpsimd")
            KEYF = sb("KEYF", [128, 256]); KEYB = sb("KEYB", [128, 256], BF16); bKEY = Buf()
            DMA(lambda e: e.dma_start(out=KEYF[:], in_=keys_f), w=[bKEY])
            V(lambda e: e.tensor_copy(out=KEYB[:], in_=KEYF[:]), r=[bKEY], w=[bKEY])

        stopped = [False]
        NS = 8
        RING = sb("RING", [128, NS, 1024], BF16)
        bRING = [Buf() for _ in range(NS)]
        rcount = [0]

        def wload(src_ap, deps, npart=128):
            k = rcount[0] % NS
            rcount[0] += 1
            DMA(lambda e: e.dma_start(out=RING[0:npart, k, :], in_=src_ap), r=list(deps), w=[bRING[k]])
            return k

        SSQ = sb("SSQ", [128, 4]); bSSQ = Buf()
        HF = sb("HF", [128, D]); bHF = Buf()
        HT = sb("HT", [128, 8, 2, 65], BF16); bHT = Buf()
        CAR = sb("CAR", [128, 8, 1], BF16); bCAR = Buf()
        STS = sb("STS", [128, 2, 8]); bSTS = Buf()
        BIGA = sb("BIGA", [128, 8192]); BIGB = sb("BIGB", [128, 4096])
        def carveA(i, three):
            a = BIGA[0:64, i * 1024:(i + 1) * 1024]
            return VW(a.rearrange("p (h t) -> p h t", h=8) if three else a)
        def carveB(i):
            return VW(BIGB[0:64, i * 1024:(i + 1) * 1024].rearrange("p (h t) -> p h t", h=8))
        RF = carveA(0, True); KF = carveA(1, True); VT = carveA(2, True)
        bRF = Buf(); bKF = Buf(); bVT = Buf()
        TW = sb("TW", [64, 128]); XA = sb("XA", [64, 128]); SG = sb("SG", [128, 128]); bTW = Buf(); bXA = Buf(); bSG = Buf()
        LD = carveA(3, False); CC = carveA(4, False); CE = carveA(5, False); CUM = carveA(6, False)
        CUMX = CE; EP = CC; EPX = CE; EM = CUM
        bLD = Buf(); bCC = Buf(); bCE = Buf(); bCUM = Buf(); bCUMX = bCE; bEP = bCC; bEPX = bCE; bEM = bCUM
        AS = carveA(7, True); bAS = Buf()
        KKR = carveB(0); SQ = carveB(1); RN = SQ; KK = KKR
        bKKR = Buf(); bSQ = Buf(); bRN = bSQ; bKK = bKKR
        T1 = carveB(2); KP = T1; BB = carveB(3); RKP = sb("RKP", [64, 8, 128])
        bT1 = Buf(); bKP = bT1; bBB = Buf(); bRKP = Buf()
        QT = sb("QT", [64, 8, 128], BF16); bQT = Buf()
        KTR = sb("KTR", [64, 2, 6, 64], BF16); bKTR = [Buf() for _ in range(6)]
        VAR = sb("VAR", [64, 6, 128], BF16); bVAR = [Buf() for _ in range(6)]
        KTF = sb("KTF", [64, 2, 128]); bKTF = Buf()
        VAF = sb("VAF", [64, 2, 128]); bVAF = Buf()
        KTOK = sb("KTOK", [128, 128]); bKTOK = Buf()
        SGT = sb("SGT", [128, 16, 128], BF16); bSGT = Buf()
        def scanbuf(name):
            f = sb(name, [128, 512])
            return f, [VW(f[0:64, :].rearrange("p (h s) -> p h s", h=8))]
        NNf, NN = scanbuf("NNf"); QQf, QQ = scanbuf("QQf"); NN2f, NN2 = scanbuf("NN2f"); QQ2f, QQ2 = scanbuf("QQ2f")
        XXf, XX = scanbuf("XXf"); ARBf, ARB = scanbuf("ARBf"); AKTf, AKT = scanbuf("AKTf"); ARKf, ARK = scanbuf("ARKf")
        bNN = [Buf(), Buf()]; bQQ = [Buf(), Buf()]; bNN2 = [Buf(), Buf()]; bQQ2 = [Buf(), Buf()]; bXX = [Buf(), Buf()]
        bARB = [Buf(), Buf()]; bAKT = [Buf(), Buf()]; bARK = [Buf(), Buf()]
        VR = [sb("VR%d" % i, [64, 8, 64]) for i in range(1)]; bVR = [Buf(), Buf()]
        BKT = [sb("BKT%d" % i, [64, 8, 2, 64]) for i in range(1)]; bBKT = [Buf(), Buf()]
        RKB = [sb("RKB%d" % i, [64, 8]) for i in range(1)]; bRKB = [Buf(), Buf()]
        HH = sb("HH", [64, 8, 64]); bHH = Buf()
        WN = sb("WN", [64, 8, 64]); UU = sb("UU", [64, 8, 64]); bWN = Buf(); bUU = Buf()
        HTMP = sb("HTMP", [64, 8, 64]); bHTMP = Buf()
        SIN = sb("SIN", [64, 8, 64]); bSIN = Buf()
        YS = sb("YS", [64, 8, 64]); YQ = sb("YQ", [64, 8, 64]); bYS = Buf(); bYQ = Buf()
        ST8 = sb("ST8", [64, 6, 8]); bST8 = Buf()
        GG = sb("GG", [64, 512]); bGG = Buf()
        ORW = sb("ORW", [64, 512], BF16); bORW = Buf()
        ORWT = sb("ORWT", [128, 4, 128], BF16); bORWT = Buf()
        OATT = sb("OATT", [64, 8, 128], BF16); bOATT = Buf()
        EE = sb("EE", [64, 3, 256]); bEE = Buf()
        PT = sb("PT", [64, 3, 256], BF16); bPT = Buf()
        DEN = sb("DEN", [64, 256]); bDEN = Buf()
        YO = sb("YO", [128, D]); bYO = Buf()
        MT1 = P3(HF); MT2 = P3(YO)
        bMT1 = bHF; bMT2 = bYO
        SOUT = sb("SOUT", [64, 512]); bSOUT = Buf()

        def rmsnorm(src, bsrc, gB, dstf, bdstf, col):
            A(lambda e: e.activation(out=JNK[:], in_=src[:], func=AF.Square, accum_out=SSQ[:, col:col + 1]),
              r=[bsrc], w=[bJNK, bSSQ])
            A(lambda e: e.activation(out=SSQ[:, col + 1:col + 2], in_=SSQ[:, col:col + 1], func=AF.Sqrt,
                                     scale=1.0 / D, bias=EPS_T[:, 0:1]), r=[bSSQ, bEPS], w=[bSSQ])
            V(lambda e: e.reciprocal(out=SSQ[:, col + 1:col + 2], in_=SSQ[:, col + 1:col + 2]), r=[bSSQ], w=[bSSQ])
            V(lambda e: e.scalar_tensor_tensor(out=dstf[:], in0=src[:], scalar=SSQ[:, col + 1:col + 2], in1=gB[:],
                                               op0=ALU.mult, op1=ALU.mult), r=[bsrc, bSSQ, bGB], w=[bdstf])

        EPS_T = sb("EPS_T", [128, 2]); bEPS = Buf()
        V(lambda e: e.memset(EPS_T[:, 0:1], 1e-6), w=[bEPS])
        V(lambda e: e.memset(EPS_T[:, 1:2], 64e-5), w=[bEPS])

        def mm(out, lhsT, rhs, start, stop):
            return lambda e: e.matmul(out, lhsT=lhsT, rhs=rhs, start=start, stop=stop)

        gchunk = [0]

        def do_tile(chunks):
            for k, c in enumerate(chunks):
                DMA((lambda k, c: lambda e: e.dma_start(out=X[64 * k:64 * k + 64, :], in_=c["xsrc"]))(k, c), w=[bX])
            G(lambda e: e.tensor_copy(out=CAR[:], in_=HT[:, :, 1, 64:65]), r=[bHT], w=[bCAR])
            rmsnorm(X, bX, G1B, HF, bHF, 0)
            A(lambda e: e.activation(out=HB[:], in_=HF[:], func=AF.Copy), r=[bHF], w=[bHB])
            for k, c in enumerate(chunks):
                if c.get("shift_out") is not None:
                    DMA((lambda k, c: lambda e: e.dma_start(out=c["shift_out"], in_=HF[64 * k + 63:64 * k + 64, :]))(k, c),
                        r=[bHF])
            hb = pbank[7][:].bitcast(BF16)
            for kc in range(8):
                PE((lambda kc: lambda e: e.transpose(out=hb[:, kc * 128:(kc + 1) * 128], in_=HB[:, kc * 128:(kc + 1) * 128],
                                                     identity=IDB[:]))(kc), r=[bHB, bIDB], w=[PB[7]])
            hbv = hb.rearrange("p (a c t) -> p a c t", a=8, c=2)
            A(lambda e: e.activation(out=HT[:, :, 0, 1:65], in_=hbv[:, :, 0, :], func=AF.Copy), r=[PB[7]], w=[bHT])
            V(lambda e: e.tensor_copy(out=HT[:, :, 1, 1:65], in_=hbv[:, :, 1, :]), r=[PB[7]], w=[bHT])
            for k, c in enumerate(chunks):
                pv = c["prev"]
                if pv == "zero":
                    G((lambda k: lambda e: e.memset(HT[:, :, k, 0:1], 0.0))(k), w=[bHT])
                elif pv == "carry":
                    G((lambda k: lambda e: e.tensor_copy(out=HT[:, :, k, 0:1], in_=CAR[:]))(k), r=[bCAR], w=[bHT])
                elif pv == "own":
                    G((lambda k: lambda e: e.tensor_copy(out=HT[:, :, k, 0:1], in_=HT[:, :, k - 1, 64:65]))(k), r=[bHT], w=[bHT])
                else:
                    b = pv[1]
                    def ld_state(e, k=k, b=b):
                        with nc.allow_non_contiguous_dma(reason="tiny"):
                            return e.dma_start(out=STS[:, k, :], in_=st_shift[b].rearrange("(a p) -> p a", p=128))
                    DMA(ld_state, w=[bSTS])
                    G((lambda k: lambda e: e.tensor_copy(out=HT[:, :, k, 0:1], in_=STS[:, k, :].unsqueeze(2)))(k), r=[bSTS], w=[bHT])

            ckpt(1)
            def proj_fm(slot_a, slot_b, col0, M, outap, wbank):
                n = 16 if slot_b is not None else 8
                i = 0
                for kc in range(8):
                    PE(mm(outap, RING[:, slot_a, kc * 128 + col0:kc * 128 + col0 + M], HT[:, kc, :, 1:65], i == 0, i == n - 1),
                       r=[bRING[slot_a], bHT], w=[wbank])
                    i += 1
                if slot_b is not None:
                    for kc in range(8):
                        PE(mm(outap, RING[:, slot_b, kc * 128 + col0:kc * 128 + col0 + M], HT[:, kc, :, 0:64], i == 0, i == n - 1),
                           r=[bRING[slot_b], bHT], w=[wbank])
                        i += 1

            def pview(bank, idx, M):
                return pbank[bank][0:M, idx * 128:(idx + 1) * 128].rearrange("p (c t) -> p c t", c=2)

            for qi, (dst, bdst) in enumerate(((RF, bRF), (KF, bKF), (VT, bVT))):
                for sp in range(4):
                    sa = wload(win_s[:, qi * 4 + sp, :], [bWIN[qi * 4 + sp]])
                    sbb = wload(win_s[:, 36 + qi * 4 + sp, :], [bWIN[36 + qi * 4 + sp]])
                    bank = sp // 2
                    for hl in range(2):
                        h = sp * 2 + hl
                        proj_fm(sa, sbb, hl * 64, 64, pview(bank, h % 4, 64), PB[bank])
                for bank in range(2):
                    eng = A if bank == 0 else V
                    if bank == 0:
                        A((lambda dst, bank: lambda e: e.activation(
                            out=dst[:, 4 * bank:4 * bank + 4, :], in_=pbank[bank][0:64, :].rearrange("p (h t) -> p h t", h=4),
                            func=AF.Copy))(dst, bank), r=[PB[bank]], w=[bdst])
                    else:
                        V((lambda dst, bank: lambda e: e.tensor_copy(
                            out=dst[:, 4 * bank:4 * bank + 4, :], in_=pbank[bank][0:64, :].rearrange("p (h t) -> p h t", h=4)))(dst, bank),
                          r=[PB[bank]], w=[bdst])
            ckpt(10)
            sa = wload(win_s[:, 12, :], [bWIN[12]]); sbb = wload(win_s[:, 48, :], [bWIN[48]])
            proj_fm(sa, sbb, 0, 64, pview(2, 0, 64), PB[2])
            proj_fm(sa, sbb, 64, 64, pview(2, 1, 64), PB[2])
            sa = wload(win_s[:, 13, :], [bWIN[13]]); sbb = wload(win_s[:, 49, :], [bWIN[49]])
            proj_fm(sa, sbb, 0, 128, pview(2, 2, 128), PB[2])
            A(lambda e: e.activation(out=TW[:], in_=pbank[2][0:64, 0:128], func=AF.Tanh), r=[PB[2]], w=[bTW])
            V(lambda e: e.tensor_copy(out=XA[:], in_=pbank[2][0:64, 128:256]), r=[PB[2]], w=[bXA])
            A(lambda e: e.activation(out=SG[:], in_=pbank[2][:, 256:384], func=AF.Sigmoid), r=[PB[2]], w=[bSG])
            ckpt(11)
            for (LOF, INP, bINP, b0) in ((0, TW, bTW, 3), (512, XA, bXA, 5)):
                for h in range(8):
                    bank = b0 + h // 4
                    o = pbank[bank][0:64, (h % 4) * 128:(h % 4 + 1) * 128]
                    PE(mm(o, LWX[0:64, LOF + h * 64:LOF + (h + 1) * 64], INP[:], True, False), r=[bLO, bINP], w=[PB[bank]])
                    PE(mm(o, LWX[64:65, LOF + h * 64:LOF + (h + 1) * 64], ONESF[64:65, :], False, True), r=[bLO, bONES], w=[PB[bank]])
            for bank in range(2):
                A((lambda bank: lambda e: e.activation(out=LD[:, 512 * bank:512 * bank + 512], in_=pbank[3 + bank][0:64, :],
                                                       func=AF.Sigmoid))(bank), r=[PB[3 + bank]], w=[bLD])
                A((lambda bank: lambda e: e.activation(out=AS[:, 4 * bank:4 * bank + 4, :],
                                                       in_=pbank[5 + bank][0:64, :].rearrange("p (h t) -> p h t", h=4),
                                                       func=AF.Sigmoid))(bank), r=[PB[5 + bank]], w=[bAS])
            ckpt(12)
            V(lambda e: e.tensor_tensor_scan(out=CC[:], data0=ONESF[0:64, 0:1].to_broadcast([64, 1024]), data1=LD[:], initial=0.0, op0=ALU.mult, op1=ALU.add),
              r=[bLD, bONES], w=[bCC])
            V(lambda e: e.tensor_tensor(out=CE[:].rearrange("p (a t) -> p a t", t=64)[:, :, 0:1],
                                        in0=CC[:].rearrange("p (a t) -> p a t", t=64)[:, :, 0:1],
                                        in1=LD[:].rearrange("p (a t) -> p a t", t=64)[:, :, 0:1], op=ALU.subtract), r=[bCC, bLD], w=[bCE])
            V(lambda e: e.tensor_tensor(out=CUM[:].rearrange("p (a t) -> p a t", t=64),
                                        in0=CC[:].rearrange("p (a t) -> p a t", t=64),
                                        in1=CE[:].rearrange("p (a t) -> p a t", t=64)[:, :, 0:1].to_broadcast([64, 16, 64]),
                                        op=ALU.subtract), r=[bCC, bCE], w=[bCUM])
            V(lambda e: e.tensor_tensor(out=CUMX[:], in0=CUM[:], in1=LD[:], op=ALU.subtract), r=[bCUM, bLD], w=[bCUMX])
            A(lambda e: e.activation(out=EP[:], in_=CUM[:], func=AF.Exp, scale=-EXPM05), r=[bCUM], w=[bEP])
            A(lambda e: e.activation(out=EPX[:], in_=CUMX[:], func=AF.Exp, scale=-EXPM05), r=[bCUMX], w=[bEPX])
            A(lambda e: e.activation(out=EM[:], in_=CUM[:], func=AF.Exp, scale=EXPM05), r=[bCUM], w=[bEM])
            ckpt(13)
            V(lambda e: e.tensor_tensor(out=KKR[:], in0=KF[:], in1=P_KK.unsqueeze(2).to_broadcast([64, 8, 128]), op=ALU.mult),
              r=[bKF, bPAR], w=[bKKR])
            A(lambda e: e.activation(out=SQ[:], in_=KKR[:], func=AF.Square), r=[bKKR], w=[bSQ])
            for bank in range(2):
                PE(mm(pbank[3 + bank][0:64, :], ONESF[0:64, 0:64], SQ[:, 4 * bank:4 * bank + 4, :], True, True), r=[bSQ, bONES], w=[PB[3 + bank]])
                A((lambda bank: lambda e: e.activation(out=RN[:, 4 * bank:4 * bank + 4, :],
                                                       in_=pbank[3 + bank][0:64, :].rearrange("p (h t) -> p h t", h=4),
                                                       func=AF.Sqrt))(bank), r=[PB[3 + bank]], w=[bRN])
            V(lambda e: e.tensor_scalar(out=RN[:], in0=RN[:], scalar1=1e-12, scalar2=None, op0=ALU.max), r=[bRN], w=[bRN])
            V(lambda e: e.reciprocal(out=RN[:], in_=RN[:]), r=[bRN], w=[bRN])
            V(lambda e: e.tensor_tensor(out=KK[:], in0=KKR[:], in1=RN[:], op=ALU.mult), r=[bKKR, bRN], w=[bKK])
            ckpt(14)
            V(lambda e: e.scalar_tensor_tensor(out=T1[:], in0=AS[:], scalar=-1.0, in1=P_KA.unsqueeze(2).to_broadcast([64, 8, 128]),
                                               op0=ALU.add, op1=ALU.mult), r=[bAS, bPAR], w=[bT1])
            V(lambda e: e.scalar_tensor_tensor(out=KP[:], in0=T1[:], scalar=1.0, in1=KF[:], op0=ALU.add, op1=ALU.mult),
              r=[bT1, bKF], w=[bKP])
            V(lambda e: e.tensor_tensor(out=BB[:], in0=KK[:], in1=AS[:], op=ALU.mult), r=[bKK, bAS], w=[bBB])
            G(lambda e: e.tensor_tensor(out=RKP[:], in0=RF[:], in1=KP[:], op=ALU.mult), r=[bRF, bKP], w=[bRKP])

            def v4(t):
                return t[:].rearrange("p (h c t) -> p h c t", h=8, c=2)

            def w4(t):
                return t[:].rearrange("p h (c t) -> p h c t", c=2)
            V(lambda e: e.tensor_tensor(out=KR[:, :, :, 0, :], in0=w4(KK), in1=v4(EPX), op=ALU.mult), r=[bKK, bEPX], w=[bKR])
            G(lambda e: e.tensor_tensor(out=KR[:, :, :, 1, :], in0=w4(RF), in1=v4(EP), op=ALU.mult), r=[bRF, bEP], w=[bKR])
            V(lambda e: e.tensor_tensor(out=BK[:, :, :, 0, :], in0=w4(BB), in1=v4(EM), op=ALU.mult), r=[bBB, bEM], w=[bBK])
            G(lambda e: e.tensor_tensor(out=BK[:, :, :, 1, :], in0=w4(KP), in1=v4(EM), op=ALU.mult), r=[bKP, bEM], w=[bBK])

            ckpt(16)
            for sp in range(4):
                sa = wload(win_s[:, 14 + sp, :], [bWIN[14 + sp]])
                bank = 3 + sp // 2
                for hl in range(2):
                    h = sp * 2 + hl
                    proj_fm(sa, None, hl * 64, 64, pview(bank, h % 4, 64), PB[bank])
            for bank in range(2):
                A((lambda bank: lambda e: e.activation(out=QT[:, 4 * bank:4 * bank + 4, :],
                                                       in_=pbank[3 + bank][0:64, :].rearrange("p (h t) -> p h t", h=4),
                                                       func=AF.Copy))(bank), r=[PB[3 + bank]], w=[bQT])
            ckpt(17)
            sa = wload(win_s[:, 18, :], [bWIN[18]])
            proj_fm(sa, None, 0, 64, pview(5, 0, 64), PB[5])
            proj_fm(sa, None, 64, 64, pview(5, 1, 64), PB[5])
            need_kout = any(c.get("k_out") is not None for c in chunks)
            for k, c in enumerate(chunks):
                sl = c["kslots"][2]
                V((lambda k, sl: lambda e: e.tensor_copy(
                    out=KTR[:, :, sl, :], in_=pbank[5][0:64, 0:256].rearrange("p (v c t) -> p v c t", v=2, c=2)[:, :, k, :]))(k, sl),
                  r=[PB[5]], w=[bKTR[sl]])
            if need_kout:
                A(lambda e: e.activation(out=KTF[:], in_=pbank[5][0:64, 0:256].rearrange("p (v t) -> p v t", v=2), func=AF.Copy),
                  r=[PB[5]], w=[bKTF])
            ckpt(18)
            sa = wload(win_s[:, 19, :], [bWIN[19]])
            for k, c in enumerate(chunks):
                sl = c["kslots"][2]
                o = pbank[6][0:64, k * 128:(k + 1) * 128]
                for kc in range(8):
                    PE(mm(o, HT[:, kc, k, 1:65], RING[:, sa, kc * 128:(kc + 1) * 128], kc == 0, kc == 7), r=[bRING[sa], bHT], w=[PB[6]])
                if c.get("v_out") is not None:
                    A((lambda k, o: lambda e: e.activation(out=VAF[:, k, :], in_=o, func=AF.Copy))(k, o), r=[PB[6]], w=[bVAF])
                    V((lambda sl, k: lambda e: e.tensor_copy(out=VAR[:, sl, :], in_=VAF[:, k, :]))(sl, k), r=[bVAF], w=[bVAR[sl]])
                    DMA((lambda k, c: lambda e: e.dma_start(out=c["v_out"], in_=VAF[:, k, :]))(k, c), r=[bVAF])
                else:
                    V((lambda sl, o: lambda e: e.tensor_copy(out=VAR[:, sl, :], in_=o))(sl, o), r=[PB[6]], w=[bVAR[sl]])
            ckpt(19)
            if need_kout:
                for kvh in range(2):
                    PE((lambda kvh: lambda e: e.transpose(out=pbank[7][:, kvh * 64:(kvh + 1) * 64], in_=KTF[:, kvh, :],
                                                          identity=IDF[0:64, 0:64]))(kvh), r=[bKTF, bCST], w=[PB[7]])
                V(lambda e: e.tensor_copy(out=KTOK[:], in_=pbank[7][:, 0:128]), r=[PB[7]], w=[bKTOK])
                for k, c in enumerate(chunks):
                    if c.get("k_out") is not None:
                        DMA((lambda k, c: lambda e: e.dma_start(out=c["k_out"], in_=KTOK[64 * k:64 * k + 64, :]))(k, c), r=[bKTOK])
            ckpt(20)
            for gq in range(4):
                bank = 3 + (gq % 2)
                for j in range(4):
                    sa = wload(win_s[:, 20 + gq * 4 + j, :], [bWIN[20 + gq * 4 + j]])
                    proj_fm(sa, None, 0, 128, pview(bank, j, 128), PB[bank])
                A((lambda gq, bank: lambda e: e.activation(out=SGT[:, 4 * gq:4 * gq + 4, :],
                                                           in_=pbank[bank][:, :].rearrange("p (h t) -> p h t", h=4),
                                                           func=AF.Sigmoid))(gq, bank), r=[PB[bank]], w=[bSGT])

            ckpt(21)
            def do_chunk(k, c):
                pr = 0
                gchunk[0] += 1
                cs = slice(64 * k, 64 * k + 64)
                def f_vtrans():
                    for h in range(8):
                        PE((lambda h: lambda e: e.transpose(out=pbank[3][0:64, h * 64:(h + 1) * 64], in_=VT[:, h, cs],
                                                            identity=IDF[0:64, 0:64]))(h), r=[bVT, bCST], w=[PB[3]])
                    A((lambda pr: lambda e: e.activation(out=VR[pr][:], in_=pbank[3][0:64, :].rearrange("p (h i) -> p h i", h=8),
                                                         func=AF.Copy))(pr), r=[PB[3]], w=[bVR[pr]])

                def f_bkt(w2):
                    bk = 4 - w2
                    for h in range(8):
                        PE((lambda h, w2: lambda e: e.transpose(out=pbank[bk][0:64, h * 64:(h + 1) * 64], in_=BK[:, h, k, w2, :],
                                                                identity=IDF[0:64, 0:64]))(h, w2), r=[bBK, bCST], w=[PB[bk]])
                    V((lambda pr, w2: lambda e: e.tensor_copy(out=BKT[pr][:, :, w2, :],
                                                              in_=pbank[bk][0:64, :].rearrange("p (h j) -> p h j", h=8)))(pr, w2),
                      r=[PB[bk]], w=[bBKT[pr]])

                def f_rkg():
                    for h in range(8):
                        PE(mm(pbank[7][0:64, h:h + 1], RKP[:, h, cs], P_RK[:, h:h + 1], True, True), r=[bRKP, bPAR], w=[PB[7]])
                    V((lambda pr: lambda e: e.tensor_copy(out=RKB[pr][:], in_=pbank[7][0:64, 0:8]))(pr), r=[PB[7]], w=[bRKB[pr]])
                    PE(mm(pbank[7][0:64, :], SG[:, cs], GL[:], True, True), r=[bSG, bLO], w=[PB[7]])
                    A(lambda e: e.activation(out=GG[:], in_=pbank[7][0:64, :], func=AF.Copy), r=[PB[7]], w=[bGG])

                ks = c["kslots"]
                valid = [i for i in range(3) if ks[i] is not None]
                al = ALIBI.rearrange("p (v k x) -> p v k x", v=2, k=3)

                def f_sc(kvh):
                    for kc in valid:
                        bank = 5 + (kc * 256) // 512
                        o = pbank[bank][0:64, (kc * 256) % 512:(kc * 256) % 512 + 256]
                        PE(mm(o, KTR[:, kvh, ks[kc], :], QT[:, 4 * kvh:4 * kvh + 4, cs], True, True),
                           r=[bKTR[ks[kc]], bQT], w=[PB[bank]])
                    for kc in valid:
                        bank = 5 + (kc * 256) // 512
                        o = pbank[bank][0:64, (kc * 256) % 512:(kc * 256) % 512 + 256]
                        A((lambda kc, o: lambda e: e.activation(out=EE[:, kc, :], in_=o, func=AF.Exp, scale=0.125))(kc, o),
                          r=[PB[bank]], w=[bEE])
                    k0, k1 = valid[0], valid[-1] + 1
                    V((lambda kvh, k0, k1: lambda e: e.tensor_tensor(out=PT[:, k0:k1, :], in0=EE[:, k0:k1, :],
                                                                     in1=al[:, kvh, k0:k1, :], op=ALU.mult))(kvh, k0, k1),
                      r=[bEE, bCST], w=[bPT])

                def f_pv(kvh):
                    for ii, kc in enumerate(valid):
                        PE(mm(pbank[7][0:64, 0:256], VAR[:, ks[kc], kvh * 64:(kvh + 1) * 64], PT[:, kc, :], ii == 0, ii == len(valid) - 1),
                           r=[bVAR[ks[kc]], bPT], w=[PB[7]])
                    for ii, kc in enumerate(valid):
                        PE(mm(pbank[7][0:64, 256:512], ONESB[:], PT[:, kc, :], ii == 0, ii == len(valid) - 1),
                           r=[bONES, bPT], w=[PB[7]])
                    V((lambda kvh: lambda e: e.tensor_tensor(
                        out=DEN[:].rearrange("p (g q) -> p g q", g=4), in0=pbank[7][0:64, 256:512].rearrange("p (g q) -> p g q", g=4),
                        in1=SNKE[:, 4 * kvh:4 * kvh + 4].unsqueeze(2).to_broadcast([64, 4, 64]), op=ALU.add))(kvh),
                      r=[PB[7], bSNK], w=[bDEN])
                    V(lambda e: e.reciprocal(out=DEN[:], in_=DEN[:]), r=[bDEN], w=[bDEN])
                    V((lambda kvh: lambda e: e.tensor_tensor(
                        out=OATT[:, 4 * kvh:4 * kvh + 4, cs], in0=pbank[7][0:64, 0:256].rearrange("p (g q) -> p g q", g=4),
                        in1=DEN[:].rearrange("p (g q) -> p g q", g=4), op=ALU.mult))(kvh), r=[PB[7], bDEN], w=[bOATT])

                fillers = [f_vtrans, lambda: f_bkt(0), lambda: f_bkt(1), f_rkg, lambda: f_sc(0), lambda: f_pv(0),
                           lambda: f_sc(1), lambda: f_pv(1)]

                def fill():
                    if fillers:
                        fillers.pop(0)()

                ckpt(30)
                for h in range(8):
                    bank = h // 4
                    PE(mm(pbank[bank][0:64, (h % 4) * 128:(h % 4 + 1) * 128], BK[:, h, k, 0, :], KR[:, h, k, :, :], True, True),
                       r=[bBK, bKR], w=[PB[bank]])
                for h in range(8):
                    bank = 2 + h // 4
                    PE(mm(pbank[bank][0:64, (h % 4) * 128:(h % 4 + 1) * 128], BK[:, h, k, 1, :], KR[:, h, k, :, :], True, True),
                       r=[bBK, bKR], w=[PB[bank]])
                for h in range(8):
                    PE(mm(pbank[4][0:64, h * 64:(h + 1) * 64], KR[:, h, k, 0, :], BK[:, h, k, 0, :], True, True),
                       r=[bBK, bKR], w=[PB[4]])
                m1v = M1.rearrange("p (w t) -> p w t", w=2)
                for bank in range(2):
                    pv4 = pbank[bank][0:64, :].rearrange("p (h w t) -> p h w t", h=4, w=2)
                    V((lambda pr, bank, pv4: lambda e: e.tensor_tensor(
                        out=NN[pr][:, 4 * bank:4 * bank + 4, :], in0=pv4[:, :, 0, :],
                        in1=m1v[:, 0:1, :].to_broadcast([64, 4, 64]), op=ALU.mult))(pr, bank, pv4), r=[PB[bank], bCST], w=[bNN[pr]])
                    V((lambda pr, bank, pv4: lambda e: e.tensor_tensor(
                        out=ARB[pr][:, 4 * bank:4 * bank + 4, :], in0=pv4[:, :, 1, :],
                        in1=m1v[:, 1:2, :].to_broadcast([64, 4, 64]), op=ALU.mult))(pr, bank, pv4), r=[PB[bank], bCST], w=[bARB[pr]])
                    pk4 = pbank[2 + bank][0:64, :].rearrange("p (h w t) -> p h w t", h=4, w=2)
                    V((lambda pr, bank, pk4: lambda e: e.tensor_tensor(
                        out=AKT[pr][:, 4 * bank:4 * bank + 4, :], in0=pk4[:, :, 0, :],
                        in1=m1v[:, 0:1, :].to_broadcast([64, 4, 64]), op=ALU.mult))(pr, bank, pk4), r=[PB[2 + bank], bCST], w=[bAKT[pr]])
                    V((lambda pr, bank, pk4: lambda e: e.tensor_tensor(
                        out=ARK[pr][:, 4 * bank:4 * bank + 4, :], in0=pk4[:, :, 1, :],
                        in1=m1v[:, 1:2, :].to_broadcast([64, 4, 64]), op=ALU.mult))(pr, bank, pk4), r=[PB[2 + bank], bCST], w=[bARK[pr]])
                V((lambda pr: lambda e: e.tensor_tensor(
                    out=QQ[pr][:], in0=pbank[4][0:64, :].rearrange("p (h s) -> p h s", h=8),
                    in1=ML.unsqueeze(1).to_broadcast([64, 8, 64]), op=ALU.mult))(pr), r=[PB[4], bCST], w=[bQQ[pr]])
                G((lambda pr: lambda e: e.tensor_tensor(
                    out=XX[pr][:], in0=IDF[0:64, 0:64].unsqueeze(1).to_broadcast([64, 8, 64]), in1=NN[pr][:], op=ALU.subtract))(pr),
                  r=[bCST, bNN[pr]], w=[bXX[pr]])
                Pc, Qc, bPc, bQc = NN[pr], QQ[pr], bNN[pr], bQQ[pr]
                Pn, Qn, bPn, bQn = NN2[pr], QQ2[pr], bNN2[pr], bQQ2[pr]
                for rd in range(1, 6):
                    last = rd == 5
                    if not last:
                        for h in range(8):
                            PE(mm(pbank[0][0:64, h * 64:(h + 1) * 64], Qc[:, h, :], Pc[:, h, :], True, True), r=[bPc, bQc], w=[PB[0]])
                    for h in range(8):
                        PE(mm(pbank[1][0:64, h * 64:(h + 1) * 64], Pc[:, h, :], Qc[:, h, :], True, True), r=[bPc, bQc], w=[PB[1]])
                    if not last:
                        A((lambda Pn: lambda e: e.activation(out=Pn[:], in_=pbank[0][0:64, :].rearrange("p (h s) -> p h s", h=8),
                                                             func=AF.Copy))(Pn), r=[PB[0]], w=[bPn])
                    V((lambda Qn: lambda e: e.tensor_copy(out=Qn[:], in_=pbank[1][0:64, :].rearrange("p (h s) -> p h s", h=8)))(Qn),
                      r=[PB[1]], w=[bQn])
                    fill()
                    for h in range(8):
                        PE(mm(pbank[2][0:64, h * 64:(h + 1) * 64], Qn[:, h, :], XX[pr][:, h, :], True, True), r=[bQn, bXX[pr]], w=[PB[2]])
                    V((lambda pr: lambda e: e.tensor_tensor(out=XX[pr][:], in0=XX[pr][:],
                                                            in1=pbank[2][0:64, :].rearrange("p (h s) -> p h s", h=8), op=ALU.add))(pr),
                      r=[PB[2], bXX[pr]], w=[bXX[pr]])
                    fill()
                    Pc, Qc, bPc, bQc, Pn, Qn, bPn, bQn = Pn, Qn, bPn, bQn, Pc, Qc, bPc, bQc

                while fillers:
                    fill()
                ckpt(31)
                stt = c["state"]
                if stt == "zero":
                    G(lambda e: e.memset(HH[:], 0.0), w=[bHH])
                elif stt != "keep":
                    b = stt[1]
                    DMA((lambda b: lambda e: e.dma_start(out=SIN[:], in_=st_wkv[b].rearrange("h i j -> i h j")))(b), w=[bSIN])
                    for h in range(8):
                        PE((lambda h: lambda e: e.transpose(out=pbank[3][0:64, h * 64:(h + 1) * 64], in_=SIN[:, h, :],
                                                            identity=IDF[0:64, 0:64]))(h), r=[bSIN, bCST], w=[PB[3]])
                    V(lambda e: e.tensor_copy(out=HH[:], in_=pbank[3][0:64, :].rearrange("p (h i) -> p h i", h=8)), r=[PB[3]], w=[bHH])
                for h in range(8):
                    o = pbank[3][0:64, h * 64:(h + 1) * 64]
                    PE(mm(o, KR[:, h, k, 0, :], HH[:, h, :], True, False), r=[bKR, bHH], w=[PB[3]])
                    PE(mm(o, AKT[pr][:, h, :], VR[pr][:, h, :], False, True), r=[bAKT[pr], bVR[pr]], w=[PB[3]])
                A(lambda e: e.activation(out=WN[:], in_=pbank[3][0:64, :].rearrange("p (h i) -> p h i", h=8), func=AF.Copy, scale=-1.0),
                  r=[PB[3]], w=[bWN])
                for h in range(8):
                    PE(mm(pbank[4][0:64, h * 64:(h + 1) * 64], XX[pr][:, h, :], WN[:, h, :], True, True), r=[bXX[pr], bWN], w=[PB[4]])
                V(lambda e: e.tensor_copy(out=UU[:], in_=pbank[4][0:64, :].rearrange("p (h i) -> p h i", h=8)), r=[PB[4]], w=[bUU])
                for h in range(8):
                    o = pbank[3][0:64, h * 64:(h + 1) * 64]
                    PE(mm(o, KR[:, h, k, 1, :], HH[:, h, :], True, False), r=[bKR, bHH], w=[PB[3]])
                    PE(mm(o, ARB[pr][:, h, :], UU[:, h, :], False, False), r=[bARB[pr], bUU], w=[PB[3]])
                    PE(mm(o, ARK[pr][:, h, :], VR[pr][:, h, :], False, True), r=[bARK[pr], bVR[pr]], w=[PB[3]])
                for h in range(8):
                    o = pbank[4][0:64, h * 64:(h + 1) * 64]
                    PE(mm(o, BKT[pr][:, h, 0, :], UU[:, h, :], True, False), r=[bBKT[pr], bUU], w=[PB[4]])
                    PE(mm(o, BKT[pr][:, h, 1, :], VR[pr][:, h, :], False, True), r=[bBKT[pr], bVR[pr]], w=[PB[4]])
                V(lambda e: e.tensor_tensor(out=HTMP[:], in0=pbank[4][0:64, :].rearrange("p (h i) -> p h i", h=8), in1=HH[:], op=ALU.add),
                  r=[PB[4], bHH], w=[bHTMP])
                epl = EP[:].rearrange("p (h c t) -> p h c t", h=8, c=2)[:, :, k, 63:64]
                V(lambda e: e.tensor_tensor(out=HH[:], in0=HTMP[:], in1=epl.to_broadcast([64, 8, 64]), op=ALU.mult),
                  r=[bHTMP, bEP], w=[bHH])
                if c.get("wkv_out") is not None:
                    for h in range(8):
                        PE((lambda h: lambda e: e.transpose(out=pbank[4][0:64, h * 64:(h + 1) * 64], in_=HH[:, h, :],
                                                            identity=IDF[0:64, 0:64]))(h), r=[bHH, bCST], w=[PB[4]])
                    V(lambda e: e.tensor_copy(out=SOUT[:], in_=pbank[4][0:64, :]), r=[PB[4]], w=[bSOUT])
                    DMA((lambda c: lambda e: e.dma_start(out=c["wkv_out"].rearrange("h i j -> i h j"),
                                                         in_=SOUT[:].rearrange("p (h j) -> p h j", h=8)))(c), r=[bSOUT])
                y3 = pbank[3][0:64, :].rearrange("p (h i) -> p h i", h=8)
                A(lambda e: e.activation(out=YS[:], in_=y3, func=AF.Copy), r=[PB[3]], w=[bYS])
                A(lambda e: e.activation(out=YQ[:], in_=y3, func=AF.Square), r=[PB[3]], w=[bYQ])
                V(lambda e: e.tensor_reduce(out=ST8[:, 0, :], in_=YS[:], axis=AX.X, op=ALU.add), r=[bYS], w=[bST8])
                V(lambda e: e.tensor_reduce(out=ST8[:, 1, :], in_=YQ[:], axis=AX.X, op=ALU.add), r=[bYQ], w=[bST8])
                V(lambda e: e.tensor_scalar(out=ST8[:, 0, :], in0=ST8[:, 0, :], scalar1=1.0 / 64, scalar2=None, op0=ALU.mult), r=[bST8], w=[bST8])
                V(lambda e: e.tensor_tensor(out=ST8[:, 2, :], in0=ST8[:, 0, :], in1=ST8[:, 0, :], op=ALU.mult), r=[bST8], w=[bST8])
                V(lambda e: e.scalar_tensor_tensor(out=ST8[:, 3, :], in0=ST8[:, 1, :], scalar=1.0 / 64, in1=ST8[:, 2, :],
                                                   op0=ALU.mult, op1=ALU.subtract), r=[bST8], w=[bST8])
                A(lambda e: e.activation(out=ST8[:, 4, :], in_=ST8[:, 3, :], func=AF.Sqrt, bias=EPS_T[0:64, 1:2]), r=[bST8, bEPS], w=[bST8])
                V(lambda e: e.reciprocal(out=ST8[:, 5, :], in_=ST8[:, 4, :]), r=[bST8], w=[bST8])
                V(lambda e: e.tensor_tensor(out=YS[:], in0=YS[:], in1=ST8[:, 0, :].unsqueeze(2).to_broadcast([64, 8, 64]), op=ALU.subtract),
                  r=[bYS, bST8], w=[bYS])
                V(lambda e: e.tensor_tensor(out=YS[:], in0=YS[:], in1=ST8[:, 5, :].unsqueeze(2).to_broadcast([64, 8, 64]), op=ALU.mult),
                  r=[bYS, bST8], w=[bYS])
                ysf = YS[:].rearrange("p h i -> p (h i)")
                G(lambda e: e.tensor_tensor(out=ysf, in0=ysf, in1=LNW[:], op=ALU.mult), r=[bYS, bLN], w=[bYS])
                G(lambda e: e.tensor_tensor(out=ysf, in0=ysf, in1=LNB[:], op=ALU.add), r=[bYS, bLN], w=[bYS])
                V((lambda pr: lambda e: e.tensor_tensor(out=YQ[:], in0=VR[pr][:], in1=RKB[pr][:].unsqueeze(2).to_broadcast([64, 8, 64]),
                                                        op=ALU.mult))(pr), r=[bVR[pr], bRKB[pr]], w=[bYQ])
                G(lambda e: e.tensor_tensor(out=YS[:], in0=YS[:], in1=YQ[:], op=ALU.add), r=[bYS, bYQ], w=[bYS])
                V(lambda e: e.tensor_tensor(out=ORW[:], in0=ysf, in1=GG[:], op=ALU.mult), r=[bYS, bGG], w=[bORW])
                ob = pbank[0][:].bitcast(BF16)
                for q4 in range(4):
                    PE((lambda q4: lambda e: e.transpose(out=ob[:, q4 * 64:(q4 + 1) * 64], in_=ORW[:, q4 * 128:(q4 + 1) * 128],
                                                         identity=IDB[0:64, 0:64]))(q4), r=[bORW, bIDB], w=[PB[0]])
                V(lambda e: e.tensor_copy(out=ORWT[:, :, cs], in_=ob[:, 0:256].rearrange("p (q t) -> p q t", q=4)), r=[PB[0]], w=[bORWT])

            for k_, c_ in enumerate(chunks):
                do_chunk(k_, c_)
                ckpt(32)

            for dp in range(4):
                sa = wload(wb0_s[:, dp, :], [bWB0])
                bank = dp // 2
                for dl in range(2):
                    dc = dp * 2 + dl
                    o = pbank[bank][:, (dc % 4) * 128:(dc % 4 + 1) * 128]
                    for kc in range(4):
                        PE(mm(o, RING[:, sa, kc * 256 + dl * 128:kc * 256 + dl * 128 + 128], ORWT[:, kc, :], kc == 0, kc == 3),
                           r=[bRING[sa], bORWT], w=[PB[bank]])
            for dc in range(8):
                sa = wload(wb1_s[:, dc, :], [bWB1], npart=64)
                bank = 2 + dc // 4
                o = pbank[bank][:, (dc % 4) * 128:(dc % 4 + 1) * 128]
                for qh in range(8):
                    PE(mm(o, RING[0:64, sa, qh * 128:(qh + 1) * 128], OATT[:, qh, :], qh == 0, qh == 7),
                       r=[bRING[sa], bOATT], w=[PB[bank]])
            for bank in range(2):
                V((lambda bank: lambda e: e.tensor_tensor(out=MT1[:, 4 * bank:4 * bank + 4, :],
                                                          in0=pbank[bank][:, :].rearrange("p (a t) -> p a t", a=4),
                                                          in1=SGT[:, 4 * bank:4 * bank + 4, :], op=ALU.mult))(bank),
                  r=[PB[bank], bSGT], w=[bMT1])
                V((lambda bank: lambda e: e.tensor_tensor(out=MT2[:, 4 * bank:4 * bank + 4, :],
                                                          in0=pbank[2 + bank][:, :].rearrange("p (a t) -> p a t", a=4),
                                                          in1=SGT[:, 8 + 4 * bank:8 + 4 * bank + 4, :], op=ALU.mult))(bank),
                  r=[PB[2 + bank], bSGT], w=[bMT2])
            V(lambda e: e.tensor_tensor(out=MRG[:], in0=MT1[:], in1=MT2[:], op=ALU.add), r=[bMT1, bMT2], w=[bMRG])
            for cs8 in range(8):
                sa = wload(wo_s[:, cs8, :], [bWO])
                bank = 4 + cs8 // 4
                o = pbank[bank][:, (cs8 % 4) * 128:(cs8 % 4 + 1) * 128]
                for kc in range(8):
                    PE(mm(o, MRG[:, kc, :], RING[:, sa, kc * 128:(kc + 1) * 128], kc == 0, kc == 7), r=[bRING[sa], bMRG], w=[PB[bank]])
            for bank in range(2):
                V((lambda bank: lambda e: e.tensor_tensor(out=X[:, 512 * bank:512 * bank + 512], in0=X[:, 512 * bank:512 * bank + 512],
                                                          in1=pbank[4 + bank][:, :], op=ALU.add))(bank), r=[PB[4 + bank], bX], w=[bX])

            ckpt(40)
            if do_peer:
                do_peer_tile()

            rmsnorm(X, bX, GFB, YO, bYO, 2)
            for k, c in enumerate(chunks):
                DMA((lambda k, c: lambda e: e.dma_start(out=c["yout"], in_=YO[64 * k:64 * k + 64, :]))(k, c), r=[bYO])

        if do_peer:
            bigA = [bRF, bKF, bVT, bLD, bCC, bCE, bCUM, bAS]
            bigB = [bKKR, bSQ, bT1, bBB]
            GWT = BIGA[:].bitcast(BF16).rearrange("p (i t) -> p i t", i=128)
            QREP = BIGB[64:128, :].bitcast(BF16).rearrange("p (t m) -> p t m", t=64)
            H2T = sb("H2T", [128, 8, 128], BF16); bH2T = Buf()
            QTP = sb("QTP", [128, 8, 128], BF16); bQTP = Buf()
            SC = sb("SC", [128, 256]); bSC = Buf()
            TOP1 = sb("TOP1", [128, 8, 16]); TOP2 = sb("TOP2", [128, 8, 16]); bTOP = Buf()
            IDXU = sb("IDXU", [128, 8, 16], U32); IDXF = sb("IDXF", [128, 8, 16]); bIDX = Buf()
            BEST = sb("BEST", [128, 8, 16]); bBEST = Buf()
            DD = sb("DD", [128, 8, 16]); bDD = Buf()
            ST = sb("ST", [128, 6, 8]); bST = Buf()
            C1T = VW(AKTf[:, 0:128].rearrange("p (h a) -> p h a", h=8)); C2T = VW(AKTf[:, 128:256].rearrange("p (h a) -> p h a", h=8))
            bCT = bAKT[0]
            CT3 = VW(ARKf[:, 0:384].rearrange("p (c t) -> p c t", c=3)); bCT3 = bARK[0]
            EXB = [VW(NNf[:].rearrange("p (j i) -> p j i", j=4)), VW(QQf[:].rearrange("p (j i) -> p j i", j=4))]
            bEXB = [bNN[0], bQQ[0]]
            RB = [VW(NN2f[:].bitcast(BF16)[:, 0:512].rearrange("p (j i) -> p j i", j=4)),
                  VW(QQ2f[:].bitcast(BF16)[:, 0:512].rearrange("p (j i) -> p j i", j=4))]
            bRB = [bNN2[0], bQQ2[0]]
            E1B = [VW(XXf[:].bitcast(BF16)[:, 0:512].rearrange("p (j i) -> p j i", j=4)),
                   VW(ARBf[:].bitcast(BF16)[:, 0:512].rearrange("p (j i) -> p j i", j=4))]
            bE1B = [bXX[0], bARB[0]]
            GB = [sb("GB%d" % i, [128, 128], BF16) for i in range(2)]; bGB_ = [Buf(), Buf()]
            GWB = [sb("GWB%d" % i, [128, 128], BF16) for i in range(2)]; bGWB = [Buf(), Buf()]
            bQp = [Buf() for _ in range(4)]
            bEXj = [[Buf() for _ in range(4)] for _ in range(2)]; bRj = [[Buf() for _ in range(4)] for _ in range(2)]
            bE1j = [[Buf() for _ in range(4)] for _ in range(2)]
            CAND = YO[:].rearrange("p (h x) -> p h x", h=4)
            CW = HF[:].rearrange("p (h x) -> p h x", h=4)

        def do_peer_tile():
            rmsnorm(X, bX, G2B, HF, bHF, 0)
            A(lambda e: e.activation(out=HB[:], in_=HF[:], func=AF.Copy), r=[bHF], w=[bHB])
            hb = pbank[7][:].bitcast(BF16)
            for kc in range(8):
                PE((lambda kc: lambda e: e.transpose(out=hb[:, kc * 128:(kc + 1) * 128], in_=HB[:, kc * 128:(kc + 1) * 128],
                                                     identity=IDB[:]))(kc), r=[bHB, bIDB], w=[PB[7]])
            A(lambda e: e.activation(out=H2T[:].rearrange("p a t -> p (a t)"), in_=hb, func=AF.Copy), r=[PB[7]], w=[bH2T])
            for h in range(8):
                sa = wload(wq_s[:, h, :], [bWQ])
                bank = 2 + h // 4
                o = pbank[bank][:, (h % 4) * 128:(h % 4 + 1) * 128]
                for kc in range(8):
                    PE(mm(o, RING[:, sa, kc * 128:(kc + 1) * 128], H2T[:, kc, :], kc == 0, kc == 7), r=[bRING[sa], bH2T], w=[PB[bank]])
            for bank in range(2):
                A((lambda bank: lambda e: e.activation(out=QTP[:, 4 * bank:4 * bank + 4, :],
                                                       in_=pbank[2 + bank][:, :].rearrange("p (h t) -> p h t", h=4),
                                                       func=AF.Copy))(bank), r=[PB[2 + bank]], w=[bQTP])
            for h in range(8):
                bank = 4 + h // 2
                o = pbank[bank][:, (h % 2) * 256:(h % 2) * 256 + 256]
                PE(mm(o, QTP[:, h, :], KEYB[:], True, True), r=[bQTP, bKEY], w=[PB[bank]])
            for h in range(8):
                bank = 4 + h // 2
                for p, TOP in enumerate((TOP1, TOP2)):
                    sv = pbank[bank][:, (h % 2) * 256 + p * 128:(h % 2) * 256 + p * 128 + 128]
                    V((lambda h, TOP, sv: lambda e: e.max(out=TOP[:, h, 0:8], in_=sv))(h, TOP, sv), r=[PB[bank]], w=[bTOP])
                    V((lambda h, TOP, sv, p: lambda e: e.match_replace(out=SC[:, p * 128:(p + 1) * 128], in_to_replace=TOP[:, h, 0:8],
                                                                       in_values=sv, imm_value=-1e30))(h, TOP, sv, p),
                      r=[PB[bank], bTOP], w=[bSC])
                    V((lambda h, TOP, p: lambda e: e.max(out=TOP[:, h, 8:16], in_=SC[:, p * 128:(p + 1) * 128]))(h, TOP, p),
                      r=[bSC], w=[bTOP])
                    if p == 0:
                        V((lambda h, sv: lambda e: e.max_index(out=IDXU[:, h, 0:8], in_max=TOP1[:, h, 0:8], in_values=sv))(h, sv),
                          r=[PB[bank], bTOP], w=[bIDX])
                        V((lambda h: lambda e: e.max_index(out=IDXU[:, h, 8:16], in_max=TOP1[:, h, 8:16], in_values=SC[:, 0:128]))(h),
                          r=[bSC, bTOP], w=[bIDX])
            V(lambda e: e.tensor_copy(out=IDXF[:], in_=IDXU[:]), r=[bIDX], w=[bIDX])
            for hh in range(2):
                V((lambda hh: lambda e: e.tensor_tensor(
                    out=CAND.rearrange("p h (a b) -> p h a b", a=16),
                    in0=TOP1[:, 4 * hh:4 * hh + 4, :].unsqueeze(3).to_broadcast([128, 4, 16, 16]),
                    in1=TOP2[:, 4 * hh:4 * hh + 4, :].unsqueeze(2).to_broadcast([128, 4, 16, 16]), op=ALU.add))(hh),
                  r=[bTOP], w=[bYO])
                for hl in range(4):
                    h = 4 * hh + hl
                    V((lambda h, hl: lambda e: e.max(out=BEST[:, h, 0:8], in_=CAND[:, hl, :]))(h, hl), r=[bYO], w=[bBEST])
                    V((lambda h, hl: lambda e: e.match_replace(out=CW[:, hl, :], in_to_replace=BEST[:, h, 0:8], in_values=CAND[:, hl, :],
                                                               imm_value=-1e30))(h, hl), r=[bYO, bBEST], w=[bHF])
                    V((lambda h, hl: lambda e: e.max(out=BEST[:, h, 8:16], in_=CW[:, hl, :]))(h, hl), r=[bHF], w=[bBEST])
            mB = BEST[:, :, 0:1]
            V(lambda e: e.tensor_tensor(out=DD[:], in0=BEST[:], in1=mB.to_broadcast([128, 8, 16]), op=ALU.subtract), r=[bBEST], w=[bDD])
            A(lambda e: e.activation(out=DD[:], in_=DD[:], func=AF.Exp), r=[bDD], w=[bDD])
            V(lambda e: e.tensor_reduce(out=ST[:, 0, :], in_=DD[:], axis=AX.X, op=ALU.add), r=[bDD], w=[bST])
            A(lambda e: e.activation(out=ST[:, 1, :], in_=ST[:, 0, :], func=AF.Ln), r=[bST], w=[bST])
            V(lambda e: e.tensor_tensor(out=ST[:, 2, :], in0=ST[:, 1, :], in1=BEST[:, :, 0], op=ALU.add), r=[bST, bBEST], w=[bST])
            V(lambda e: e.tensor_scalar(out=ST[:, 3, :], in0=BEST[:, :, 15], scalar1=-1e-5, scalar2=None, op0=ALU.add), r=[bBEST], w=[bST])
            V(lambda e: e.tensor_tensor(out=C1T[:], in0=TOP1[:], in1=ST[:, 2, :].unsqueeze(2).to_broadcast([128, 8, 16]), op=ALU.subtract),
              r=[bTOP, bST], w=[bCT])
            V(lambda e: e.tensor_tensor(out=ST[:, 4, :], in0=ST[:, 3, :], in1=ST[:, 2, :], op=ALU.subtract), r=[bST], w=[bST])
            A(lambda e: e.activation(out=ST[:, 5, :], in_=ST[:, 4, :], func=AF.Exp), r=[bST], w=[bST])
            V(lambda e: e.tensor_copy(out=C2T[:], in_=ST[:, 5, :].unsqueeze(2).to_broadcast([128, 8, 16])), r=[bST], w=[bCT])
            for q3, SRC in enumerate((C1T, C2T, IDXF)):
                PE((lambda q3, SRC: lambda e: e.transpose(out=pbank[3][:, q3 * 128:(q3 + 1) * 128],
                                                          in_=SRC[:].rearrange("p h a -> p (h a)"), identity=IDF))(q3, SRC),
                   r=[bCT, bIDX, bCST], w=[PB[3]])
            V(lambda e: e.tensor_copy(out=CT3[:].rearrange("p c t -> p (c t)"), in_=pbank[3][:, 0:384]), r=[PB[3]], w=[bCT3])
            def wb_bc(idx):
                half, g = divmod(idx, 16)
                pp = idx % 2
                bb = 4 + pp
                pc = g // 4
                if g % 4 == 0:
                    A((lambda half, pc: lambda e: e.activation(
                        out=QREP[:, 16 * pc:16 * pc + 16, :].rearrange("p t (h a) -> p t h a", h=8),
                        in_=QTP[64:128, :, 64 * half + 16 * pc:64 * half + 16 * pc + 16].rearrange("p h t -> p t h").unsqueeze(3).to_broadcast([64, 16, 8, 16]),
                        func=AF.Copy))(half, pc), r=[bQTP] + bigB, w=[bQp[pc]])
                for j in range(4):
                    tl = g * 4 + j
                    PE(mm(pbank[bb][:, j * 128:(j + 1) * 128], QREP[:, tl, :], KEYB[64:128, 128:256], True, True), r=[bQp[pc], bKEY], w=[PB[bb]])

            def wb_ew(idx):
                half, g = divmod(idx, 16)
                pp = idx % 2
                bb = 4 + pp
                for j in range(4):
                    t = 64 * half + g * 4 + j
                    bcj = pbank[bb][:, j * 128:(j + 1) * 128]
                    A((lambda pp, j, t, bcj: lambda e: e.activation(out=EXB[pp][:, j, :], in_=bcj, func=AF.Exp,
                                                                    bias=CT3[:, 0, t:t + 1]))(pp, j, t, bcj),
                      r=[PB[bb], bCT3, bEXB[pp]], w=[bEXj[pp][j]])
                    V((lambda pp, j, t: lambda e: e.scalar_tensor_tensor(
                        out=RB[pp][:, j, :], in0=EXB[pp][:, j, :], scalar=CT3[:, 1, t:t + 1], in1=EXB[pp][:, j, :],
                        op0=ALU.is_ge, op1=ALU.mult))(pp, j, t), r=[bCT3, bEXj[pp][j], bRB[pp]], w=[bRj[pp][j]])
                    V((lambda pp, j, t: lambda e: e.tensor_scalar(out=E1B[pp][:, j, :], in0=IOTA, scalar1=CT3[:, 2, t:t + 1],
                                                                  scalar2=None, op0=ALU.is_equal))(pp, j, t),
                      r=[bCST, bCT3, bE1B[pp]], w=[bE1j[pp][j]])

            def wb_wt(idx):
                half, g = divmod(idx, 16)
                pp = idx % 2
                wb = 6 + pp
                for j in range(4):
                    PE(mm(pbank[wb][:, j * 128:(j + 1) * 128], RB[pp][:, j, :], E1B[pp][:, j, :], True, True),
                       r=[bRj[pp][j], bE1j[pp][j]], w=[PB[wb]])
                t0 = 64 * half + 4 * g
                A((lambda wb, t0: lambda e: e.activation(out=GWT[:, :, t0:t0 + 4],
                                                         in_=pbank[wb][:, :].rearrange("p (t i) -> p i t", t=4),
                                                         func=AF.Copy))(wb, t0), r=[PB[wb]], w=bigA)
            wb_bc(0)
            wb_bc(1)
            wb_ew(0)
            for idx in range(32):
                if idx + 1 < 32:
                    wb_ew(idx + 1)
                wb_wt(idx)
                if idx + 2 < 32:
                    wb_bc(idx + 2)

            slots = {}

            def de_at(ec):
                su = wload(ut_s[:, ec, :], [bUT[ec // 16]])
                sv_ = wload(pv_s[:, ec, :], [bPV[ec // 16]])
                slots[ec] = sv_
                ab = 2 + ec % 2
                for kc in range(8):
                    PE(mm(pbank[ab][:, 0:128], RING[:, su, kc * 128:(kc + 1) * 128], H2T[:, kc, :], kc == 0, kc == 7),
                       r=[bRING[su], bH2T], w=[PB[ab]])

            def de_ew(ec):
                pp = ec % 2
                ab = 2 + pp
                A((lambda pp, ab: lambda e: e.activation(out=GB[pp][:], in_=pbank[ab][:, 0:128], func=AF.Gelu))(pp, ab),
                  r=[PB[ab]], w=[bGB_[pp]])
                V((lambda pp, ec: lambda e: e.tensor_tensor(out=GWB[pp][:], in0=GB[pp][:], in1=GWT[:, ec, :], op=ALU.mult))(pp, ec),
                  r=[bGB_[pp]] + bigA, w=[bGWB[pp]])

            def de_mm2(ec):
                pp = ec % 2
                sv_ = slots.pop(ec)
                for hf in range(2):
                    PE(mm(pbank[hf][:, :], GWB[pp][:], RING[:, sv_, hf * 512:(hf + 1) * 512], ec == 0, ec == 127),
                       r=[bGWB[pp], bRING[sv_]], w=[PB[hf]])
            de_at(0)
            for ec in range(128):
                if ec + 1 < 128:
                    de_at(ec + 1)
                de_ew(ec)
                de_mm2(ec)
            for hf in range(2):
                V((lambda hf: lambda e: e.tensor_tensor(out=X[:, 512 * hf:512 * hf + 512], in0=X[:, 512 * hf:512 * hf + 512],
                                                        in1=pbank[hf][:, :], op=ALU.add))(hf), r=[PB[hf], bX], w=[bX])

        nchunk = [0]
        try:
          ckpt(0)
          for sq in DBG_SEQS:
            for t in range(NTP):
                chunks = []
                for k in range(2):
                    ci = 2 * t + k
                    g = nchunk[0]
                    nchunk[0] += 1
                    c = dict(xsrc=xp[sq, 64 * ci:64 * ci + 64, :], yout=y_p[sq, 64 * ci:64 * ci + 64, :])
                    c["prev"] = "zero" if ci == 0 else ("carry" if k == 0 else "own")
                    c["state"] = "zero" if ci == 0 else "keep"
                    c["kslots"] = [((g - 2) % 6) if ci >= 2 else None, ((g - 1) % 6) if ci >= 1 else None, g % 6]
                    if t == NTP - 1:
                        c["k_out"] = k_p[sq, 64 * k:64 * k + 64, :]
                        c["v_out"] = v_p[sq, 64 * k:64 * k + 64, :]
                        if k == 1:
                            c["shift_out"] = shift_p[sq:sq + 1, :]
                            c["wkv_out"] = wkv_p[sq]
                    chunks.append(c)
                do_tile(chunks)
          g0 = nchunk[0]
          slots = [(g0 + i) % 6 for i in range(6)]
          CKF = sb("CKF", [128, 128]); bCKF = Buf()
          CVF = VW(SOUT[:, 0:256].rearrange("p (c f) -> p c f", c=2)); bCVF = bSOUT
          chunks = []
          for k in range(2):
              sA, sB, sC = slots[3 * k], slots[3 * k + 1], slots[3 * k + 2]
              DMA((lambda k: lambda e: e.dma_start(out=CKF[:], in_=ck[k]))(k), w=[bCKF])
              for kvh in range(2):
                  PE((lambda kvh: lambda e: e.transpose(out=pbank[7][0:64, kvh * 128:(kvh + 1) * 128], in_=CKF[:, kvh * 64:(kvh + 1) * 64],
                                                        identity=IDF))(kvh), r=[bCKF, bCST], w=[PB[7]])
              for j, sl in enumerate((sA, sB)):
                  V((lambda j, sl: lambda e: e.tensor_copy(
                      out=KTR[:, :, sl, :], in_=pbank[7][0:64, 0:256].rearrange("p (v c t) -> p v c t", v=2, c=2)[:, :, j, :]))(j, sl),
                    r=[PB[7]], w=[bKTR[sl]])
              DMA((lambda k: lambda e: e.dma_start(out=CVF[:], in_=cv[k].rearrange("(c p) f -> p c f", p=64)))(k), w=[bCVF])
              for j, sl in enumerate((sA, sB)):
                  V((lambda j, sl: lambda e: e.tensor_copy(out=VAR[:, sl, :], in_=CVF[:, j, :]))(j, sl), r=[bCVF], w=[bVAR[sl]])
              c = dict(xsrc=xs[k], yout=y_s[k], prev=("state", k), state=("load", k), kslots=[sA, sB, sC],
                       k_out=k_s[k], v_out=v_s[k], shift_out=shift_s[k:k + 1, :], wkv_out=wkv_s[k])
              chunks.append(c)
          do_tile(chunks)
        except _Stop:
            pass
        S.emit()
    return nc, S


def host_layout(inp, do_peer=True):
    f = lambda a: np.ascontiguousarray(a, dtype=np.float32)
    w_in = inp["w_in"][0]
    sh = {}
    sh["win_f"] = f(w_in.reshape(8, 128, 36, 128).transpose(1, 2, 0, 3).reshape(128, 36, 1024))
    mu = inp["mu_shift"][0]
    sh["mu_row"] = f(mu)
    par = np.zeros((128, NPAR), np.float32)
    hj = lambda v: np.asarray(v).reshape(8, 64).T
    par[0:64, 27:35] = hj(inp["k_k"][0]); par[0:64, 35:43] = hj(inp["k_a"][0])
    par[0:64, 59:67] = hj(inp["r_k"][0].reshape(512))
    sh["par"] = par
    sh["cst"] = make_consts()
    sh["n1g"] = f(inp["norm1_g"][0]); sh["n2g"] = f(inp["norm2_g"][0]); sh["fg"] = f(inp["final_g"])
    sh["w0row"] = f(inp["w_decay0"][0].reshape(1, 512)); sh["a0row"] = f(inp["a_icl0"][0].reshape(1, 512))
    sh["lw"] = f(inp["w_decay_lora"][0]); sh["la"] = f(inp["a_icl_lora"][0]); sh["gl"] = f(inp["g_lora"][0])
    sh["lnw"] = f(inp["lnx_w"][0]); sh["lnb"] = f(inp["lnx_b"][0]); sh["sinks"] = f(inp["attn_sinks"][0])
    wb = inp["w_branch"][0]
    sh["wb0_f"] = f(wb[0].reshape(4, 128, 4, 2, 128).transpose(1, 2, 0, 3, 4).reshape(128, 4, 1024))
    sh["wb1_f"] = f(wb[1].reshape(8, 64, 8, 128).transpose(1, 2, 0, 3).reshape(64, 8, 1024))
    sh["wo_f"] = f(inp["w_out"][0].reshape(8, 128, 8, 128).transpose(1, 2, 0, 3).reshape(128, 8, 1024))
    if do_peer:
        sh["wq_f"] = f(inp["peer_wq"][0].reshape(8, 128, 8, 128).transpose(1, 2, 0, 3).reshape(128, 8, 1024))
        sk = inp["peer_sub_keys"][0]
        keys = np.zeros((128, 256), np.float32)
        keys[0:64, 0:128] = sk[0].T
        keys[64:128, 128:256] = sk[1].T
        sh["keys_f"] = keys
        sh["ut_f"] = f(inp["peer_u"][0].reshape(128, 128, 8, 128).transpose(3, 0, 2, 1).reshape(128, 128, 1024))
        sh["pv_f"] = f(inp["peer_v"][0].reshape(128, 128, 1024).transpose(1, 0, 2))
    return sh


def core_inputs(inp, c, sh):
    f = lambda a: np.ascontiguousarray(a, dtype=np.float32)
    m = dict(sh)
    m["xp"] = f(inp["x_prompt"][2 * c:2 * c + 2]); m["xs"] = f(inp["x_sample"][2 * c:2 * c + 2])
    m["st_shift"] = f(inp["state_shift"][0, 2 * c:2 * c + 2]); m["st_wkv"] = f(inp["state_wkv"][0, 2 * c:2 * c + 2])
    m["ck"] = f(inp["cache_k"][0, 2 * c:2 * c + 2].reshape(2, 128, 128)); m["cv"] = f(inp["cache_v"][0, 2 * c:2 * c + 2].reshape(2, 128, 128))
    return m


DO_PEER = True
_CACHE = {}


def kernel(**inp):
    B, SEQ, _ = inp["x_prompt"].shape
    NTP = SEQ // 128
    key = (NTP, DO_PEER)
    if key not in _CACHE:
        _CACHE[key] = build(NTP, DO_PEER)[0]
    nc = _CACHE[key]
    sh = host_layout(inp, DO_PEER)
    in_maps = [core_inputs(inp, c, sh) for c in range(8)]
    res = run_bass_kernel_spmd(nc, in_maps, core_ids=list(range(8)))
    R = res.results
    cat = lambda k: np.concatenate([np.asarray(r[k]) for r in R], axis=0)
    y_p = cat("y_p"); y_s = cat("y_s")
    return (y_p, y_s, cat("shift_p")[None], cat("wkv_p")[None],
            cat("k_p").reshape(1, 16, 128, 2, 64), cat("v_p").reshape(1, 16, 128, 2, 64),
            cat("shift_s")[None], cat("wkv_s")[None],
            cat("k_s").reshape(1, 16, 64, 2, 64), cat("v_s").reshape(1, 16, 64, 2, 64))
```

```python
import contextlib
import math
import numpy as np
import concourse.bass as bass
import concourse.mybir as mybir
from concourse.bass_utils import run_bass_kernel_spmd

F32 = mybir.dt.float32
BF16 = mybir.dt.bfloat16
U32 = mybir.dt.uint32
AF = mybir.ActivationFunctionType
ALU = mybir.AluOpType
AX = mybir.AxisListType

COMPUTE = ("tensor", "vector", "scalar", "gpsimd")
D = 1024
RW_COLS = 1792
EXPM05 = math.exp(-0.5)


class P64:
    def __init__(self, t):
        self.t = t

    def __getitem__(self, idx):
        if not isinstance(idx, tuple):
            idx = (idx,)
        assert idx[0] == slice(None)
        return self.t[(slice(0, 64),) + tuple(idx[1:])]


class P3:
    def __init__(self, t):
        self.t = t

    def __getitem__(self, idx):
        v = self.t[:].rearrange("p (a t) -> p a t", a=8)
        return v[idx]


class VW:
    def __init__(self, ap):
        self.ap = ap

    def __getitem__(self, idx):
        return self.ap[idx]


class Buf:
    __slots__ = ("name", "w", "r", "excl")

    def __init__(self, name="", excl=False):
        self.name = name
        self.w = None
        self.r = []
        self.excl = excl


class Op:
    __slots__ = ("eng", "fn", "deps", "isdma", "sem", "semval", "needinc")

    def __init__(self, eng, fn, isdma):
        self.eng = eng
        self.fn = fn
        self.isdma = isdma
        self.deps = set()
        self.needinc = False
        self.sem = None
        self.semval = None


class Sched:
    NDMASEM = 16

    def __init__(self, nc):
        self.nc = nc
        self.ops = []

    def op(self, eng, fn, reads=(), writes=(), dma=False):
        idx = len(self.ops)
        o = Op(eng, fn, dma)
        reads = list(reads)
        writes = list(writes)
        for b in list(reads):
            if b.excl:
                reads.remove(b)
                if b not in writes:
                    writes.append(b)
        for b in reads:
            if b.w is not None:
                o.deps.add(b.w)
        for b in writes:
            if b.w is not None:
                o.deps.add(b.w)
            for r in b.r:
                o.deps.add(r)
        for b in reads:
            b.r.append(idx)
        for b in writes:
            b.w = idx
            b.r = []
        o.deps.discard(idx)
        self.ops.append(o)
        return idx

    def emit(self):
        nc = self.nc
        ops = self.ops
        engs = []
        for o in ops:
            if o.eng not in engs:
                engs.append(o.eng)
        for o in ops:
            for d in list(o.deps):
                po = ops[d]
                if (not po.isdma) and (not o.isdma) and po.eng == "tensor" and o.eng == "tensor":
                    o.deps.discard(d)
        with contextlib.ExitStack() as st:
            csem = {e: st.enter_context(nc.semaphore("cs_" + e)) for e in COMPUTE}
            dengs = set(o.eng for o in ops if o.isdma)
            dsems = {e: ([st.enter_context(nc.semaphore("ds_%s_%d" % (e, k))) for k in range(self.NDMASEM)] if e in dengs else [])
                     for e in engs}
            dcount = {e: 0 for e in engs}
            dhist = {e: [[] for _ in range(self.NDMASEM)] for e in engs}
            for i, o in enumerate(ops):
                if o.isdma:
                    k = dcount[o.eng] % self.NDMASEM
                    dcount[o.eng] += 1
                    o.sem = dsems[o.eng][k]
                    h = dhist[o.eng][k]
                    if h:
                        o.deps.add(h[-1])
                    h.append(i)
                    o.semval = 16 * len(h)
            for o in ops:
                for d in o.deps:
                    ops[d].needinc = True
            ccount = {e: 0 for e in COMPUTE}
            for o in ops:
                if (not o.isdma) and o.needinc:
                    ccount[o.eng] += 1
                    o.sem = csem[o.eng]
                    o.semval = ccount[o.eng]
            per_eng = {e: [] for e in engs}
            for i, o in enumerate(ops):
                per_eng[o.eng].append(i)
            self.stats = {e: len(per_eng[e]) for e in engs}
            block = st.enter_context(nc.Block())

            def make(e):
                def body(engine):
                    seen = {}
                    for i in per_eng[e]:
                        o = ops[i]
                        need = {}
                        for d in o.deps:
                            po = ops[d]
                            if po.sem is None:
                                continue
                            key = id(po.sem)
                            if key not in need or need[key][1] < po.semval:
                                need[key] = (po.sem, po.semval)
                        for key, (sem, val) in need.items():
                            if seen.get(key, 0) >= val:
                                continue
                            engine.wait_ge(sem, val)
                            seen[key] = val
                        ins = o.fn(engine)
                        if o.isdma:
                            ins.then_inc(o.sem, 16)
                        elif o.needinc:
                            ins.then_inc(o.sem, 1)
                    for k in range(len(dsems[e])):
                        h = dhist[e][k]
                        if h and seen.get(id(dsems[e][k]), 0) < 16 * len(h):
                            engine.wait_ge(dsems[e][k], 16 * len(h))
                return body

            for e in engs:
                getattr(block, e)(make(e))


NCST = 448 + 1536


def make_consts():
    c = np.zeros((128, NCST), np.float32)
    c[:, 0:128] = np.eye(128, dtype=np.float32)
    c[:, 128:256] = np.arange(128, dtype=np.float32)[None, :]
    s = np.arange(64)[:, None]
    t = np.arange(64)[None, :]
    c[0:64, 256:320] = (s < t)
    c[0:64, 320:384] = (s <= t)
    c[0:64, 384:448] = (t < s)
    slopes = np.exp2(-8.0 * np.arange(1, 9, dtype=np.float64) / 8.0)
    al = np.zeros((64, 2, 3, 4, 64), np.float64)
    tk = np.arange(64)[:, None]
    tq = np.arange(64)[None, :]
    for kvh in range(2):
        for kc in range(3):
            for g in range(4):
                dist = np.abs(128 - 64 * kc + tq - tk)
                al[:, kvh, kc, g, :] = np.exp(-slopes[kvh * 4 + g] * dist)
    c[0:64, 448:448 + 1536] = al.reshape(64, 1536)
    return c


NPAR = 67
DBG_STOP = None
DBG_SEQS = (0, 1)


class _Stop(Exception):
    pass


_HITS = {}


def ckpt(n):
    if DBG_STOP is None:
        return
    _HITS[n] = _HITS.get(n, 0) + 1
    if DBG_STOP % 100 == n and _HITS[n] - 1 == DBG_STOP // 100:
        raise _Stop()


def build(NTP, do_peer=True):
    SEQ = NTP * 128
    nc = bass.Bass("TRN2", target_bir_lowering=False)
    S = Sched(nc)

    def din(name, shape, dt=F32):
        return nc.dram_tensor(name, list(shape), dt, kind="ExternalInput").ap()

    def dout(name, shape):
        return nc.dram_tensor(name, list(shape), F32, kind="ExternalOutput").ap()

    def dscr(name, shape, dt=BF16):
        return nc.dram_tensor(name, list(shape), dt, kind="Internal").ap()

    xp = din("xp", [2, SEQ, D]); xs = din("xs", [2, 64, D])
    st_shift = din("st_shift", [2, D]); st_wkv = din("st_wkv", [2, 8, 64, 64])
    ck = din("ck", [2, 128, 128]); cv = din("cv", [2, 128, 128])
    cst = din("cst", [128, NCST]); par = din("par", [128, NPAR])
    n1g = din("n1g", [D]); n2g = din("n2g", [D]); fg = din("fg", [D])
    mu_row = din("mu_row", [14 * 128])
    win_f = din("win_f", [128, 36, 1024])
    w0row = din("w0row", [1, 512]); a0row = din("a0row", [1, 512])
    lw = din("lw", [64, 512]); la = din("la", [64, 512]); gl = din("gl", [128, 512])
    lnw = din("lnw", [512]); lnb = din("lnb", [512]); sinks = din("sinks", [8])
    wb0_f = din("wb0_f", [128, 4, 1024]); wb1_f = din("wb1_f", [64, 8, 1024])
    wo_f = din("wo_f", [128, 8, 1024])
    if do_peer:
        wq_f = din("wq_f", [128, 8, 1024]); keys_f = din("keys_f", [128, 256])
        ut_f = din("ut_f", [128, 128, 1024]); pv_f = din("pv_f", [128, 128, 1024])

    y_p = dout("y_p", [2, SEQ, D]); y_s = dout("y_s", [2, 64, D])
    shift_p = dout("shift_p", [2, D]); wkv_p = dout("wkv_p", [2, 8, 64, 64])
    k_p = dout("k_p", [2, 128, 128]); v_p = dout("v_p", [2, 128, 128])
    shift_s = dout("shift_s", [2, D]); wkv_s = dout("wkv_s", [2, 8, 64, 64])
    k_s = dout("k_s", [2, 64, 128]); v_s = dout("v_s", [2, 64, 128])

    win_s = dscr("win_s", [128, 50, 1024])
    wb0_s = dscr("wb0_s", [128, 4, 1024]); wb1_s = dscr("wb1_s", [64, 8, 1024]); wo_s = dscr("wo_s", [128, 8, 1024])
    if do_peer:
        wq_s = dscr("wq_s", [128, 8, 1024]); ut_s = dscr("ut_s", [128, 128, 1024]); pv_s = dscr("pv_s", [128, 128, 1024])

    with contextlib.ExitStack() as es:
        def sb(name, shape, dt=F32):
            return es.enter_context(nc.sbuf_tensor(name, list(shape), dt))

        pbank = [es.enter_context(nc.psum_tensor("pb%d" % k, [128, 512], F32)) for k in range(8)]
        PB = [Buf("pb%d" % k, excl=True) for k in range(8)]

        def V(fn, r=(), w=()): S.op("vector", fn, r, w)
        def A(fn, r=(), w=()): S.op("scalar", fn, r, w)
        def G(fn, r=(), w=()): S.op("gpsimd", fn, r, w)
        def PE(fn, r=(), w=()): S.op("tensor", fn, r, w)
        def DMA(fn, r=(), w=(), q="sync"): S.op(q, fn, r, w, dma=True)

        CST = sb("CST", [128, NCST]); bCST = Buf()
        PAR = sb("PAR", [128, NPAR]); bPAR = Buf()
        DMA(lambda e: e.dma_start(out=CST[:], in_=cst), w=[bCST])
        DMA(lambda e: e.dma_start(out=PAR[:], in_=par), w=[bPAR])
        IDF = CST[:, 0:128]
        IOTA = CST[:, 128:256]
        M1 = CST[0:64, 256:384]
        ML = CST[0:64, 384:448]
        ALIBI = CST[0:64, 448:448 + 1536]
        IDB = sb("IDB", [128, 128], BF16); bIDB = Buf()
        V(lambda e: e.tensor_copy(out=IDB[:], in_=IDF), r=[bCST], w=[bIDB])
        ONESF = sb("ONESF", [128, 128]); bONES = Buf()
        ONESB = sb("ONESB", [64, 64], BF16)
        V(lambda e: e.memset(ONESF[:], 1.0), w=[bONES])
        V(lambda e: e.memset(ONESB[:], 1.0), w=[bONES])
        G1B = sb("G1B", [128, D]); G2B = sb("G2B", [128, D]); GFB = sb("GFB", [128, D]); bGB = Buf()
        DMA(lambda e: e.dma_start(out=G1B[:], in_=n1g.partition_broadcast(128)), w=[bGB])
        DMA(lambda e: e.dma_start(out=G2B[:], in_=n2g.partition_broadcast(128)), w=[bGB])
        DMA(lambda e: e.dma_start(out=GFB[:], in_=fg.partition_broadcast(128)), w=[bGB])
        LNW = sb("LNW", [64, 512]); LNB = sb("LNB", [64, 512]); SNK = sb("SNK", [64, 8]); bLN = Buf()
        DMA(lambda e: e.dma_start(out=LNW[:], in_=lnw.partition_broadcast(64)), w=[bLN])
        DMA(lambda e: e.dma_start(out=LNB[:], in_=lnb.partition_broadcast(64)), w=[bLN])
        DMA(lambda e: e.dma_start(out=SNK[:], in_=sinks.partition_broadcast(64)), w=[bLN])
        SNKE = sb("SNKE", [64, 8]); bSNK = Buf()
        A(lambda e: e.activation(out=SNKE[:], in_=SNK[:], func=AF.Exp), r=[bLN], w=[bSNK])
        LWX = sb("LWX", [128, 1024]); GL = sb("GL", [128, 512]); bLO = Buf()
        DMA(lambda e: e.dma_start(out=LWX[64:65, 0:512], in_=w0row), w=[bLO])
        DMA(lambda e: e.dma_start(out=LWX[64:65, 512:1024], in_=a0row), w=[bLO])
        DMA(lambda e: e.dma_start(out=LWX[0:64, 0:512], in_=lw), w=[bLO])
        DMA(lambda e: e.dma_start(out=LWX[0:64, 512:1024], in_=la), w=[bLO])
        DMA(lambda e: e.dma_start(out=GL[:], in_=gl), w=[bLO])
        P_KK = PAR[0:64, 27:35]; P_KA = PAR[0:64, 35:43]; P_RK = PAR[0:64, 59:67]

        bWIN = [Buf() for _ in range(50)]
        X = sb("X", [128, D]); bX = Buf()
        HB = sb("HB", [128, D], BF16); bHB = Buf()
        JNK = HB; bJNK = bHB
        MRG = sb("MRG", [128, 8, 128], BF16); bMRG = Buf()
        MRGF = VW(MRG[:].rearrange("p a t -> p (a t)"))
        KRf = sb("KRf", [128, 8, 2, 2, 64]); BKf = sb("BKf", [128, 8, 2, 2, 64]); bKR = Buf(); bBK = Buf()
        KR = P64(KRf); BK = P64(BKf)
        MUB = KRf[:].rearrange("p a b c d -> p (a b c d)"); OMUB = BKf[:].rearrange("p a b c d -> p (a b c d)")
        bMU = Buf()
        DMA(lambda e: e.dma_start(out=MUB[:, 0:1792], in_=mu_row.partition_broadcast(128)), w=[bMU, bKR, bBK])
        V(lambda e: e.tensor_scalar(out=OMUB[:, 0:1792], in0=MUB[:, 0:1792], scalar1=-1.0, scalar2=1.0, op0=ALU.mult, op1=ALU.add),
          r=[bMU], w=[bMU])
        STGl = [X]; bSTG = [bX]
        STOl = [HB, MRGF]; bSTO = [bHB, bMRG]
        nst = [0]
        for s in range(14):
            k = 0
            DMA((lambda s, k: lambda e: e.dma_start(out=STGl[k][:], in_=win_f[:, s, :]))(s, k), w=[bSTG[k]])
            for half, (MM, dst) in enumerate(((OMUB, s), (MUB, 36 + s))):
                o = nst[0] % 2
                nst[0] += 1
                eng = V if half == 0 else G
                eng((lambda s, k, o, MM: lambda e: e.tensor_tensor(
                    out=STOl[o][:].rearrange("p (a m) -> p a m", a=8),
                    in0=STGl[k][:].rearrange("p (a m) -> p a m", a=8),
                    in1=MM[:, s * 128:(s + 1) * 128].unsqueeze(1).to_broadcast([128, 8, 128]),
                    op=ALU.mult))(s, k, o, MM), r=[bSTG[k], bMU, bKR, bBK], w=[bSTO[o]])
                DMA((lambda o, dst: lambda e: e.dma_start(out=win_s[:, dst, :], in_=STOl[o][:]))(o, dst),
                    r=[bSTO[o]], w=[bWIN[dst]])
        for s0 in range(14, 36, 11):
            DMA((lambda s0: lambda e: e.dma_start(out=win_s[:, s0:s0 + 11, :], in_=win_f[:, s0:s0 + 11, :]))(s0),
                w=[bWIN[s] for s in range(s0, s0 + 11)], q="gpsimd")
        bWB0 = Buf(); bWB1 = Buf(); bWO = Buf()
        DMA(lambda e: e.dma_start(out=wb0_s, in_=wb0_f), w=[bWB0], q="gpsimd")
        DMA(lambda e: e.dma_start(out=wb1_s, in_=wb1_f), w=[bWB1], q="gpsimd")
        DMA(lambda e: e.dma_start(out=wo_s, in_=wo_f), w=[bWO], q="gpsimd")
        if do_peer:
            bWQ = Buf()
            DMA(lambda e: e.dma_start(out=wq_s, in_=wq_f), w=[bWQ], q="gpsimd")
            bUT = [Buf() for _ in range(8)]; bPV = [Buf() for _ in range(8)]
            for k in range(8):
                DMA((lambda k: lambda e: e.dma_start(out=ut_s[:, 16 * k:16 * k + 16, :], in_=ut_f[:, 16 * k:16 * k + 16, :]))(k),
                    w=[bUT[k]], q="gpsimd")
                DMA((lambda k: lambda e: e.dma_start(out=pv_s[:, 16 * k:16 * k + 16, :], in_=pv_f[:, 16 * k:16 * k + 16, :]))(k),
                    w=[bPV[k]], q="gpsimd")
            KEYF = sb("KEYF", [128, 256]); KEYB = sb("KEYB", [128, 256], BF16); bKEY = Buf()
            DMA(lambda e: e.dma_start(out=KEYF[:], in_=keys_f), w=[bKEY])
            V(lambda e: e.tensor_copy(out=KEYB[:], in_=KEYF[:]), r=[bKEY], w=[bKEY])

        stopped = [False]
        NS = 8
        RING = sb("RING", [128, NS, 1024], BF16)
        bRING = [Buf() for _ in range(NS)]
        rcount = [0]

        def wload(src_ap, deps, npart=128):
            k = rcount[0] % NS
            rcount[0] += 1
            DMA(lambda e: e.dma_start(out=RING[0:npart, k, :], in_=src_ap), r=list(deps), w=[bRING[k]])
            return k

        SSQ = sb("SSQ", [128, 4]); bSSQ = Buf()
        HF = sb("HF", [128, D]); bHF = Buf()
        HT = sb("HT", [128, 8, 2, 65], BF16); bHT = Buf()
        CAR = sb("CAR", [128, 8, 1], BF16); bCAR = Buf()
        STS = sb("STS", [128, 2, 8]); bSTS = Buf()
        BIGA = sb("BIGA", [128, 8192]); BIGB = sb("BIGB", [128, 4096])
        def carveA(i, three):
            a = BIGA[0:64, i * 1024:(i + 1) * 1024]
            return VW(a.rearrange("p (h t) -> p h t", h=8) if three else a)
        def carveB(i):
            return VW(BIGB[0:64, i * 1024:(i + 1) * 1024].rearrange("p (h t) -> p h t", h=8))
        RF = carveA(0, True); KF = carveA(1, True); VT = carveA(2, True)
        bRF = Buf(); bKF = Buf(); bVT = Buf()
        TW = sb("TW", [64, 128]); XA = sb("XA", [64, 128]); SG = sb("SG", [128, 128]); bTW = Buf(); bXA = Buf(); bSG = Buf()
        LD = carveA(3, False); CC = carveA(4, False); CE = carveA(5, False); CUM = carveA(6, False)
        CUMX = CE; EP = CC; EPX = CE; EM = CUM
        bLD = Buf(); bCC = Buf(); bCE = Buf(); bCUM = Buf(); bCUMX = bCE; bEP = bCC; bEPX = bCE; bEM = bCUM
        AS = carveA(7, True); bAS = Buf()
        KKR = carveB(0); SQ = carveB(1); RN = SQ; KK = KKR
        bKKR = Buf(); bSQ = Buf(); bRN = bSQ; bKK = bKKR
        T1 = carveB(2); KP = T1; BB = carveB(3); RKP = sb("RKP", [64, 8, 128])
        bT1 = Buf(); bKP = bT1; bBB = Buf(); bRKP = Buf()
        QT = sb("QT", [64, 8, 128], BF16); bQT = Buf()
        KTR = sb("KTR", [64, 2, 6, 64], BF16); bKTR = [Buf() for _ in range(6)]
        VAR = sb("VAR", [64, 6, 128], BF16); bVAR = [Buf() for _ in range(6)]
        KTF = sb("KTF", [64, 2, 128]); bKTF = Buf()
        VAF = sb("VAF", [64, 2, 128]); bVAF = Buf()
        KTOK = sb("KTOK", [128, 128]); bKTOK = Buf()
        SGT = sb("SGT", [128, 16, 128], BF16); bSGT = Buf()
        def scanbuf(name):
            f = sb(name, [128, 512])
            return f, [VW(f[0:64, :].rearrange("p (h s) -> p h s", h=8))]
        NNf, NN = scanbuf("NNf"); QQf, QQ = scanbuf("QQf"); NN2f, NN2 = scanbuf("NN2f"); QQ2f, QQ2 = scanbuf("QQ2f")
        XXf, XX = scanbuf("XXf"); ARBf, ARB = scanbuf("ARBf"); AKTf, AKT = scanbuf("AKTf"); ARKf, ARK = scanbuf("ARKf")
        bNN = [Buf(), Buf()]; bQQ = [Buf(), Buf()]; bNN2 = [Buf(), Buf()]; bQQ2 = [Buf(), Buf()]; bXX = [Buf(), Buf()]
        bARB = [Buf(), Buf()]; bAKT = [Buf(), Buf()]; bARK = [Buf(), Buf()]
        VR = [sb("VR%d" % i, [64, 8, 64]) for i in range(1)]; bVR = [Buf(), Buf()]
        BKT = [sb("BKT%d" % i, [64, 8, 2, 64]) for i in range(1)]; bBKT = [Buf(), Buf()]
        RKB = [sb("RKB%d" % i, [64, 8]) for i in range(1)]; bRKB = [Buf(), Buf()]
        HH = sb("HH", [64, 8, 64]); bHH = Buf()
        WN = sb("WN", [64, 8, 64]); UU = sb("UU", [64, 8, 64]); bWN = Buf(); bUU = Buf()
        HTMP = sb("HTMP", [64, 8, 64]); bHTMP = Buf()
        SIN = sb("SIN", [64, 8, 64]); bSIN = Buf()
        YS = sb("YS", [64, 8, 64]); YQ = sb("YQ", [64, 8, 64]); bYS = Buf(); bYQ = Buf()
        ST8 = sb("ST8", [64, 6, 8]); bST8 = Buf()
        GG = sb("GG", [64, 512]); bGG = Buf()
        ORW = sb("ORW", [64, 512], BF16); bORW = Buf()
        ORWT = sb("ORWT", [128, 4, 128], BF16); bORWT = Buf()
        OATT = sb("OATT", [64, 8, 128], BF16); bOATT = Buf()
        EE = sb("EE", [64, 3, 256]); bEE = Buf()
        PT = sb("PT", [64, 3, 256], BF16); bPT = Buf()
        DEN = sb("DEN", [64, 256]); bDEN = Buf()
        YO = sb("YO", [128, D]); bYO = Buf()
        MT1 = P3(HF); MT2 = P3(YO)
        bMT1 = bHF; bMT2 = bYO
        SOUT = sb("SOUT", [64, 512]); bSOUT = Buf()

        def rmsnorm(src, bsrc, gB, dstf, bdstf, col):
            A(lambda e: e.activation(out=JNK[:], in_=src[:], func=AF.Square, accum_out=SSQ[:, col:col + 1]),
              r=[bsrc], w=[bJNK, bSSQ])
            A(lambda e: e.activation(out=SSQ[:, col + 1:col + 2], in_=SSQ[:, col:col + 1], func=AF.Sqrt,
                                     scale=1.0 / D, bias=EPS_T[:, 0:1]), r=[bSSQ, bEPS], w=[bSSQ])
            V(lambda e: e.reciprocal(out=SSQ[:, col + 1:col + 2], in_=SSQ[:, col + 1:col + 2]), r=[bSSQ], w=[bSSQ])
            V(lambda e: e.scalar_tensor_tensor(out=dstf[:], in0=src[:], scalar=SSQ[:, col + 1:col + 2], in1=gB[:],
                                               op0=ALU.mult, op1=ALU.mult), r=[bsrc, bSSQ, bGB], w=[bdstf])

        EPS_T = sb("EPS_T", [128, 2]); bEPS = Buf()
        V(lambda e: e.memset(EPS_T[:, 0:1], 1e-6), w=[bEPS])
        V(lambda e: e.memset(EPS_T[:, 1:2], 64e-5), w=[bEPS])

        def mm(out, lhsT, rhs, start, stop):
            return lambda e: e.matmul(out, lhsT=lhsT, rhs=rhs, start=start, stop=stop)

        gchunk = [0]

        def do_tile(chunks):
            for k, c in enumerate(chunks):
                DMA((lambda k, c: lambda e: e.dma_start(out=X[64 * k:64 * k + 64, :], in_=c["xsrc"]))(k, c), w=[bX])
            G(lambda e: e.tensor_copy(out=CAR[:], in_=HT[:, :, 1, 64:65]), r=[bHT], w=[bCAR])
            rmsnorm(X, bX, G1B, HF, bHF, 0)
            A(lambda e: e.activation(out=HB[:], in_=HF[:], func=AF.Copy), r=[bHF], w=[bHB])
            for k, c in enumerate(chunks):
                if c.get("shift_out") is not None:
                    DMA((lambda k, c: lambda e: e.dma_start(out=c["shift_out"], in_=HF[64 * k + 63:64 * k + 64, :]))(k, c),
                        r=[bHF])
            hb = pbank[7][:].bitcast(BF16)
            for kc in range(8):
                PE((lambda kc: lambda e: e.transpose(out=hb[:, kc * 128:(kc + 1) * 128], in_=HB[:, kc * 128:(kc + 1) * 128],
                                                     identity=IDB[:]))(kc), r=[bHB, bIDB], w=[PB[7]])
            hbv = hb.rearrange("p (a c t) -> p a c t", a=8, c=2)
            A(lambda e: e.activation(out=HT[:, :, 0, 1:65], in_=hbv[:, :, 0, :], func=AF.Copy), r=[PB[7]], w=[bHT])
            V(lambda e: e.tensor_copy(out=HT[:, :, 1, 1:65], in_=hbv[:, :, 1, :]), r=[PB[7]], w=[bHT])
            for k, c in enumerate(chunks):
                pv = c["prev"]
                if pv == "zero":
                    G((lambda k: lambda e: e.memset(HT[:, :, k, 0:1], 0.0))(k), w=[bHT])
                elif pv == "carry":
                    G((lambda k: lambda e: e.tensor_copy(out=HT[:, :, k, 0:1], in_=CAR[:]))(k), r=[bCAR], w=[bHT])
                elif pv == "own":
                    G((lambda k: lambda e: e.tensor_copy(out=HT[:, :, k, 0:1], in_=HT[:, :, k - 1, 64:65]))(k), r=[bHT], w=[bHT])
                else:
                    b = pv[1]
                    def ld_state(e, k=k, b=b):
                        with nc.allow_non_contiguous_dma(reason="tiny"):
                            return e.dma_start(out=STS[:, k, :], in_=st_shift[b].rearrange("(a p) -> p a", p=128))
                    DMA(ld_state, w=[bSTS])
                    G((lambda k: lambda e: e.tensor_copy(out=HT[:, :, k, 0:1], in_=STS[:, k, :].unsqueeze(2)))(k), r=[bSTS], w=[bHT])

            ckpt(1)
            def proj_fm(slot_a, slot_b, col0, M, outap, wbank):
                n = 16 if slot_b is not None else 8
                i = 0
                for kc in range(8):
                    PE(mm(outap, RING[:, slot_a, kc * 128 + col0:kc * 128 + col0 + M], HT[:, kc, :, 1:65], i == 0, i == n - 1),
                       r=[bRING[slot_a], bHT], w=[wbank])
                    i += 1
                if slot_b is not None:
                    for kc in range(8):
                        PE(mm(outap, RING[:, slot_b, kc * 128 + col0:kc * 128 + col0 + M], HT[:, kc, :, 0:64], i == 0, i == n - 1),
                           r=[bRING[slot_b], bHT], w=[wbank])
                        i += 1

            def pview(bank, idx, M):
                return pbank[bank][0:M, idx * 128:(idx + 1) * 128].rearrange("p (c t) -> p c t", c=2)

            for qi, (dst, bdst) in enumerate(((RF, bRF), (KF, bKF), (VT, bVT))):
                for sp in range(4):
                    sa = wload(win_s[:, qi * 4 + sp, :], [bWIN[qi * 4 + sp]])
                    sbb = wload(win_s[:, 36 + qi * 4 + sp, :], [bWIN[36 + qi * 4 + sp]])
                    bank = sp // 2
                    for hl in range(2):
                        h = sp * 2 + hl
                        proj_fm(sa, sbb, hl * 64, 64, pview(bank, h % 4, 64), PB[bank])
                for bank in range(2):
                    eng = A if bank == 0 else V
                    if bank == 0:
                        A((lambda dst, bank: lambda e: e.activation(
                            out=dst[:, 4 * bank:4 * bank + 4, :], in_=pbank[bank][0:64, :].rearrange("p (h t) -> p h t", h=4),
                            func=AF.Copy))(dst, bank), r=[PB[bank]], w=[bdst])
                    else:
                        V((lambda dst, bank: lambda e: e.tensor_copy(
                            out=dst[:, 4 * bank:4 * bank + 4, :], in_=pbank[bank][0:64, :].rearrange("p (h t) -> p h t", h=4)))(dst, bank),
                          r=[PB[bank]], w=[bdst])
            ckpt(10)
            sa = wload(win_s[:, 12, :], [bWIN[12]]); sbb = wload(win_s[:, 48, :], [bWIN[48]])
            proj_fm(sa, sbb, 0, 64, pview(2, 0, 64), PB[2])
            proj_fm(sa, sbb, 64, 64, pview(2, 1, 64), PB[2])
            sa = wload(win_s[:, 13, :], [bWIN[13]]); sbb = wload(win_s[:, 49, :], [bWIN[49]])
            proj_fm(sa, sbb, 0, 128, pview(2, 2, 128), PB[2])
            A(lambda e: e.activation(out=TW[:], in_=pbank[2][0:64, 0:128], func=AF.Tanh), r=[PB[2]], w=[bTW])
            V(lambda e: e.tensor_copy(out=XA[:], in_=pbank[2][0:64, 128:256]), r=[PB[2]], w=[bXA])
            A(lambda e: e.activation(out=SG[:], in_=pbank[2][:, 256:384], func=AF.Sigmoid), r=[PB[2]], w=[bSG])
            ckpt(11)
            for (LOF, INP, bINP, b0) in ((0, TW, bTW, 3), (512, XA, bXA, 5)):
                for h in range(8):
                    bank = b0 + h // 4
                    o = pbank[bank][0:64, (h % 4) * 128:(h % 4 + 1) * 128]
                    PE(mm(o, LWX[0:64, LOF + h * 64:LOF + (h + 1) * 64], INP[:], True, False), r=[bLO, bINP], w=[PB[bank]])
                    PE(mm(o, LWX[64:65, LOF + h * 64:LOF + (h + 1) * 64], ONESF[64:65, :], False, True), r=[bLO, bONES], w=[PB[bank]])
            for bank in range(2):
                A((lambda bank: lambda e: e.activation(out=LD[:, 512 * bank:512 * bank + 512], in_=pbank[3 + bank][0:64, :],
                                                       func=AF.Sigmoid))(bank), r=[PB[3 + bank]], w=[bLD])
                A((lambda bank: lambda e: e.activation(out=AS[:, 4 * bank:4 * bank + 4, :],
                                                       in_=pbank[5 + bank][0:64, :].rearrange("p (h t) -> p h t", h=4),
                                                       func=AF.Sigmoid))(bank), r=[PB[5 + bank]], w=[bAS])
            ckpt(12)
            G(lambda e: e.tensor_scalar(out=LD[:], in0=LD[:], scalar1=-EXPM05, scalar2=None, op0=ALU.mult), r=[bLD], w=[bLD])
            V(lambda e: e.tensor_tensor_scan(out=CC[:], data0=ONESF[0:64, 0:1].to_broadcast([64, 1024]), data1=LD[:], initial=0.0, op0=ALU.mult, op1=ALU.add),
              r=[bLD, bONES], w=[bCC])
            G(lambda e: e.tensor_tensor(out=CE[:], in0=CC[:], in1=LD[:], op=ALU.subtract), r=[bCC, bLD], w=[bCE])
            V(lambda e: e.tensor_tensor(out=CUM[:].rearrange("p (a t) -> p a t", t=64),
                                        in0=CC[:].rearrange("p (a t) -> p a t", t=64),
                                        in1=CE[:].rearrange("p (a t) -> p a t", t=64)[:, :, 0:1].to_broadcast([64, 16, 64]),
                                        op=ALU.subtract), r=[bCC, bCE], w=[bCUM])
            G(lambda e: e.tensor_tensor(out=CUMX[:], in0=CUM[:], in1=LD[:], op=ALU.subtract), r=[bCUM, bLD], w=[bCUMX])
            A(lambda e: e.activation(out=EP[:], in_=CUM[:], func=AF.Exp), r=[bCUM], w=[bEP])
            A(lambda e: e.activation(out=EPX[:], in_=CUMX[:], func=AF.Exp), r=[bCUMX], w=[bEPX])
            A(lambda e: e.activation(out=EM[:], in_=CUM[:], func=AF.Exp, scale=-1.0), r=[bCUM], w=[bEM])
            ckpt(13)
            V(lambda e: e.tensor_tensor(out=KKR[:], in0=KF[:], in1=P_KK.unsqueeze(2).to_broadcast([64, 8, 128]), op=ALU.mult),
              r=[bKF, bPAR], w=[bKKR])
            G(lambda e: e.tensor_tensor(out=SQ[:], in0=KKR[:], in1=KKR[:], op=ALU.mult), r=[bKKR], w=[bSQ])
            for bank in range(2):
                PE(mm(pbank[3 + bank][0:64, :], ONESF[0:64, 0:64], SQ[:, 4 * bank:4 * bank + 4, :], True, True), r=[bSQ, bONES], w=[PB[3 + bank]])
                A((lambda bank: lambda e: e.activation(out=RN[:, 4 * bank:4 * bank + 4, :],
                                                       in_=pbank[3 + bank][0:64, :].rearrange("p (h t) -> p h t", h=4),
                                                       func=AF.Sqrt))(bank), r=[PB[3 + bank]], w=[bRN])
            V(lambda e: e.tensor_scalar(out=RN[:], in0=RN[:], scalar1=1e-12, scalar2=None, op0=ALU.max), r=[bRN], w=[bRN])
            V(lambda e: e.reciprocal(out=RN[:], in_=RN[:]), r=[bRN], w=[bRN])
            V(lambda e: e.tensor_tensor(out=KK[:], in0=KKR[:], in1=RN[:], op=ALU.mult), r=[bKKR, bRN], w=[bKK])
            ckpt(14)
            V(lambda e: e.scalar_tensor_tensor(out=T1[:], in0=AS[:], scalar=-1.0, in1=P_KA.unsqueeze(2).to_broadcast([64, 8, 128]),
                                               op0=ALU.add, op1=ALU.mult), r=[bAS, bPAR], w=[bT1])
            V(lambda e: e.scalar_tensor_tensor(out=KP[:], in0=T1[:], scalar=1.0, in1=KF[:], op0=ALU.add, op1=ALU.mult),
              r=[bT1, bKF], w=[bKP])
            G(lambda e: e.tensor_tensor(out=BB[:], in0=KK[:], in1=AS[:], op=ALU.mult), r=[bKK, bAS], w=[bBB])
            G(lambda e: e.tensor_tensor(out=RKP[:], in0=RF[:], in1=KP[:], op=ALU.mult), r=[bRF, bKP], w=[bRKP])

            def v4(t):
                return t[:].rearrange("p (h c t) -> p h c t", h=8, c=2)

            def w4(t):
                return t[:].rearrange("p h (c t) -> p h c t", c=2)
            V(lambda e: e.tensor_tensor(out=KR[:, :, :, 0, :], in0=w4(KK), in1=v4(EPX), op=ALU.mult), r=[bKK, bEPX], w=[bKR])
            G(lambda e: e.tensor_tensor(out=KR[:, :, :, 1, :], in0=w4(RF), in1=v4(EP), op=ALU.mult), r=[bRF, bEP], w=[bKR])
            V(lambda e: e.tensor_tensor(out=BK[:, :, :, 0, :], in0=w4(BB), in1=v4(EM), op=ALU.mult), r=[bBB, bEM], w=[bBK])
            G(lambda e: e.tensor_tensor(out=BK[:, :, :, 1, :], in0=w4(KP), in1=v4(EM), op=ALU.mult), r=[bKP, bEM], w=[bBK])

            ckpt(16)
            for sp in range(4):
                sa = wload(win_s[:, 14 + sp, :], [bWIN[14 + sp]])
                bank = 3 + sp // 2
                for hl in range(2):
                    h = sp * 2 + hl
                    proj_fm(sa, None, hl * 64, 64, pview(bank, h % 4, 64), PB[bank])
            for bank in range(2):
                A((lambda bank: lambda e: e.activation(out=QT[:, 4 * bank:4 * bank + 4, :],
                                                       in_=pbank[3 + bank][0:64, :].rearrange("p (h t) -> p h t", h=4),
                                                       func=AF.Copy))(bank), r=[PB[3 + bank]], w=[bQT])
            ckpt(17)
            sa = wload(win_s[:, 18, :], [bWIN[18]])
            proj_fm(sa, None, 0, 64, pview(5, 0, 64), PB[5])
            proj_fm(sa, None, 64, 64, pview(5, 1, 64), PB[5])
            need_kout = any(c.get("k_out") is not None for c in chunks)
            for k, c in enumerate(chunks):
                sl = c["kslots"][2]
                V((lambda k, sl: lambda e: e.tensor_copy(
                    out=KTR[:, :, sl, :], in_=pbank[5][0:64, 0:256].rearrange("p (v c t) -> p v c t", v=2, c=2)[:, :, k, :]))(k, sl),
                  r=[PB[5]], w=[bKTR[sl]])
            if need_kout:
                A(lambda e: e.activation(out=KTF[:], in_=pbank[5][0:64, 0:256].rearrange("p (v t) -> p v t", v=2), func=AF.Copy),
                  r=[PB[5]], w=[bKTF])
            ckpt(18)
            sa = wload(win_s[:, 19, :], [bWIN[19]])
            for k, c in enumerate(chunks):
                sl = c["kslots"][2]
                o = pbank[6][0:64, k * 128:(k + 1) * 128]
                for kc in range(8):
                    PE(mm(o, HT[:, kc, k, 1:65], RING[:, sa, kc * 128:(kc + 1) * 128], kc == 0, kc == 7), r=[bRING[sa], bHT], w=[PB[6]])
                if c.get("v_out") is not None:
                    A((lambda k, o: lambda e: e.activation(out=VAF[:, k, :], in_=o, func=AF.Copy))(k, o), r=[PB[6]], w=[bVAF])
                    V((lambda sl, k: lambda e: e.tensor_copy(out=VAR[:, sl, :], in_=VAF[:, k, :]))(sl, k), r=[bVAF], w=[bVAR[sl]])
                    DMA((lambda k, c: lambda e: e.dma_start(out=c["v_out"], in_=VAF[:, k, :]))(k, c), r=[bVAF])
                else:
                    V((lambda sl, o: lambda e: e.tensor_copy(out=VAR[:, sl, :], in_=o))(sl, o), r=[PB[6]], w=[bVAR[sl]])
            ckpt(19)
            if need_kout:
                for kvh in range(2):
                    PE((lambda kvh: lambda e: e.transpose(out=pbank[7][:, kvh * 64:(kvh + 1) * 64], in_=KTF[:, kvh, :],
                                                          identity=IDF[0:64, 0:64]))(kvh), r=[bKTF, bCST], w=[PB[7]])
                V(lambda e: e.tensor_copy(out=KTOK[:], in_=pbank[7][:, 0:128]), r=[PB[7]], w=[bKTOK])
                for k, c in enumerate(chunks):
                    if c.get("k_out") is not None:
                        DMA((lambda k, c: lambda e: e.dma_start(out=c["k_out"], in_=KTOK[64 * k:64 * k + 64, :]))(k, c), r=[bKTOK])
            ckpt(20)
            for gq in range(4):
                bank = 3 + (gq % 2)
                for j in range(4):
                    sa = wload(win_s[:, 20 + gq * 4 + j, :], [bWIN[20 + gq * 4 + j]])
                    proj_fm(sa, None, 0, 128, pview(bank, j, 128), PB[bank])
                A((lambda gq, bank: lambda e: e.activation(out=SGT[:, 4 * gq:4 * gq + 4, :],
                                                           in_=pbank[bank][:, :].rearrange("p (h t) -> p h t", h=4),
                                                           func=AF.Sigmoid))(gq, bank), r=[PB[bank]], w=[bSGT])

            ckpt(21)
            def do_chunk(k, c):
                pr = 0
                gchunk[0] += 1
                cs = slice(64 * k, 64 * k + 64)
                def f_vtrans():
                    for h in range(8):
                        PE((lambda h: lambda e: e.transpose(out=pbank[3][0:64, h * 64:(h + 1) * 64], in_=VT[:, h, cs],
                                                            identity=IDF[0:64, 0:64]))(h), r=[bVT, bCST], w=[PB[3]])
                    A((lambda pr: lambda e: e.activation(out=VR[pr][:], in_=pbank[3][0:64, :].rearrange("p (h i) -> p h i", h=8),
                                                         func=AF.Copy))(pr), r=[PB[3]], w=[bVR[pr]])

                def f_bkt(w2):
                    bk = 4 - w2
                    for h in range(8):
                        PE((lambda h, w2: lambda e: e.transpose(out=pbank[bk][0:64, h * 64:(h + 1) * 64], in_=BK[:, h, k, w2, :],
                                                                identity=IDF[0:64, 0:64]))(h, w2), r=[bBK, bCST], w=[PB[bk]])
                    V((lambda pr, w2: lambda e: e.tensor_copy(out=BKT[pr][:, :, w2, :],
                                                              in_=pbank[bk][0:64, :].rearrange("p (h j) -> p h j", h=8)))(pr, w2),
                      r=[PB[bk]], w=[bBKT[pr]])

                def f_rkg():
                    for h in range(8):
                        PE(mm(pbank[7][0:64, h:h + 1], RKP[:, h, cs], P_RK[:, h:h + 1], True, True), r=[bRKP, bPAR], w=[PB[7]])
                    V((lambda pr: lambda e: e.tensor_copy(out=RKB[pr][:], in_=pbank[7][0:64, 0:8]))(pr), r=[PB[7]], w=[bRKB[pr]])
                    PE(mm(pbank[7][0:64, :], SG[:, cs], GL[:], True, True), r=[bSG, bLO], w=[PB[7]])
                    A(lambda e: e.activation(out=GG[:], in_=pbank[7][0:64, :], func=AF.Copy), r=[PB[7]], w=[bGG])

                ks = c["kslots"]
                valid = [i for i in range(3) if ks[i] is not None]
                al = ALIBI.rearrange("p (v k x) -> p v k x", v=2, k=3)

                def f_sc(kvh):
                    for kc in valid:
                        bank = 5 + (kc * 256) // 512
                        o = pbank[bank][0:64, (kc * 256) % 512:(kc * 256) % 512 + 256]
                        PE(mm(o, KTR[:, kvh, ks[kc], :], QT[:, 4 * kvh:4 * kvh + 4, cs], True, True),
                           r=[bKTR[ks[kc]], bQT], w=[PB[bank]])
                    for kc in valid:
                        bank = 5 + (kc * 256) // 512
                        o = pbank[bank][0:64, (kc * 256) % 512:(kc * 256) % 512 + 256]
                        A((lambda kc, o: lambda e: e.activation(out=EE[:, kc, :], in_=o, func=AF.Exp, scale=0.125))(kc, o),
                          r=[PB[bank]], w=[bEE])
                    k0, k1 = valid[0], valid[-1] + 1
                    V((lambda kvh, k0, k1: lambda e: e.tensor_tensor(out=PT[:, k0:k1, :], in0=EE[:, k0:k1, :],
                                                                     in1=al[:, kvh, k0:k1, :], op=ALU.mult))(kvh, k0, k1),
                      r=[bEE, bCST], w=[bPT])

                def f_pv(kvh):
                    for ii, kc in enumerate(valid):
                        PE(mm(pbank[7][0:64, 0:256], VAR[:, ks[kc], kvh * 64:(kvh + 1) * 64], PT[:, kc, :], ii == 0, ii == len(valid) - 1),
                           r=[bVAR[ks[kc]], bPT], w=[PB[7]])
                    for ii, kc in enumerate(valid):
                        PE(mm(pbank[7][0:64, 256:512], ONESB[:], PT[:, kc, :], ii == 0, ii == len(valid) - 1),
                           r=[bONES, bPT], w=[PB[7]])
                    V((lambda kvh: lambda e: e.tensor_tensor(
                        out=DEN[:].rearrange("p (g q) -> p g q", g=4), in0=pbank[7][0:64, 256:512].rearrange("p (g q) -> p g q", g=4),
                        in1=SNKE[:, 4 * kvh:4 * kvh + 4].unsqueeze(2).to_broadcast([64, 4, 64]), op=ALU.add))(kvh),
                      r=[PB[7], bSNK], w=[bDEN])
                    V(lambda e: e.reciprocal(out=DEN[:], in_=DEN[:]), r=[bDEN], w=[bDEN])
                    V((lambda kvh: lambda e: e.tensor_tensor(
                        out=OATT[:, 4 * kvh:4 * kvh + 4, cs], in0=pbank[7][0:64, 0:256].rearrange("p (g q) -> p g q", g=4),
                        in1=DEN[:].rearrange("p (g q) -> p g q", g=4), op=ALU.mult))(kvh), r=[PB[7], bDEN], w=[bOATT])

                fillers = [f_vtrans, lambda: f_bkt(0), lambda: f_bkt(1), f_rkg, lambda: f_sc(0), lambda: f_pv(0),
                           lambda: f_sc(1), lambda: f_pv(1)]

                def fill():
                    if fillers:
                        fillers.pop(0)()

                ckpt(30)
                for h in range(8):
                    bank = h // 4
                    PE(mm(pbank[bank][0:64, (h % 4) * 128:(h % 4 + 1) * 128], BK[:, h, k, 0, :], KR[:, h, k, :, :], True, True),
                       r=[bBK, bKR], w=[PB[bank]])
                for h in range(8):
                    bank = 2 + h // 4
                    PE(mm(pbank[bank][0:64, (h % 4) * 128:(h % 4 + 1) * 128], BK[:, h, k, 1, :], KR[:, h, k, :, :], True, True),
                       r=[bBK, bKR], w=[PB[bank]])
                for h in range(8):
                    PE(mm(pbank[4][0:64, h * 64:(h + 1) * 64], KR[:, h, k, 0, :], BK[:, h, k, 0, :], True, True),
                       r=[bBK, bKR], w=[PB[4]])
                m1v = M1.rearrange("p (w t) -> p w t", w=2)
                for bank in range(2):
                    pv4 = pbank[bank][0:64, :].rearrange("p (h w t) -> p h w t", h=4, w=2)
                    V((lambda pr, bank, pv4: lambda e: e.tensor_tensor(
                        out=NN[pr][:, 4 * bank:4 * bank + 4, :], in0=pv4[:, :, 0, :],
                        in1=m1v[:, 0:1, :].to_broadcast([64, 4, 64]), op=ALU.mult))(pr, bank, pv4), r=[PB[bank], bCST], w=[bNN[pr]])
                    V((lambda pr, bank, pv4: lambda e: e.tensor_tensor(
                        out=ARB[pr][:, 4 * bank:4 * bank + 4, :], in0=pv4[:, :, 1, :],
                        in1=m1v[:, 1:2, :].to_broadcast([64, 4, 64]), op=ALU.mult))(pr, bank, pv4), r=[PB[bank], bCST], w=[bARB[pr]])
                    pk4 = pbank[2 + bank][0:64, :].rearrange("p (h w t) -> p h w t", h=4, w=2)
                    V((lambda pr, bank, pk4: lambda e: e.tensor_tensor(
                        out=AKT[pr][:, 4 * bank:4 * bank + 4, :], in0=pk4[:, :, 0, :],
                        in1=m1v[:, 0:1, :].to_broadcast([64, 4, 64]), op=ALU.mult))(pr, bank, pk4), r=[PB[2 + bank], bCST], w=[bAKT[pr]])
                    V((lambda pr, bank, pk4: lambda e: e.tensor_tensor(
                        out=ARK[pr][:, 4 * bank:4 * bank + 4, :], in0=pk4[:, :, 1, :],
                        in1=m1v[:, 1:2, :].to_broadcast([64, 4, 64]), op=ALU.mult))(pr, bank, pk4), r=[PB[2 + bank], bCST], w=[bARK[pr]])
                V((lambda pr: lambda e: e.tensor_tensor(
                    out=QQ[pr][:], in0=pbank[4][0:64, :].rearrange("p (h s) -> p h s", h=8),
                    in1=ML.unsqueeze(1).to_broadcast([64, 8, 64]), op=ALU.mult))(pr), r=[PB[4], bCST], w=[bQQ[pr]])
                G((lambda pr: lambda e: e.tensor_tensor(
                    out=XX[pr][:], in0=IDF[0:64, 0:64].unsqueeze(1).to_broadcast([64, 8, 64]), in1=NN[pr][:], op=ALU.subtract))(pr),
                  r=[bCST, bNN[pr]], w=[bXX[pr]])
                Pc, Qc, bPc, bQc = NN[pr], QQ[pr], bNN[pr], bQQ[pr]
                Pn, Qn, bPn, bQn = NN2[pr], QQ2[pr], bNN2[pr], bQQ2[pr]
                for rd in range(1, 6):
                    last = rd == 5
                    if not last:
                        for h in range(8):
                            PE(mm(pbank[0][0:64, h * 64:(h + 1) * 64], Qc[:, h, :], Pc[:, h, :], True, True), r=[bPc, bQc], w=[PB[0]])
                    for h in range(8):
                        PE(mm(pbank[1][0:64, h * 64:(h + 1) * 64], Pc[:, h, :], Qc[:, h, :], True, True), r=[bPc, bQc], w=[PB[1]])
                    if not last:
                        A((lambda Pn: lambda e: e.activation(out=Pn[:], in_=pbank[0][0:64, :].rearrange("p (h s) -> p h s", h=8),
                                                             func=AF.Copy))(Pn), r=[PB[0]], w=[bPn])
                    V((lambda Qn: lambda e: e.tensor_copy(out=Qn[:], in_=pbank[1][0:64, :].rearrange("p (h s) -> p h s", h=8)))(Qn),
                      r=[PB[1]], w=[bQn])
                    fill()
                    for h in range(8):
                        PE(mm(pbank[2][0:64, h * 64:(h + 1) * 64], Qn[:, h, :], XX[pr][:, h, :], True, True), r=[bQn, bXX[pr]], w=[PB[2]])
                    V((lambda pr: lambda e: e.tensor_tensor(out=XX[pr][:], in0=XX[pr][:],
                                                            in1=pbank[2][0:64, :].rearrange("p (h s) -> p h s", h=8), op=ALU.add))(pr),
                      r=[PB[2], bXX[pr]], w=[bXX[pr]])
                    fill()
                    Pc, Qc, bPc, bQc, Pn, Qn, bPn, bQn = Pn, Qn, bPn, bQn, Pc, Qc, bPc, bQc

                while fillers:
                    fill()
                ckpt(31)
                stt = c["state"]
                if stt == "zero":
                    G(lambda e: e.memset(HH[:], 0.0), w=[bHH])
                elif stt != "keep":
                    b = stt[1]
                    DMA((lambda b: lambda e: e.dma_start(out=SIN[:], in_=st_wkv[b].rearrange("h i j -> i h j")))(b), w=[bSIN])
                    for h in range(8):
                        PE((lambda h: lambda e: e.transpose(out=pbank[3][0:64, h * 64:(h + 1) * 64], in_=SIN[:, h, :],
                                                            identity=IDF[0:64, 0:64]))(h), r=[bSIN, bCST], w=[PB[3]])
                    V(lambda e: e.tensor_copy(out=HH[:], in_=pbank[3][0:64, :].rearrange("p (h i) -> p h i", h=8)), r=[PB[3]], w=[bHH])
                for h in range(8):
                    o = pbank[3][0:64, h * 64:(h + 1) * 64]
                    PE(mm(o, KR[:, h, k, 0, :], HH[:, h, :], True, False), r=[bKR, bHH], w=[PB[3]])
                    PE(mm(o, AKT[pr][:, h, :], VR[pr][:, h, :], False, True), r=[bAKT[pr], bVR[pr]], w=[PB[3]])
                A(lambda e: e.activation(out=WN[:], in_=pbank[3][0:64, :].rearrange("p (h i) -> p h i", h=8), func=AF.Copy, scale=-1.0),
                  r=[PB[3]], w=[bWN])
                for h in range(8):
                    PE(mm(pbank[4][0:64, h * 64:(h + 1) * 64], XX[pr][:, h, :], WN[:, h, :], True, True), r=[bXX[pr], bWN], w=[PB[4]])
                V(lambda e: e.tensor_copy(out=UU[:], in_=pbank[4][0:64, :].rearrange("p (h i) -> p h i", h=8)), r=[PB[4]], w=[bUU])
                for h in range(8):
                    o = pbank[3][0:64, h * 64:(h + 1) * 64]
                    PE(mm(o, KR[:, h, k, 1, :], HH[:, h, :], True, False), r=[bKR, bHH], w=[PB[3]])
                    PE(mm(o, ARB[pr][:, h, :], UU[:, h, :], False, False), r=[bARB[pr], bUU], w=[PB[3]])
                    PE(mm(o, ARK[pr][:, h, :], VR[pr][:, h, :], False, True), r=[bARK[pr], bVR[pr]], w=[PB[3]])
                for h in range(8):
                    o = pbank[4][0:64, h * 64:(h + 1) * 64]
                    PE(mm(o, BKT[pr][:, h, 0, :], UU[:, h, :], True, False), r=[bBKT[pr], bUU], w=[PB[4]])
                    PE(mm(o, BKT[pr][:, h, 1, :], VR[pr][:, h, :], False, True), r=[bBKT[pr], bVR[pr]], w=[PB[4]])
                V(lambda e: e.tensor_tensor(out=HTMP[:], in0=pbank[4][0:64, :].rearrange("p (h i) -> p h i", h=8), in1=HH[:], op=ALU.add),
                  r=[PB[4], bHH], w=[bHTMP])
                epl = EP[:].rearrange("p (h c t) -> p h c t", h=8, c=2)[:, :, k, 63:64]
                V(lambda e: e.tensor_tensor(out=HH[:], in0=HTMP[:], in1=epl.to_broadcast([64, 8, 64]), op=ALU.mult),
                  r=[bHTMP, bEP], w=[bHH])
                if c.get("wkv_out") is not None:
                    for h in range(8):
                        PE((lambda h: lambda e: e.transpose(out=pbank[4][0:64, h * 64:(h + 1) * 64], in_=HH[:, h, :],
                                                            identity=IDF[0:64, 0:64]))(h), r=[bHH, bCST], w=[PB[4]])
                    V(lambda e: e.tensor_copy(out=SOUT[:], in_=pbank[4][0:64, :]), r=[PB[4]], w=[bSOUT])
                    DMA((lambda c: lambda e: e.dma_start(out=c["wkv_out"].rearrange("h i j -> i h j"),
                                                         in_=SOUT[:].rearrange("p (h j) -> p h j", h=8)))(c), r=[bSOUT])
                y3 = pbank[3][0:64, :].rearrange("p (h i) -> p h i", h=8)
                A(lambda e: e.activation(out=YS[:], in_=y3, func=AF.Copy), r=[PB[3]], w=[bYS])
                A(lambda e: e.activation(out=YQ[:], in_=y3, func=AF.Square), r=[PB[3]], w=[bYQ])
                V(lambda e: e.tensor_reduce(out=ST8[:, 0, :], in_=YS[:], axis=AX.X, op=ALU.add), r=[bYS], w=[bST8])
                V(lambda e: e.tensor_reduce(out=ST8[:, 1, :], in_=YQ[:], axis=AX.X, op=ALU.add), r=[bYQ], w=[bST8])
                V(lambda e: e.tensor_scalar(out=ST8[:, 0, :], in0=ST8[:, 0, :], scalar1=1.0 / 64, scalar2=None, op0=ALU.mult), r=[bST8], w=[bST8])
                V(lambda e: e.tensor_tensor(out=ST8[:, 2, :], in0=ST8[:, 0, :], in1=ST8[:, 0, :], op=ALU.mult), r=[bST8], w=[bST8])
                V(lambda e: e.scalar_tensor_tensor(out=ST8[:, 3, :], in0=ST8[:, 1, :], scalar=1.0 / 64, in1=ST8[:, 2, :],
                                                   op0=ALU.mult, op1=ALU.subtract), r=[bST8], w=[bST8])
                A(lambda e: e.activation(out=ST8[:, 4, :], in_=ST8[:, 3, :], func=AF.Sqrt, bias=EPS_T[0:64, 1:2]), r=[bST8, bEPS], w=[bST8])
                V(lambda e: e.reciprocal(out=ST8[:, 5, :], in_=ST8[:, 4, :]), r=[bST8], w=[bST8])
                V(lambda e: e.tensor_tensor(out=YS[:], in0=YS[:], in1=ST8[:, 0, :].unsqueeze(2).to_broadcast([64, 8, 64]), op=ALU.subtract),
                  r=[bYS, bST8], w=[bYS])
                V(lambda e: e.tensor_tensor(out=YS[:], in0=YS[:], in1=ST8[:, 5, :].unsqueeze(2).to_broadcast([64, 8, 64]), op=ALU.mult),
                  r=[bYS, bST8], w=[bYS])
                ysf = YS[:].rearrange("p h i -> p (h i)")
                G(lambda e: e.tensor_tensor(out=ysf, in0=ysf, in1=LNW[:], op=ALU.mult), r=[bYS, bLN], w=[bYS])
                G(lambda e: e.tensor_tensor(out=ysf, in0=ysf, in1=LNB[:], op=ALU.add), r=[bYS, bLN], w=[bYS])
                V((lambda pr: lambda e: e.tensor_tensor(out=YQ[:], in0=VR[pr][:], in1=RKB[pr][:].unsqueeze(2).to_broadcast([64, 8, 64]),
                                                        op=ALU.mult))(pr), r=[bVR[pr], bRKB[pr]], w=[bYQ])
                G(lambda e: e.tensor_tensor(out=YS[:], in0=YS[:], in1=YQ[:], op=ALU.add), r=[bYS, bYQ], w=[bYS])
                V(lambda e: e.tensor_tensor(out=ORW[:], in0=ysf, in1=GG[:], op=ALU.mult), r=[bYS, bGG], w=[bORW])
                ob = pbank[0][:].bitcast(BF16)
                for q4 in range(4):
                    PE((lambda q4: lambda e: e.transpose(out=ob[:, q4 * 64:(q4 + 1) * 64], in_=ORW[:, q4 * 128:(q4 + 1) * 128],
                                                         identity=IDB[0:64, 0:64]))(q4), r=[bORW, bIDB], w=[PB[0]])
                V(lambda e: e.tensor_copy(out=ORWT[:, :, cs], in_=ob[:, 0:256].rearrange("p (q t) -> p q t", q=4)), r=[PB[0]], w=[bORWT])

            for k_, c_ in enumerate(chunks):
                do_chunk(k_, c_)
                ckpt(32)

            for dp in range(4):
                sa = wload(wb0_s[:, dp, :], [bWB0])
                bank = dp // 2
                for dl in range(2):
                    dc = dp * 2 + dl
                    o = pbank[bank][:, (dc % 4) * 128:(dc % 4 + 1) * 128]
                    for kc in range(4):
                        PE(mm(o, RING[:, sa, kc * 256 + dl * 128:kc * 256 + dl * 128 + 128], ORWT[:, kc, :], kc == 0, kc == 3),
                           r=[bRING[sa], bORWT], w=[PB[bank]])
            for dc in range(8):
                sa = wload(wb1_s[:, dc, :], [bWB1], npart=64)
                bank = 2 + dc // 4
                o = pbank[bank][:, (dc % 4) * 128:(dc % 4 + 1) * 128]
                for qh in range(8):
                    PE(mm(o, RING[0:64, sa, qh * 128:(qh + 1) * 128], OATT[:, qh, :], qh == 0, qh == 7),
                       r=[bRING[sa], bOATT], w=[PB[bank]])
            for bank in range(2):
                V((lambda bank: lambda e: e.tensor_tensor(out=MT1[:, 4 * bank:4 * bank + 4, :],
                                                          in0=pbank[bank][:, :].rearrange("p (a t) -> p a t", a=4),
                                                          in1=SGT[:, 4 * bank:4 * bank + 4, :], op=ALU.mult))(bank),
                  r=[PB[bank], bSGT], w=[bMT1])
                V((lambda bank: lambda e: e.tensor_tensor(out=MT2[:, 4 * bank:4 * bank + 4, :],
                                                          in0=pbank[2 + bank][:, :].rearrange("p (a t) -> p a t", a=4),
                                                          in1=SGT[:, 8 + 4 * bank:8 + 4 * bank + 4, :], op=ALU.mult))(bank),
                  r=[PB[2 + bank], bSGT], w=[bMT2])
            V(lambda e: e.tensor_tensor(out=MRG[:], in0=MT1[:], in1=MT2[:], op=ALU.add), r=[bMT1, bMT2], w=[bMRG])
            for cs8 in range(8):
                sa = wload(wo_s[:, cs8, :], [bWO])
                bank = 4 + cs8 // 4
                o = pbank[bank][:, (cs8 % 4) * 128:(cs8 % 4 + 1) * 128]
                for kc in range(8):
                    PE(mm(o, MRG[:, kc, :], RING[:, sa, kc * 128:(kc + 1) * 128], kc == 0, kc == 7), r=[bRING[sa], bMRG], w=[PB[bank]])
            for bank in range(2):
                V((lambda bank: lambda e: e.tensor_tensor(out=X[:, 512 * bank:512 * bank + 512], in0=X[:, 512 * bank:512 * bank + 512],
                                                          in1=pbank[4 + bank][:, :], op=ALU.add))(bank), r=[PB[4 + bank], bX], w=[bX])

            ckpt(40)
            if do_peer:
                do_peer_tile()

            rmsnorm(X, bX, GFB, YO, bYO, 2)
            for k, c in enumerate(chunks):
                DMA((lambda k, c: lambda e: e.dma_start(out=c["yout"], in_=YO[64 * k:64 * k + 64, :]))(k, c), r=[bYO])

        if do_peer:
            bigA = [bRF, bKF, bVT, bLD, bCC, bCE, bCUM, bAS]
            bigB = [bKKR, bSQ, bT1, bBB]
            GWT = BIGA[:].bitcast(BF16).rearrange("p (i t) -> p i t", i=128)
            QREP = BIGB[64:128, :].bitcast(BF16).rearrange("p (t m) -> p t m", t=64)
            H2T = sb("H2T", [128, 8, 128], BF16); bH2T = Buf()
            QTP = sb("QTP", [128, 8, 128], BF16); bQTP = Buf()
            SC = sb("SC", [128, 256]); bSC = Buf()
            TOP1 = sb("TOP1", [128, 8, 16]); TOP2 = sb("TOP2", [128, 8, 16]); bTOP = Buf()
            IDXU = sb("IDXU", [128, 8, 16], U32); IDXF = sb("IDXF", [128, 8, 16]); bIDX = Buf()
            BEST = sb("BEST", [128, 8, 16]); bBEST = Buf()
            DD = sb("DD", [128, 8, 16]); bDD = Buf()
            ST = sb("ST", [128, 6, 8]); bST = Buf()
            C1T = VW(AKTf[:, 0:128].rearrange("p (h a) -> p h a", h=8)); C2T = VW(AKTf[:, 128:256].rearrange("p (h a) -> p h a", h=8))
            bCT = bAKT[0]
            CT3 = VW(ARKf[:, 0:384].rearrange("p (c t) -> p c t", c=3)); bCT3 = bARK[0]
            EXB = [VW(NNf[:].rearrange("p (j i) -> p j i", j=4)), VW(QQf[:].rearrange("p (j i) -> p j i", j=4))]
            bEXB = [bNN[0], bQQ[0]]
            RB = [VW(NN2f[:].bitcast(BF16)[:, 0:512].rearrange("p (j i) -> p j i", j=4)),
                  VW(QQ2f[:].bitcast(BF16)[:, 0:512].rearrange("p (j i) -> p j i", j=4))]
            bRB = [bNN2[0], bQQ2[0]]
            E1B = [VW(XXf[:].bitcast(BF16)[:, 0:512].rearrange("p (j i) -> p j i", j=4)),
                   VW(ARBf[:].bitcast(BF16)[:, 0:512].rearrange("p (j i) -> p j i", j=4))]
            bE1B = [bXX[0], bARB[0]]
            GB = [sb("GB%d" % i, [128, 128], BF16) for i in range(2)]; bGB_ = [Buf(), Buf()]
            GWB = [sb("GWB%d" % i, [128, 128], BF16) for i in range(2)]; bGWB = [Buf(), Buf()]
            bEXj = [[Buf() for _ in range(4)] for _ in range(2)]; bRj = [[Buf() for _ in range(4)] for _ in range(2)]
            bE1j = [[Buf() for _ in range(4)] for _ in range(2)]
            CAND = YO[:].rearrange("p (h x) -> p h x", h=4)
            CW = HF[:].rearrange("p (h x) -> p h x", h=4)

        def do_peer_tile():
            rmsnorm(X, bX, G2B, HF, bHF, 0)
            A(lambda e: e.activation(out=HB[:], in_=HF[:], func=AF.Copy), r=[bHF], w=[bHB])
            hb = pbank[7][:].bitcast(BF16)
            for kc in range(8):
                PE((lambda kc: lambda e: e.transpose(out=hb[:, kc * 128:(kc + 1) * 128], in_=HB[:, kc * 128:(kc + 1) * 128],
                                                     identity=IDB[:]))(kc), r=[bHB, bIDB], w=[PB[7]])
            A(lambda e: e.activation(out=H2T[:].rearrange("p a t -> p (a t)"), in_=hb, func=AF.Copy), r=[PB[7]], w=[bH2T])
            for h in range(8):
                sa = wload(wq_s[:, h, :], [bWQ])
                bank = 2 + h // 4
                o = pbank[bank][:, (h % 4) * 128:(h % 4 + 1) * 128]
                for kc in range(8):
                    PE(mm(o, RING[:, sa, kc * 128:(kc + 1) * 128], H2T[:, kc, :], kc == 0, kc == 7), r=[bRING[sa], bH2T], w=[PB[bank]])
            for bank in range(2):
                A((lambda bank: lambda e: e.activation(out=QTP[:, 4 * bank:4 * bank + 4, :],
                                                       in_=pbank[2 + bank][:, :].rearrange("p (h t) -> p h t", h=4),
                                                       func=AF.Copy))(bank), r=[PB[2 + bank]], w=[bQTP])
            for h in range(8):
                bank = 4 + h // 2
                o = pbank[bank][:, (h % 2) * 256:(h % 2) * 256 + 256]
                PE(mm(o, QTP[:, h, :], KEYB[:], True, True), r=[bQTP, bKEY], w=[PB[bank]])
            for h in range(8):
                bank = 4 + h // 2
                svs = [pbank[bank][:, (h % 2) * 256 + p * 128:(h % 2) * 256 + p * 128 + 128] for p in range(2)]
                TOPS = (TOP1, TOP2)
                for p in range(2):
                    V((lambda h, TOP, sv: lambda e: e.max(out=TOP[:, h, 0:8], in_=sv))(h, TOPS[p], svs[p]), r=[PB[bank]], w=[bTOP])
                for p in range(2):
                    V((lambda h, TOP, sv, p: lambda e: e.match_replace(out=SC[:, p * 128:(p + 1) * 128], in_to_replace=TOP[:, h, 0:8],
                                                                       in_values=sv, imm_value=-1e30))(h, TOPS[p], svs[p], p),
                      r=[PB[bank], bTOP], w=[bSC])
                V((lambda h, sv: lambda e: e.max_index(out=IDXU[:, h, 0:8], in_max=TOP1[:, h, 0:8], in_values=sv))(h, svs[0]),
                  r=[PB[bank], bTOP], w=[bIDX])
                for p in range(2):
                    V((lambda h, TOP, p: lambda e: e.max(out=TOP[:, h, 8:16], in_=SC[:, p * 128:(p + 1) * 128]))(h, TOPS[p], p),
                      r=[bSC], w=[bTOP])
                V((lambda h: lambda e: e.max_index(out=IDXU[:, h, 8:16], in_max=TOP1[:, h, 8:16], in_values=SC[:, 0:128]))(h),
                  r=[bSC, bTOP], w=[bIDX])
            V(lambda e: e.tensor_copy(out=IDXF[:], in_=IDXU[:]), r=[bIDX], w=[bIDX])
            for hh in range(2):
                V((lambda hh: lambda e: e.tensor_tensor(
                    out=CAND.rearrange("p h (a b) -> p h a b", a=16),
                    in0=TOP1[:, 4 * hh:4 * hh + 4, :].unsqueeze(3).to_broadcast([128, 4, 16, 16]),
                    in1=TOP2[:, 4 * hh:4 * hh + 4, :].unsqueeze(2).to_broadcast([128, 4, 16, 16]), op=ALU.add))(hh),
                  r=[bTOP], w=[bYO])
                for hl in range(4):
                    h = 4 * hh + hl
                    V((lambda h, hl: lambda e: e.max(out=BEST[:, h, 0:8], in_=CAND[:, hl, :]))(h, hl), r=[bYO], w=[bBEST])
                for hl in range(4):
                    h = 4 * hh + hl
                    V((lambda h, hl: lambda e: e.match_replace(out=CW[:, hl, :], in_to_replace=BEST[:, h, 0:8], in_values=CAND[:, hl, :],
                                                               imm_value=-1e30))(h, hl), r=[bYO, bBEST], w=[bHF])
                for hl in range(4):
                    h = 4 * hh + hl
                    V((lambda h, hl: lambda e: e.max(out=BEST[:, h, 8:16], in_=CW[:, hl, :]))(h, hl), r=[bHF], w=[bBEST])
            mB = BEST[:, :, 0:1]
            V(lambda e: e.tensor_tensor(out=DD[:], in0=BEST[:], in1=mB.to_broadcast([128, 8, 16]), op=ALU.subtract), r=[bBEST], w=[bDD])
            A(lambda e: e.activation(out=DD[:], in_=DD[:], func=AF.Exp), r=[bDD], w=[bDD])
            V(lambda e: e.tensor_reduce(out=ST[:, 0, :], in_=DD[:], axis=AX.X, op=ALU.add), r=[bDD], w=[bST])
            A(lambda e: e.activation(out=ST[:, 1, :], in_=ST[:, 0, :], func=AF.Ln), r=[bST], w=[bST])
            V(lambda e: e.tensor_tensor(out=ST[:, 2, :], in0=ST[:, 1, :], in1=BEST[:, :, 0], op=ALU.add), r=[bST, bBEST], w=[bST])
            V(lambda e: e.tensor_scalar(out=ST[:, 3, :], in0=BEST[:, :, 15], scalar1=-1e-5, scalar2=None, op0=ALU.add), r=[bBEST], w=[bST])
            V(lambda e: e.tensor_tensor(out=C1T[:], in0=TOP1[:], in1=ST[:, 2, :].unsqueeze(2).to_broadcast([128, 8, 16]), op=ALU.subtract),
              r=[bTOP, bST], w=[bCT])
            V(lambda e: e.tensor_tensor(out=ST[:, 4, :], in0=ST[:, 3, :], in1=ST[:, 2, :], op=ALU.subtract), r=[bST], w=[bST])
            A(lambda e: e.activation(out=ST[:, 5, :], in_=ST[:, 4, :], func=AF.Exp), r=[bST], w=[bST])
            V(lambda e: e.tensor_copy(out=C2T[:], in_=ST[:, 5, :].unsqueeze(2).to_broadcast([128, 8, 16])), r=[bST], w=[bCT])
            for q3, SRC in enumerate((C1T, C2T, IDXF)):
                PE((lambda q3, SRC: lambda e: e.transpose(out=pbank[3][:, q3 * 128:(q3 + 1) * 128],
                                                          in_=SRC[:].rearrange("p h a -> p (h a)"), identity=IDF))(q3, SRC),
                   r=[bCT, bIDX, bCST], w=[PB[3]])
            V(lambda e: e.tensor_copy(out=CT3[:].rearrange("p c t -> p (c t)"), in_=pbank[3][:, 0:384]), r=[PB[3]], w=[bCT3])
            def wb_bc(idx):
                half, g = divmod(idx, 16)
                pp = idx % 2
                bb = 4 + pp
                if g == 0:
                    A((lambda half: lambda e: e.activation(
                        out=QREP.rearrange("p t (h a) -> p t h a", h=8),
                        in_=QTP[64:128, :, 64 * half:64 * half + 64].rearrange("p h t -> p t h").unsqueeze(3).to_broadcast([64, 64, 8, 16]),
                        func=AF.Copy))(half), r=[bQTP] + bigB, w=bigB)
                for j in range(4):
                    tl = g * 4 + j
                    PE(mm(pbank[bb][:, j * 128:(j + 1) * 128], QREP[:, tl, :], KEYB[64:128, 128:256], True, True), r=bigB + [bKEY], w=[PB[bb]])

            def wb_ew(idx):
                half, g = divmod(idx, 16)
                pp = idx % 2
                bb = 4 + pp
                for j in range(4):
                    t = 64 * half + g * 4 + j
                    bcj = pbank[bb][:, j * 128:(j + 1) * 128]
                    A((lambda pp, j, t, bcj: lambda e: e.activation(out=EXB[pp][:, j, :], in_=bcj, func=AF.Exp,
                                                                    bias=CT3[:, 0, t:t + 1]))(pp, j, t, bcj),
                      r=[PB[bb], bCT3, bEXB[pp]], w=[bEXj[pp][j]])
                    V((lambda pp, j, t: lambda e: e.scalar_tensor_tensor(
                        out=RB[pp][:, j, :], in0=EXB[pp][:, j, :], scalar=CT3[:, 1, t:t + 1], in1=EXB[pp][:, j, :],
                        op0=ALU.is_ge, op1=ALU.mult))(pp, j, t), r=[bCT3, bEXj[pp][j], bRB[pp]], w=[bRj[pp][j]])
                    V((lambda pp, j, t: lambda e: e.tensor_scalar(out=E1B[pp][:, j, :], in0=IOTA, scalar1=CT3[:, 2, t:t + 1],
                                                                  scalar2=None, op0=ALU.is_equal))(pp, j, t),
                      r=[bCST, bCT3, bE1B[pp]], w=[bE1j[pp][j]])

            def wb_wt(idx):
                half, g = divmod(idx, 16)
                pp = idx % 2
                wb = 6 + pp
                for j in range(4):
                    PE(mm(pbank[wb][:, j * 128:(j + 1) * 128], RB[pp][:, j, :], E1B[pp][:, j, :], True, True),
                       r=[bRj[pp][j], bE1j[pp][j]], w=[PB[wb]])
                t0 = 64 * half + 4 * g
                A((lambda wb, t0: lambda e: e.activation(out=GWT[:, :, t0:t0 + 4],
                                                         in_=pbank[wb][:, :].rearrange("p (t i) -> p i t", t=4),
                                                         func=AF.Copy))(wb, t0), r=[PB[wb]], w=bigA)
            wb_bc(0)
            wb_bc(1)
            wb_ew(0)
            for idx in range(32):
                if idx + 1 < 32:
                    wb_ew(idx + 1)
                wb_wt(idx)
                if idx + 2 < 32:
                    wb_bc(idx + 2)

            slots = {}

            def de_at(ec):
                su = wload(ut_s[:, ec, :], [bUT[ec // 16]])
                sv_ = wload(pv_s[:, ec, :], [bPV[ec // 16]])
                slots[ec] = sv_
                ab = 2 + ec % 2
                for kc in range(8):
                    PE(mm(pbank[ab][:, 0:128], RING[:, su, kc * 128:(kc + 1) * 128], H2T[:, kc, :], kc == 0, kc == 7),
                       r=[bRING[su], bH2T], w=[PB[ab]])

            def de_ew(ec):
                pp = ec % 2
                ab = 2 + pp
                A((lambda pp, ab: lambda e: e.activation(out=GB[pp][:], in_=pbank[ab][:, 0:128], func=AF.Gelu))(pp, ab),
                  r=[PB[ab]], w=[bGB_[pp]])
                V((lambda pp, ec: lambda e: e.tensor_tensor(out=GWB[pp][:], in0=GB[pp][:], in1=GWT[:, ec, :], op=ALU.mult))(pp, ec),
                  r=[bGB_[pp]] + bigA, w=[bGWB[pp]])

            def de_mm2(ec):
                pp = ec % 2
                sv_ = slots.pop(ec)
                for hf in range(2):
                    PE(mm(pbank[hf][:, :], GWB[pp][:], RING[:, sv_, hf * 512:(hf + 1) * 512], ec == 0, ec == 127),
                       r=[bGWB[pp], bRING[sv_]], w=[PB[hf]])
            de_at(0)
            for ec in range(128):
                if ec + 1 < 128:
                    de_at(ec + 1)
                de_ew(ec)
                de_mm2(ec)
            for hf in range(2):
                V((lambda hf: lambda e: e.tensor_tensor(out=X[:, 512 * hf:512 * hf + 512], in0=X[:, 512 * hf:512 * hf + 512],
                                                        in1=pbank[hf][:, :], op=ALU.add))(hf), r=[PB[hf], bX], w=[bX])

        nchunk = [0]
        try:
          ckpt(0)
          for sq in DBG_SEQS:
            for t in range(NTP):
                chunks = []
                for k in range(2):
                    ci = 2 * t + k
                    g = nchunk[0]
                    nchunk[0] += 1
                    c = dict(xsrc=xp[sq, 64 * ci:64 * ci + 64, :], yout=y_p[sq, 64 * ci:64 * ci + 64, :])
                    c["prev"] = "zero" if ci == 0 else ("carry" if k == 0 else "own")
                    c["state"] = "zero" if ci == 0 else "keep"
                    c["kslots"] = [((g - 2) % 6) if ci >= 2 else None, ((g - 1) % 6) if ci >= 1 else None, g % 6]
                    if t == NTP - 1:
                        c["k_out"] = k_p[sq, 64 * k:64 * k + 64, :]
                        c["v_out"] = v_p[sq, 64 * k:64 * k + 64, :]
                        if k == 1:
                            c["shift_out"] = shift_p[sq:sq + 1, :]
                            c["wkv_out"] = wkv_p[sq]
                    chunks.append(c)
                do_tile(chunks)
          g0 = nchunk[0]
          slots = [(g0 + i) % 6 for i in range(6)]
          CKF = sb("CKF", [128, 128]); bCKF = Buf()
          CVF = VW(SOUT[:, 0:256].rearrange("p (c f) -> p c f", c=2)); bCVF = bSOUT
          chunks = []
          for k in range(2):
              sA, sB, sC = slots[3 * k], slots[3 * k + 1], slots[3 * k + 2]
              DMA((lambda k: lambda e: e.dma_start(out=CKF[:], in_=ck[k]))(k), w=[bCKF])
              for kvh in range(2):
                  PE((lambda kvh: lambda e: e.transpose(out=pbank[7][0:64, kvh * 128:(kvh + 1) * 128], in_=CKF[:, kvh * 64:(kvh + 1) * 64],
                                                        identity=IDF))(kvh), r=[bCKF, bCST], w=[PB[7]])
              for j, sl in enumerate((sA, sB)):
                  V((lambda j, sl: lambda e: e.tensor_copy(
                      out=KTR[:, :, sl, :], in_=pbank[7][0:64, 0:256].rearrange("p (v c t) -> p v c t", v=2, c=2)[:, :, j, :]))(j, sl),
                    r=[PB[7]], w=[bKTR[sl]])
              DMA((lambda k: lambda e: e.dma_start(out=CVF[:], in_=cv[k].rearrange("(c p) f -> p c f", p=64)))(k), w=[bCVF])
              for j, sl in enumerate((sA, sB)):
                  V((lambda j, sl: lambda e: e.tensor_copy(out=VAR[:, sl, :], in_=CVF[:, j, :]))(j, sl), r=[bCVF], w=[bVAR[sl]])
              c = dict(xsrc=xs[k], yout=y_s[k], prev=("state", k), state=("load", k), kslots=[sA, sB, sC],
                       k_out=k_s[k], v_out=v_s[k], shift_out=shift_s[k:k + 1, :], wkv_out=wkv_s[k])
              chunks.append(c)
          do_tile(chunks)
        except _Stop:
            pass
        S.emit()
    return nc, S


def host_layout(inp, do_peer=True):
    f = lambda a: np.ascontiguousarray(a, dtype=np.float32)
    w_in = inp["w_in"][0]
    sh = {}
    sh["win_f"] = f(w_in.reshape(8, 128, 36, 128).transpose(1, 2, 0, 3).reshape(128, 36, 1024))
    mu = inp["mu_shift"][0]
    sh["mu_row"] = f(mu)
    par = np.zeros((128, NPAR), np.float32)
    hj = lambda v: np.asarray(v).reshape(8, 64).T
    par[0:64, 27:35] = hj(inp["k_k"][0]); par[0:64, 35:43] = hj(inp["k_a"][0])
    par[0:64, 59:67] = hj(inp["r_k"][0].reshape(512))
    sh["par"] = par
    sh["cst"] = make_consts()
    sh["n1g"] = f(inp["norm1_g"][0]); sh["n2g"] = f(inp["norm2_g"][0]); sh["fg"] = f(inp["final_g"])
    sh["w0row"] = f(inp["w_decay0"][0].reshape(1, 512)); sh["a0row"] = f(inp["a_icl0"][0].reshape(1, 512))
    sh["lw"] = f(inp["w_decay_lora"][0]); sh["la"] = f(inp["a_icl_lora"][0]); sh["gl"] = f(inp["g_lora"][0])
    sh["lnw"] = f(inp["lnx_w"][0]); sh["lnb"] = f(inp["lnx_b"][0]); sh["sinks"] = f(inp["attn_sinks"][0])
    wb = inp["w_branch"][0]
    sh["wb0_f"] = f(wb[0].reshape(4, 128, 4, 2, 128).transpose(1, 2, 0, 3, 4).reshape(128, 4, 1024))
    sh["wb1_f"] = f(wb[1].reshape(8, 64, 8, 128).transpose(1, 2, 0, 3).reshape(64, 8, 1024))
    sh["wo_f"] = f(inp["w_out"][0].reshape(8, 128, 8, 128).transpose(1, 2, 0, 3).reshape(128, 8, 1024))
    if do_peer:
        sh["wq_f"] = f(inp["peer_wq"][0].reshape(8, 128, 8, 128).transpose(1, 2, 0, 3).reshape(128, 8, 1024))
        sk = inp["peer_sub_keys"][0]
        keys = np.zeros((128, 256), np.float32)
        keys[0:64, 0:128] = sk[0].T
        keys[64:128, 128:256] = sk[1].T
        sh["keys_f"] = keys
        sh["ut_f"] = f(inp["peer_u"][0].reshape(128, 128, 8, 128).transpose(3, 0, 2, 1).reshape(128, 128, 1024))
        sh["pv_f"] = f(inp["peer_v"][0].reshape(128, 128, 1024).transpose(1, 0, 2))
    return sh


def core_inputs(inp, c, sh):
    f = lambda a: np.ascontiguousarray(a, dtype=np.float32)
    m = dict(sh)
    m["xp"] = f(inp["x_prompt"][2 * c:2 * c + 2]); m["xs"] = f(inp["x_sample"][2 * c:2 * c + 2])
    m["st_shift"] = f(inp["state_shift"][0, 2 * c:2 * c + 2]); m["st_wkv"] = f(inp["state_wkv"][0, 2 * c:2 * c + 2])
    m["ck"] = f(inp["cache_k"][0, 2 * c:2 * c + 2].reshape(2, 128, 128)); m["cv"] = f(inp["cache_v"][0, 2 * c:2 * c + 2].reshape(2, 128, 128))
    return m


DO_PEER = True
_CACHE = {}


def kernel(**inp):
    B, SEQ, _ = inp["x_prompt"].shape
    NTP = SEQ // 128
    key = (NTP, DO_PEER)
    if key not in _CACHE:
        _CACHE[key] = build(NTP, DO_PEER)[0]
    nc = _CACHE[key]
    sh = host_layout(inp, DO_PEER)
    in_maps = [core_inputs(inp, c, sh) for c in range(8)]
    res = run_bass_kernel_spmd(nc, in_maps, core_ids=list(range(8)))
    R = res.results
    cat = lambda k: np.concatenate([np.asarray(r[k]) for r in R], axis=0)
    y_p = cat("y_p"); y_s = cat("y_s")
    return (y_p, y_s, cat("shift_p")[None], cat("wkv_p")[None],
            cat("k_p").reshape(1, 16, 128, 2, 64), cat("v_p").reshape(1, 16, 128, 2, 64),
            cat("shift_s")[None], cat("wkv_s")[None],
            cat("k_s").reshape(1, 16, 64, 2, 64), cat("v_s").reshape(1, 16, 64, 2, 64))
```

```python
import contextlib
import math
import numpy as np
import concourse.bass as bass
import concourse.mybir as mybir
from concourse.bass_utils import run_bass_kernel_spmd

F32 = mybir.dt.float32
BF16 = mybir.dt.bfloat16
U32 = mybir.dt.uint32
AF = mybir.ActivationFunctionType
ALU = mybir.AluOpType
AX = mybir.AxisListType

COMPUTE = ("tensor", "vector", "scalar", "gpsimd")
D = 1024
RW_COLS = 1792
EXPM05 = math.exp(-0.5)


class P64:
    def __init__(self, t):
        self.t = t

    def __getitem__(self, idx):
        if not isinstance(idx, tuple):
            idx = (idx,)
        assert idx[0] == slice(None)
        return self.t[(slice(0, 64),) + tuple(idx[1:])]


class P3:
    def __init__(self, t):
        self.t = t

    def __getitem__(self, idx):
        v = self.t[:].rearrange("p (a t) -> p a t", a=8)
        return v[idx]


class VW:
    def __init__(self, ap):
        self.ap = ap

    def __getitem__(self, idx):
        return self.ap[idx]


class Buf:
    __slots__ = ("name", "w", "r", "excl")

    def __init__(self, name="", excl=False):
        self.name = name
        self.w = None
        self.r = []
        self.excl = excl


class Op:
    __slots__ = ("eng", "fn", "deps", "isdma", "sem", "semval", "needinc")

    def __init__(self, eng, fn, isdma):
        self.eng = eng
        self.fn = fn
        self.isdma = isdma
        self.deps = set()
        self.needinc = False
        self.sem = None
        self.semval = None


class Sched:
    NDMASEM = 16

    def __init__(self, nc):
        self.nc = nc
        self.ops = []

    def op(self, eng, fn, reads=(), writes=(), dma=False):
        idx = len(self.ops)
        o = Op(eng, fn, dma)
        reads = list(reads)
        writes = list(writes)
        for b in list(reads):
            if b.excl:
                reads.remove(b)
                if b not in writes:
                    writes.append(b)
        for b in reads:
            if b.w is not None:
                o.deps.add(b.w)
        for b in writes:
            if b.w is not None:
                o.deps.add(b.w)
            for r in b.r:
                o.deps.add(r)
        for b in reads:
            b.r.append(idx)
        for b in writes:
            b.w = idx
            b.r = []
        o.deps.discard(idx)
        self.ops.append(o)
        return idx

    def emit(self):
        nc = self.nc
        ops = self.ops
        engs = []
        for o in ops:
            if o.eng not in engs:
                engs.append(o.eng)
        for o in ops:
            for d in list(o.deps):
                po = ops[d]
                if (not po.isdma) and (not o.isdma) and po.eng == "tensor" and o.eng == "tensor":
                    o.deps.discard(d)
        with contextlib.ExitStack() as st:
            csem = {e: st.enter_context(nc.semaphore("cs_" + e)) for e in COMPUTE}
            dengs = set(o.eng for o in ops if o.isdma)
            dsems = {e: ([st.enter_context(nc.semaphore("ds_%s_%d" % (e, k))) for k in range(self.NDMASEM)] if e in dengs else [])
                     for e in engs}
            dcount = {e: 0 for e in engs}
            dhist = {e: [[] for _ in range(self.NDMASEM)] for e in engs}
            for i, o in enumerate(ops):
                if o.isdma:
                    k = dcount[o.eng] % self.NDMASEM
                    dcount[o.eng] += 1
                    o.sem = dsems[o.eng][k]
                    h = dhist[o.eng][k]
                    if h:
                        o.deps.add(h[-1])
                    h.append(i)
                    o.semval = 16 * len(h)
            for o in ops:
                for d in o.deps:
                    ops[d].needinc = True
            ccount = {e: 0 for e in COMPUTE}
            for o in ops:
                if (not o.isdma) and o.needinc:
                    ccount[o.eng] += 1
                    o.sem = csem[o.eng]
                    o.semval = ccount[o.eng]
            per_eng = {e: [] for e in engs}
            for i, o in enumerate(ops):
                per_eng[o.eng].append(i)
            self.stats = {e: len(per_eng[e]) for e in engs}
            block = st.enter_context(nc.Block())

            def make(e):
                def body(engine):
                    seen = {}
                    for i in per_eng[e]:
                        o = ops[i]
                        need = {}
                        for d in o.deps:
                            po = ops[d]
                            if po.sem is None:
                                continue
                            key = id(po.sem)
                            if key not in need or need[key][1] < po.semval:
                                need[key] = (po.sem, po.semval)
                        for key, (sem, val) in need.items():
                            if seen.get(key, 0) >= val:
                                continue
                            engine.wait_ge(sem, val)
                            seen[key] = val
                        ins = o.fn(engine)
                        if o.isdma:
                            ins.then_inc(o.sem, 16)
                        elif o.needinc:
                            ins.then_inc(o.sem, 1)
                    for k in range(len(dsems[e])):
                        h = dhist[e][k]
                        if h and seen.get(id(dsems[e][k]), 0) < 16 * len(h):
                            engine.wait_ge(dsems[e][k], 16 * len(h))
                return body

            for e in engs:
                getattr(block, e)(make(e))


NCST = 448 + 1536


def make_consts():
    c = np.zeros((128, NCST), np.float32)
    c[:, 0:128] = np.eye(128, dtype=np.float32)
    c[:, 128:256] = np.arange(128, dtype=np.float32)[None, :]
    s = np.arange(64)[:, None]
    t = np.arange(64)[None, :]
    c[0:64, 256:320] = (s < t)
    c[0:64, 320:384] = (s <= t)
    c[0:64, 384:448] = (t < s)
    slopes = np.exp2(-8.0 * np.arange(1, 9, dtype=np.float64) / 8.0)
    al = np.zeros((64, 2, 3, 4, 64), np.float64)
    tk = np.arange(64)[:, None]
    tq = np.arange(64)[None, :]
    for kvh in range(2):
        for kc in range(3):
            for g in range(4):
                dist = np.abs(128 - 64 * kc + tq - tk)
                al[:, kvh, kc, g, :] = np.exp(-slopes[kvh * 4 + g] * dist)
    c[0:64, 448:448 + 1536] = al.reshape(64, 1536)
    return c


NPAR = 67
DBG_STOP = None
DBG_SEQS = (0, 1)


class _Stop(Exception):
    pass


_HITS = {}


def ckpt(n):
    if DBG_STOP is None:
        return
    _HITS[n] = _HITS.get(n, 0) + 1
    if DBG_STOP % 100 == n and _HITS[n] - 1 == DBG_STOP // 100:
        raise _Stop()


def build(NTP, do_peer=True):
    SEQ = NTP * 128
    nc = bass.Bass("TRN2", target_bir_lowering=False)
    S = Sched(nc)

    def din(name, shape, dt=F32):
        return nc.dram_tensor(name, list(shape), dt, kind="ExternalInput").ap()

    def dout(name, shape):
        return nc.dram_tensor(name, list(shape), F32, kind="ExternalOutput").ap()

    def dscr(name, shape, dt=BF16):
        return nc.dram_tensor(name, list(shape), dt, kind="Internal").ap()

    xp = din("xp", [2, SEQ, D]); xs = din("xs", [2, 64, D])
    st_shift = din("st_shift", [2, D]); st_wkv = din("st_wkv", [2, 8, 64, 64])
    ck = din("ck", [2, 128, 128]); cv = din("cv", [2, 128, 128])
    cst = din("cst", [128, NCST]); par = din("par", [128, NPAR])
    n1g = din("n1g", [D]); n2g = din("n2g", [D]); fg = din("fg", [D])
    mu_row = din("mu_row", [14 * 128])
    win_f = din("win_f", [128, 36, 1024])
    w0row = din("w0row", [1, 512]); a0row = din("a0row", [1, 512])
    lw = din("lw", [64, 512]); la = din("la", [64, 512]); gl = din("gl", [128, 512])
    lnw = din("lnw", [512]); lnb = din("lnb", [512]); sinks = din("sinks", [8])
    wb0_f = din("wb0_f", [128, 4, 1024]); wb1_f = din("wb1_f", [64, 8, 1024])
    wo_f = din("wo_f", [128, 8, 1024])
    if do_peer:
        wq_f = din("wq_f", [128, 8, 1024]); keys_f = din("keys_f", [128, 256])
        ut_f = din("ut_f", [128, 128, 1024]); pv_f = din("pv_f", [128, 128, 1024])

    y_p = dout("y_p", [2, SEQ, D]); y_s = dout("y_s", [2, 64, D])
    shift_p = dout("shift_p", [2, D]); wkv_p = dout("wkv_p", [2, 8, 64, 64])
    k_p = dout("k_p", [2, 128, 128]); v_p = dout("v_p", [2, 128, 128])
    shift_s = dout("shift_s", [2, D]); wkv_s = dout("wkv_s", [2, 8, 64, 64])
    k_s = dout("k_s", [2, 64, 128]); v_s = dout("v_s", [2, 64, 128])

    win_s = dscr("win_s", [128, 50, 1024])
    wb0_s = dscr("wb0_s", [128, 4, 1024]); wb1_s = dscr("wb1_s", [64, 8, 1024]); wo_s = dscr("wo_s", [128, 8, 1024])
    if do_peer:
        wq_s = dscr("wq_s", [128, 8, 1024]); ut_s = dscr("ut_s", [128, 128, 1024]); pv_s = dscr("pv_s", [128, 128, 1024])

    with contextlib.ExitStack() as es:
        def sb(name, shape, dt=F32):
            return es.enter_context(nc.sbuf_tensor(name, list(shape), dt))

        pbank = [es.enter_context(nc.psum_tensor("pb%d" % k, [128, 512], F32)) for k in range(8)]
        PB = [Buf("pb%d" % k, excl=True) for k in range(8)]

        def V(fn, r=(), w=()): S.op("vector", fn, r, w)
        def A(fn, r=(), w=()): S.op("scalar", fn, r, w)
        def G(fn, r=(), w=()): S.op("gpsimd", fn, r, w)
        def PE(fn, r=(), w=()): S.op("tensor", fn, r, w)
        def DMA(fn, r=(), w=(), q="sync"): S.op(q, fn, r, w, dma=True)

        CST = sb("CST", [128, NCST]); bCST = Buf()
        PAR = sb("PAR", [128, NPAR]); bPAR = Buf()
        DMA(lambda e: e.dma_start(out=CST[:], in_=cst), w=[bCST])
        DMA(lambda e: e.dma_start(out=PAR[:], in_=par), w=[bPAR])
        IDF = CST[:, 0:128]
        IOTA = CST[:, 128:256]
        M1 = CST[0:64, 256:384]
        ML = CST[0:64, 384:448]
        ALIBI = CST[0:64, 448:448 + 1536]
        IDB = sb("IDB", [128, 128], BF16); bIDB = Buf()
        V(lambda e: e.tensor_copy(out=IDB[:], in_=IDF), r=[bCST], w=[bIDB])
        ONESF = sb("ONESF", [128, 128]); bONES = Buf()
        ONESB = sb("ONESB", [64, 64], BF16)
        V(lambda e: e.memset(ONESF[:], 1.0), w=[bONES])
        V(lambda e: e.memset(ONESB[:], 1.0), w=[bONES])
        G1B = sb("G1B", [128, D]); G2B = sb("G2B", [128, D]); GFB = sb("GFB", [128, D]); bGB = Buf()
        DMA(lambda e: e.dma_start(out=G1B[:], in_=n1g.partition_broadcast(128)), w=[bGB])
        DMA(lambda e: e.dma_start(out=G2B[:], in_=n2g.partition_broadcast(128)), w=[bGB])
        DMA(lambda e: e.dma_start(out=GFB[:], in_=fg.partition_broadcast(128)), w=[bGB])
        LNW = sb("LNW", [64, 512]); LNB = sb("LNB", [64, 512]); SNK = sb("SNK", [64, 8]); bLN = Buf()
        DMA(lambda e: e.dma_start(out=LNW[:], in_=lnw.partition_broadcast(64)), w=[bLN])
        DMA(lambda e: e.dma_start(out=LNB[:], in_=lnb.partition_broadcast(64)), w=[bLN])
        DMA(lambda e: e.dma_start(out=SNK[:], in_=sinks.partition_broadcast(64)), w=[bLN])
        SNKE = sb("SNKE", [64, 8]); bSNK = Buf()
        A(lambda e: e.activation(out=SNKE[:], in_=SNK[:], func=AF.Exp), r=[bLN], w=[bSNK])
        LWX = sb("LWX", [128, 1024]); GL = sb("GL", [128, 512]); bLO = Buf()
        DMA(lambda e: e.dma_start(out=LWX[64:65, 0:512], in_=w0row), w=[bLO])
        DMA(lambda e: e.dma_start(out=LWX[64:65, 512:1024], in_=a0row), w=[bLO])
        DMA(lambda e: e.dma_start(out=LWX[0:64, 0:512], in_=lw), w=[bLO])
        DMA(lambda e: e.dma_start(out=LWX[0:64, 512:1024], in_=la), w=[bLO])
        DMA(lambda e: e.dma_start(out=GL[:], in_=gl), w=[bLO])
        P_KK = PAR[0:64, 27:35]; P_KA = PAR[0:64, 35:43]; P_RK = PAR[0:64, 59:67]

        bWIN = [Buf() for _ in range(50)]
        X = sb("X", [128, D]); bX = Buf()
        HB = sb("HB", [128, D], BF16); bHB = Buf()
        JNK = HB; bJNK = bHB
        MRG = sb("MRG", [128, 8, 128], BF16); bMRG = Buf()
        MRGF = VW(MRG[:].rearrange("p a t -> p (a t)"))
        KRf = sb("KRf", [128, 8, 2, 2, 64]); BKf = sb("BKf", [128, 8, 2, 2, 64]); bKR = Buf(); bBK = Buf()
        KR = P64(KRf); BK = P64(BKf)
        MUB = KRf[:].rearrange("p a b c d -> p (a b c d)"); OMUB = BKf[:].rearrange("p a b c d -> p (a b c d)")
        bMU = Buf()
        DMA(lambda e: e.dma_start(out=MUB[:, 0:1792], in_=mu_row.partition_broadcast(128)), w=[bMU, bKR, bBK])
        V(lambda e: e.tensor_scalar(out=OMUB[:, 0:1792], in0=MUB[:, 0:1792], scalar1=-1.0, scalar2=1.0, op0=ALU.mult, op1=ALU.add),
          r=[bMU], w=[bMU])
        STGl = [X]; bSTG = [bX]
        STOl = [HB, MRGF]; bSTO = [bHB, bMRG]
        nst = [0]
        for s in range(14):
            k = 0
            DMA((lambda s, k: lambda e: e.dma_start(out=STGl[k][:], in_=win_f[:, s, :]))(s, k), w=[bSTG[k]])
            for half, (MM, dst) in enumerate(((OMUB, s), (MUB, 36 + s))):
                o = nst[0] % 2
                nst[0] += 1
                eng = V if half == 0 else G
                eng((lambda s, k, o, MM: lambda e: e.tensor_tensor(
                    out=STOl[o][:].rearrange("p (a m) -> p a m", a=8),
                    in0=STGl[k][:].rearrange("p (a m) -> p a m", a=8),
                    in1=MM[:, s * 128:(s + 1) * 128].unsqueeze(1).to_broadcast([128, 8, 128]),
                    op=ALU.mult))(s, k, o, MM), r=[bSTG[k], bMU, bKR, bBK], w=[bSTO[o]])
                DMA((lambda o, dst: lambda e: e.dma_start(out=win_s[:, dst, :], in_=STOl[o][:]))(o, dst),
                    r=[bSTO[o]], w=[bWIN[dst]])
        for s0 in range(14, 36, 11):
            DMA((lambda s0: lambda e: e.dma_start(out=win_s[:, s0:s0 + 11, :], in_=win_f[:, s0:s0 + 11, :]))(s0),
                w=[bWIN[s] for s in range(s0, s0 + 11)], q="gpsimd")
        bWB0 = Buf(); bWB1 = Buf(); bWO = Buf()
        DMA(lambda e: e.dma_start(out=wb0_s, in_=wb0_f), w=[bWB0], q="gpsimd")
        DMA(lambda e: e.dma_start(out=wb1_s, in_=wb1_f), w=[bWB1], q="gpsimd")
        DMA(lambda e: e.dma_start(out=wo_s, in_=wo_f), w=[bWO], q="gpsimd")
        if do_peer:
            bWQ = Buf()
            DMA(lambda e: e.dma_start(out=wq_s, in_=wq_f), w=[bWQ], q="gpsimd")
            bUT = [Buf() for _ in range(8)]; bPV = [Buf() for _ in range(8)]
            for k in range(8):
                DMA((lambda k: lambda e: e.dma_start(out=ut_s[:, 16 * k:16 * k + 16, :], in_=ut_f[:, 16 * k:16 * k + 16, :]))(k),
                    w=[bUT[k]], q="gpsimd")
                DMA((lambda k: lambda e: e.dma_start(out=pv_s[:, 16 * k:16 * k + 16, :], in_=pv_f[:, 16 * k:16 * k + 16, :]))(k),
                    w=[bPV[k]], q="gpsimd")
            KEYF = sb("KEYF", [128, 256]); KEYB = sb("KEYB", [128, 256], BF16); bKEY = Buf()
            DMA(lambda e: e.dma_start(out=KEYF[:], in_=keys_f), w=[bKEY])
            V(lambda e: e.tensor_copy(out=KEYB[:], in_=KEYF[:]), r=[bKEY], w=[bKEY])

        stopped = [False]
        NS = 8
        RING = sb("RING", [128, NS, 1024], BF16)
        bRING = [Buf() for _ in range(NS)]
        rcount = [0]

        def wload(src_ap, deps, npart=128):
            k = rcount[0] % NS
            rcount[0] += 1
            DMA(lambda e: e.dma_start(out=RING[0:npart, k, :], in_=src_ap), r=list(deps), w=[bRING[k]])
            return k

        SSQ = sb("SSQ", [128, 4]); bSSQ = Buf()
        HF = sb("HF", [128, D]); bHF = Buf()
        HT = sb("HT", [128, 8, 2, 65], BF16); bHT = Buf()
        CAR = sb("CAR", [128, 8, 1], BF16); bCAR = Buf()
        STS = sb("STS", [128, 2, 8]); bSTS = Buf()
        BIGA = sb("BIGA", [128, 8192]); BIGB = sb("BIGB", [128, 4096])
        def carveA(i, three):
            a = BIGA[0:64, i * 1024:(i + 1) * 1024]
            return VW(a.rearrange("p (h t) -> p h t", h=8) if three else a)
        def carveB(i):
            return VW(BIGB[0:64, i * 1024:(i + 1) * 1024].rearrange("p (h t) -> p h t", h=8))
        RF = carveA(0, True); KF = carveA(1, True); VT = carveA(2, True)
        bRF = Buf(); bKF = Buf(); bVT = Buf()
        TW = sb("TW", [64, 128]); XA = sb("XA", [64, 128]); SG = sb("SG", [128, 128]); bTW = Buf(); bXA = Buf(); bSG = Buf()
        LD = carveA(3, False); CC = carveA(4, False); CE = carveA(5, False); CUM = carveA(6, False)
        CUMX = CE; EP = CC; EPX = CE; EM = CUM
        bLD = Buf(); bCC = Buf(); bCE = Buf(); bCUM = Buf(); bCUMX = bCE; bEP = bCC; bEPX = bCE; bEM = bCUM
        AS = carveA(7, True); bAS = Buf()
        KKR = carveB(0); SQ = carveB(1); RN = SQ; KK = KKR
        bKKR = Buf(); bSQ = Buf(); bRN = bSQ; bKK = bKKR
        T1 = carveB(2); KP = T1; BB = carveB(3); RKP = sb("RKP", [64, 8, 128])
        bT1 = Buf(); bKP = bT1; bBB = Buf(); bRKP = Buf()
        QT = sb("QT", [64, 8, 128], BF16); bQT = Buf()
        KTR = sb("KTR", [64, 2, 6, 64], BF16); bKTR = [Buf() for _ in range(6)]
        VAR = sb("VAR", [64, 6, 128], BF16); bVAR = [Buf() for _ in range(6)]
        KTF = sb("KTF", [64, 2, 128]); bKTF = Buf()
        VAF = sb("VAF", [64, 2, 128]); bVAF = Buf()
        KTOK = sb("KTOK", [128, 128]); bKTOK = Buf()
        SGT = sb("SGT", [128, 16, 128], BF16); bSGT = Buf()
        def scanbuf(name):
            f = sb(name, [128, 512])
            return f, [VW(f[0:64, :].rearrange("p (h s) -> p h s", h=8))]
        NNf, NN = scanbuf("NNf"); QQf, QQ = scanbuf("QQf"); NN2f, NN2 = scanbuf("NN2f"); QQ2f, QQ2 = scanbuf("QQ2f")
        XXf, XX = scanbuf("XXf"); ARBf, ARB = scanbuf("ARBf"); AKTf, AKT = scanbuf("AKTf"); ARKf, ARK = scanbuf("ARKf")
        bNN = [Buf(), Buf()]; bQQ = [Buf(), Buf()]; bNN2 = [Buf(), Buf()]; bQQ2 = [Buf(), Buf()]; bXX = [Buf(), Buf()]
        bARB = [Buf(), Buf()]; bAKT = [Buf(), Buf()]; bARK = [Buf(), Buf()]
        VR = [sb("VR%d" % i, [64, 8, 64]) for i in range(1)]; bVR = [Buf(), Buf()]
        BKT = [sb("BKT%d" % i, [64, 8, 2, 64]) for i in range(1)]; bBKT = [Buf(), Buf()]
        RKB = [sb("RKB%d" % i, [64, 8]) for i in range(1)]; bRKB = [Buf(), Buf()]
        HH = sb("HH", [64, 8, 64]); bHH = Buf()
        WN = sb("WN", [64, 8, 64]); UU = sb("UU", [64, 8, 64]); bWN = Buf(); bUU = Buf()
        HTMP = sb("HTMP", [64, 8, 64]); bHTMP = Buf()
        SIN = sb("SIN", [64, 8, 64]); bSIN = Buf()
        YS = sb("YS", [64, 8, 64]); YQ = sb("YQ", [64, 8, 64]); bYS = Buf(); bYQ = Buf()
        ST8 = sb("ST8", [64, 6, 8]); bST8 = Buf()
        GG = sb("GG", [64, 512]); bGG = Buf()
        ORW = sb("ORW", [64, 512], BF16); bORW = Buf()
        ORWT = sb("ORWT", [128, 4, 128], BF16); bORWT = Buf()
        OATT = sb("OATT", [64, 8, 128], BF16); bOATT = Buf()
        EE = sb("EE", [64, 3, 256]); bEE = Buf()
        PT = sb("PT", [64, 3, 256], BF16); bPT = Buf()
        DEN = sb("DEN", [64, 256]); bDEN = Buf()
        YO = sb("YO", [128, D]); bYO = Buf()
        MT1 = P3(HF); MT2 = P3(YO)
        bMT1 = bHF; bMT2 = bYO
        SOUT = sb("SOUT", [64, 512]); bSOUT = Buf()

        def rmsnorm(src, bsrc, gB, dstf, bdstf, col):
            A(lambda e: e.activation(out=JNK[:], in_=src[:], func=AF.Square, accum_out=SSQ[:, col:col + 1]),
              r=[bsrc], w=[bJNK, bSSQ])
            A(lambda e: e.activation(out=SSQ[:, col + 1:col + 2], in_=SSQ[:, col:col + 1], func=AF.Sqrt,
                                     scale=1.0 / D, bias=EPS_T[:, 0:1]), r=[bSSQ, bEPS], w=[bSSQ])
            V(lambda e: e.reciprocal(out=SSQ[:, col + 1:col + 2], in_=SSQ[:, col + 1:col + 2]), r=[bSSQ], w=[bSSQ])
            V(lambda e: e.scalar_tensor_tensor(out=dstf[:], in0=src[:], scalar=SSQ[:, col + 1:col + 2], in1=gB[:],
                                               op0=ALU.mult, op1=ALU.mult), r=[bsrc, bSSQ, bGB], w=[bdstf])

        EPS_T = sb("EPS_T", [128, 2]); bEPS = Buf()
        V(lambda e: e.memset(EPS_T[:, 0:1], 1e-6), w=[bEPS])
        V(lambda e: e.memset(EPS_T[:, 1:2], 64e-5), w=[bEPS])

        def mm(out, lhsT, rhs, start, stop):
            return lambda e: e.matmul(out, lhsT=lhsT, rhs=rhs, start=start, stop=stop)

        gchunk = [0]

        def do_tile(chunks):
            for k, c in enumerate(chunks):
                DMA((lambda k, c: lambda e: e.dma_start(out=X[64 * k:64 * k + 64, :], in_=c["xsrc"]))(k, c), w=[bX])
            G(lambda e: e.tensor_copy(out=CAR[:], in_=HT[:, :, 1, 64:65]), r=[bHT], w=[bCAR])
            rmsnorm(X, bX, G1B, HF, bHF, 0)
            A(lambda e: e.activation(out=HB[:], in_=HF[:], func=AF.Copy), r=[bHF], w=[bHB])
            for k, c in enumerate(chunks):
                if c.get("shift_out") is not None:
                    DMA((lambda k, c: lambda e: e.dma_start(out=c["shift_out"], in_=HF[64 * k + 63:64 * k + 64, :]))(k, c),
                        r=[bHF])
            hb = pbank[7][:].bitcast(BF16)
            for kc in range(8):
                PE((lambda kc: lambda e: e.transpose(out=hb[:, kc * 128:(kc + 1) * 128], in_=HB[:, kc * 128:(kc + 1) * 128],
                                                     identity=IDB[:]))(kc), r=[bHB, bIDB], w=[PB[7]])
            hbv = hb.rearrange("p (a c t) -> p a c t", a=8, c=2)
            A(lambda e: e.activation(out=HT[:, :, 0, 1:65], in_=hbv[:, :, 0, :], func=AF.Copy), r=[PB[7]], w=[bHT])
            V(lambda e: e.tensor_copy(out=HT[:, :, 1, 1:65], in_=hbv[:, :, 1, :]), r=[PB[7]], w=[bHT])
            for k, c in enumerate(chunks):
                pv = c["prev"]
                if pv == "zero":
                    G((lambda k: lambda e: e.memset(HT[:, :, k, 0:1], 0.0))(k), w=[bHT])
                elif pv == "carry":
                    G((lambda k: lambda e: e.tensor_copy(out=HT[:, :, k, 0:1], in_=CAR[:]))(k), r=[bCAR], w=[bHT])
                elif pv == "own":
                    G((lambda k: lambda e: e.tensor_copy(out=HT[:, :, k, 0:1], in_=HT[:, :, k - 1, 64:65]))(k), r=[bHT], w=[bHT])
                else:
                    b = pv[1]
                    def ld_state(e, k=k, b=b):
                        with nc.allow_non_contiguous_dma(reason="tiny"):
                            return e.dma_start(out=STS[:, k, :], in_=st_shift[b].rearrange("(a p) -> p a", p=128))
                    DMA(ld_state, w=[bSTS])
                    G((lambda k: lambda e: e.tensor_copy(out=HT[:, :, k, 0:1], in_=STS[:, k, :].unsqueeze(2)))(k), r=[bSTS], w=[bHT])

            ckpt(1)
            def proj_fm(slot_a, slot_b, col0, M, outap, wbank):
                n = 16 if slot_b is not None else 8
                i = 0
                for kc in range(8):
                    PE(mm(outap, RING[:, slot_a, kc * 128 + col0:kc * 128 + col0 + M], HT[:, kc, :, 1:65], i == 0, i == n - 1),
                       r=[bRING[slot_a], bHT], w=[wbank])
                    i += 1
                if slot_b is not None:
                    for kc in range(8):
                        PE(mm(outap, RING[:, slot_b, kc * 128 + col0:kc * 128 + col0 + M], HT[:, kc, :, 0:64], i == 0, i == n - 1),
                           r=[bRING[slot_b], bHT], w=[wbank])
                        i += 1

            def pview(bank, idx, M):
                return pbank[bank][0:M, idx * 128:(idx + 1) * 128].rearrange("p (c t) -> p c t", c=2)

            for qi, (dst, bdst) in enumerate(((RF, bRF), (KF, bKF), (VT, bVT))):
                for sp in range(4):
                    sa = wload(win_s[:, qi * 4 + sp, :], [bWIN[qi * 4 + sp]])
                    sbb = wload(win_s[:, 36 + qi * 4 + sp, :], [bWIN[36 + qi * 4 + sp]])
                    bank = sp // 2
                    for hl in range(2):
                        h = sp * 2 + hl
                        proj_fm(sa, sbb, hl * 64, 64, pview(bank, h % 4, 64), PB[bank])
                for bank in range(2):
                    eng = A if bank == 0 else V
                    if bank == 0:
                        A((lambda dst, bank: lambda e: e.activation(
                            out=dst[:, 4 * bank:4 * bank + 4, :], in_=pbank[bank][0:64, :].rearrange("p (h t) -> p h t", h=4),
                            func=AF.Copy))(dst, bank), r=[PB[bank]], w=[bdst])
                    else:
                        V((lambda dst, bank: lambda e: e.tensor_copy(
                            out=dst[:, 4 * bank:4 * bank + 4, :], in_=pbank[bank][0:64, :].rearrange("p (h t) -> p h t", h=4)))(dst, bank),
                          r=[PB[bank]], w=[bdst])
            ckpt(10)
            sa = wload(win_s[:, 12, :], [bWIN[12]]); sbb = wload(win_s[:, 48, :], [bWIN[48]])
            proj_fm(sa, sbb, 0, 64, pview(2, 0, 64), PB[2])
            proj_fm(sa, sbb, 64, 64, pview(2, 1, 64), PB[2])
            sa = wload(win_s[:, 13, :], [bWIN[13]]); sbb = wload(win_s[:, 49, :], [bWIN[49]])
            proj_fm(sa, sbb, 0, 128, pview(2, 2, 128), PB[2])
            A(lambda e: e.activation(out=TW[:], in_=pbank[2][0:64, 0:128], func=AF.Tanh), r=[PB[2]], w=[bTW])
            V(lambda e: e.tensor_copy(out=XA[:], in_=pbank[2][0:64, 128:256]), r=[PB[2]], w=[bXA])
            A(lambda e: e.activation(out=SG[:], in_=pbank[2][:, 256:384], func=AF.Sigmoid), r=[PB[2]], w=[bSG])
            ckpt(11)
            for (LOF, INP, bINP, b0) in ((0, TW, bTW, 3), (512, XA, bXA, 5)):
                for h in range(8):
                    bank = b0 + h // 4
                    o = pbank[bank][0:64, (h % 4) * 128:(h % 4 + 1) * 128]
                    PE(mm(o, LWX[0:64, LOF + h * 64:LOF + (h + 1) * 64], INP[:], True, False), r=[bLO, bINP], w=[PB[bank]])
                    PE(mm(o, LWX[64:65, LOF + h * 64:LOF + (h + 1) * 64], ONESF[64:65, :], False, True), r=[bLO, bONES], w=[PB[bank]])
            for bank in range(2):
                A((lambda bank: lambda e: e.activation(out=LD[:, 512 * bank:512 * bank + 512], in_=pbank[3 + bank][0:64, :],
                                                       func=AF.Sigmoid))(bank), r=[PB[3 + bank]], w=[bLD])
                A((lambda bank: lambda e: e.activation(out=AS[:, 4 * bank:4 * bank + 4, :],
                                                       in_=pbank[5 + bank][0:64, :].rearrange("p (h t) -> p h t", h=4),
                                                       func=AF.Sigmoid))(bank), r=[PB[5 + bank]], w=[bAS])
            ckpt(12)
            G(lambda e: e.tensor_scalar(out=LD[:], in0=LD[:], scalar1=-EXPM05, scalar2=None, op0=ALU.mult), r=[bLD], w=[bLD])
            V(lambda e: e.tensor_tensor_scan(out=CC[:], data0=ONESF[0:64, 0:1].to_broadcast([64, 1024]), data1=LD[:], initial=0.0, op0=ALU.mult, op1=ALU.add),
              r=[bLD, bONES], w=[bCC])
            G(lambda e: e.tensor_tensor(out=CE[:], in0=CC[:], in1=LD[:], op=ALU.subtract), r=[bCC, bLD], w=[bCE])
            V(lambda e: e.tensor_tensor(out=CUM[:].rearrange("p (a t) -> p a t", t=64),
                                        in0=CC[:].rearrange("p (a t) -> p a t", t=64),
                                        in1=CE[:].rearrange("p (a t) -> p a t", t=64)[:, :, 0:1].to_broadcast([64, 16, 64]),
                                        op=ALU.subtract), r=[bCC, bCE], w=[bCUM])
            G(lambda e: e.tensor_tensor(out=CUMX[:], in0=CUM[:], in1=LD[:], op=ALU.subtract), r=[bCUM, bLD], w=[bCUMX])
            A(lambda e: e.activation(out=EP[:], in_=CUM[:], func=AF.Exp), r=[bCUM], w=[bEP])
            A(lambda e: e.activation(out=EPX[:], in_=CUMX[:], func=AF.Exp), r=[bCUMX], w=[bEPX])
            A(lambda e: e.activation(out=EM[:], in_=CUM[:], func=AF.Exp, scale=-1.0), r=[bCUM], w=[bEM])
            ckpt(13)
            V(lambda e: e.tensor_tensor(out=KKR[:], in0=KF[:], in1=P_KK.unsqueeze(2).to_broadcast([64, 8, 128]), op=ALU.mult),
              r=[bKF, bPAR], w=[bKKR])
            G(lambda e: e.tensor_tensor(out=SQ[:], in0=KKR[:], in1=KKR[:], op=ALU.mult), r=[bKKR], w=[bSQ])
            for bank in range(2):
                PE(mm(pbank[3 + bank][0:64, :], ONESF[0:64, 0:64], SQ[:, 4 * bank:4 * bank + 4, :], True, True), r=[bSQ, bONES], w=[PB[3 + bank]])
                A((lambda bank: lambda e: e.activation(out=RN[:, 4 * bank:4 * bank + 4, :],
                                                       in_=pbank[3 + bank][0:64, :].rearrange("p (h t) -> p h t", h=4),
                                                       func=AF.Sqrt))(bank), r=[PB[3 + bank]], w=[bRN])
            V(lambda e: e.tensor_scalar(out=RN[:], in0=RN[:], scalar1=1e-12, scalar2=None, op0=ALU.max), r=[bRN], w=[bRN])
            V(lambda e: e.reciprocal(out=RN[:], in_=RN[:]), r=[bRN], w=[bRN])
            V(lambda e: e.tensor_tensor(out=KK[:], in0=KKR[:], in1=RN[:], op=ALU.mult), r=[bKKR, bRN], w=[bKK])
            ckpt(14)
            V(lambda e: e.scalar_tensor_tensor(out=T1[:], in0=AS[:], scalar=-1.0, in1=P_KA.unsqueeze(2).to_broadcast([64, 8, 128]),
                                               op0=ALU.add, op1=ALU.mult), r=[bAS, bPAR], w=[bT1])
            V(lambda e: e.scalar_tensor_tensor(out=KP[:], in0=T1[:], scalar=1.0, in1=KF[:], op0=ALU.add, op1=ALU.mult),
              r=[bT1, bKF], w=[bKP])
            G(lambda e: e.tensor_tensor(out=BB[:], in0=KK[:], in1=AS[:], op=ALU.mult), r=[bKK, bAS], w=[bBB])
            G(lambda e: e.tensor_tensor(out=RKP[:], in0=RF[:], in1=KP[:], op=ALU.mult), r=[bRF, bKP], w=[bRKP])

            def v4(t):
                return t[:].rearrange("p (h c t) -> p h c t", h=8, c=2)

            def w4(t):
                return t[:].rearrange("p h (c t) -> p h c t", c=2)
            V(lambda e: e.tensor_tensor(out=KR[:, :, :, 0, :], in0=w4(KK), in1=v4(EPX), op=ALU.mult), r=[bKK, bEPX], w=[bKR])
            G(lambda e: e.tensor_tensor(out=KR[:, :, :, 1, :], in0=w4(RF), in1=v4(EP), op=ALU.mult), r=[bRF, bEP], w=[bKR])
            V(lambda e: e.tensor_tensor(out=BK[:, :, :, 0, :], in0=w4(BB), in1=v4(EM), op=ALU.mult), r=[bBB, bEM], w=[bBK])
            G(lambda e: e.tensor_tensor(out=BK[:, :, :, 1, :], in0=w4(KP), in1=v4(EM), op=ALU.mult), r=[bKP, bEM], w=[bBK])

            ckpt(16)
            for sp in range(4):
                sa = wload(win_s[:, 14 + sp, :], [bWIN[14 + sp]])
                bank = 3 + sp // 2
                for hl in range(2):
                    h = sp * 2 + hl
                    proj_fm(sa, None, hl * 64, 64, pview(bank, h % 4, 64), PB[bank])
            for bank in range(2):
                A((lambda bank: lambda e: e.activation(out=QT[:, 4 * bank:4 * bank + 4, :],
                                                       in_=pbank[3 + bank][0:64, :].rearrange("p (h t) -> p h t", h=4),
                                                       func=AF.Copy))(bank), r=[PB[3 + bank]], w=[bQT])
            ckpt(17)
            sa = wload(win_s[:, 18, :], [bWIN[18]])
            proj_fm(sa, None, 0, 64, pview(5, 0, 64), PB[5])
            proj_fm(sa, None, 64, 64, pview(5, 1, 64), PB[5])
            need_kout = any(c.get("k_out") is not None for c in chunks)
            for k, c in enumerate(chunks):
                sl = c["kslots"][2]
                V((lambda k, sl: lambda e: e.tensor_copy(
                    out=KTR[:, :, sl, :], in_=pbank[5][0:64, 0:256].rearrange("p (v c t) -> p v c t", v=2, c=2)[:, :, k, :]))(k, sl),
                  r=[PB[5]], w=[bKTR[sl]])
            if need_kout:
                A(lambda e: e.activation(out=KTF[:], in_=pbank[5][0:64, 0:256].rearrange("p (v t) -> p v t", v=2), func=AF.Copy),
                  r=[PB[5]], w=[bKTF])
            ckpt(18)
            sa = wload(win_s[:, 19, :], [bWIN[19]])
            for k, c in enumerate(chunks):
                sl = c["kslots"][2]
                o = pbank[6][0:64, k * 128:(k + 1) * 128]
                for kc in range(8):
                    PE(mm(o, HT[:, kc, k, 1:65], RING[:, sa, kc * 128:(kc + 1) * 128], kc == 0, kc == 7), r=[bRING[sa], bHT], w=[PB[6]])
                if c.get("v_out") is not None:
                    A((lambda k, o: lambda e: e.activation(out=VAF[:, k, :], in_=o, func=AF.Copy))(k, o), r=[PB[6]], w=[bVAF])
                    V((lambda sl, k: lambda e: e.tensor_copy(out=VAR[:, sl, :], in_=VAF[:, k, :]))(sl, k), r=[bVAF], w=[bVAR[sl]])
                    DMA((lambda k, c: lambda e: e.dma_start(out=c["v_out"], in_=VAF[:, k, :]))(k, c), r=[bVAF])
                else:
                    V((lambda sl, o: lambda e: e.tensor_copy(out=VAR[:, sl, :], in_=o))(sl, o), r=[PB[6]], w=[bVAR[sl]])
            ckpt(19)
            if need_kout:
                for kvh in range(2):
                    PE((lambda kvh: lambda e: e.transpose(out=pbank[7][:, kvh * 64:(kvh + 1) * 64], in_=KTF[:, kvh, :],
                                                          identity=IDF[0:64, 0:64]))(kvh), r=[bKTF, bCST], w=[PB[7]])
                V(lambda e: e.tensor_copy(out=KTOK[:], in_=pbank[7][:, 0:128]), r=[PB[7]], w=[bKTOK])
                for k, c in enumerate(chunks):
                    if c.get("k_out") is not None:
                        DMA((lambda k, c: lambda e: e.dma_start(out=c["k_out"], in_=KTOK[64 * k:64 * k + 64, :]))(k, c), r=[bKTOK])
            ckpt(20)
            for gq in range(4):
                bank = 3 + (gq % 2)
                for j in range(4):
                    sa = wload(win_s[:, 20 + gq * 4 + j, :], [bWIN[20 + gq * 4 + j]])
                    proj_fm(sa, None, 0, 128, pview(bank, j, 128), PB[bank])
                A((lambda gq, bank: lambda e: e.activation(out=SGT[:, 4 * gq:4 * gq + 4, :],
                                                           in_=pbank[bank][:, :].rearrange("p (h t) -> p h t", h=4),
                                                           func=AF.Sigmoid))(gq, bank), r=[PB[bank]], w=[bSGT])

            ckpt(21)
            def do_chunk(k, c):
                pr = 0
                gchunk[0] += 1
                cs = slice(64 * k, 64 * k + 64)
                def f_vtrans():
                    for h in range(8):
                        PE((lambda h: lambda e: e.transpose(out=pbank[3][0:64, h * 64:(h + 1) * 64], in_=VT[:, h, cs],
                                                            identity=IDF[0:64, 0:64]))(h), r=[bVT, bCST], w=[PB[3]])
                    A((lambda pr: lambda e: e.activation(out=VR[pr][:], in_=pbank[3][0:64, :].rearrange("p (h i) -> p h i", h=8),
                                                         func=AF.Copy))(pr), r=[PB[3]], w=[bVR[pr]])

                def f_bkt(w2):
                    bk = 4 - w2
                    for h in range(8):
                        PE((lambda h, w2: lambda e: e.transpose(out=pbank[bk][0:64, h * 64:(h + 1) * 64], in_=BK[:, h, k, w2, :],
                                                                identity=IDF[0:64, 0:64]))(h, w2), r=[bBK, bCST], w=[PB[bk]])
                    V((lambda pr, w2: lambda e: e.tensor_copy(out=BKT[pr][:, :, w2, :],
                                                              in_=pbank[bk][0:64, :].rearrange("p (h j) -> p h j", h=8)))(pr, w2),
                      r=[PB[bk]], w=[bBKT[pr]])

                def f_rkg():
                    for h in range(8):
                        PE(mm(pbank[7][0:64, h:h + 1], RKP[:, h, cs], P_RK[:, h:h + 1], True, True), r=[bRKP, bPAR], w=[PB[7]])
                    V((lambda pr: lambda e: e.tensor_copy(out=RKB[pr][:], in_=pbank[7][0:64, 0:8]))(pr), r=[PB[7]], w=[bRKB[pr]])
                    PE(mm(pbank[7][0:64, :], SG[:, cs], GL[:], True, True), r=[bSG, bLO], w=[PB[7]])
                    A(lambda e: e.activation(out=GG[:], in_=pbank[7][0:64, :], func=AF.Copy), r=[PB[7]], w=[bGG])

                ks = c["kslots"]
                valid = [i for i in range(3) if ks[i] is not None]
                al = ALIBI.rearrange("p (v k x) -> p v k x", v=2, k=3)

                def f_sc(kvh):
                    for kc in valid:
                        bank = 5 + (kc * 256) // 512
                        o = pbank[bank][0:64, (kc * 256) % 512:(kc * 256) % 512 + 256]
                        PE(mm(o, KTR[:, kvh, ks[kc], :], QT[:, 4 * kvh:4 * kvh + 4, cs], True, True),
                           r=[bKTR[ks[kc]], bQT], w=[PB[bank]])
                    for kc in valid:
                        bank = 5 + (kc * 256) // 512
                        o = pbank[bank][0:64, (kc * 256) % 512:(kc * 256) % 512 + 256]
                        A((lambda kc, o: lambda e: e.activation(out=EE[:, kc, :], in_=o, func=AF.Exp, scale=0.125))(kc, o),
                          r=[PB[bank]], w=[bEE])
                    k0, k1 = valid[0], valid[-1] + 1
                    V((lambda kvh, k0, k1: lambda e: e.tensor_tensor(out=PT[:, k0:k1, :], in0=EE[:, k0:k1, :],
                                                                     in1=al[:, kvh, k0:k1, :], op=ALU.mult))(kvh, k0, k1),
                      r=[bEE, bCST], w=[bPT])

                def f_pv(kvh):
                    for ii, kc in enumerate(valid):
                        PE(mm(pbank[7][0:64, 0:256], VAR[:, ks[kc], kvh * 64:(kvh + 1) * 64], PT[:, kc, :], ii == 0, ii == len(valid) - 1),
                           r=[bVAR[ks[kc]], bPT], w=[PB[7]])
                    for ii, kc in enumerate(valid):
                        PE(mm(pbank[7][0:64, 256:512], ONESB[:], PT[:, kc, :], ii == 0, ii == len(valid) - 1),
                           r=[bONES, bPT], w=[PB[7]])
                    V((lambda kvh: lambda e: e.tensor_tensor(
                        out=DEN[:].rearrange("p (g q) -> p g q", g=4), in0=pbank[7][0:64, 256:512].rearrange("p (g q) -> p g q", g=4),
                        in1=SNKE[:, 4 * kvh:4 * kvh + 4].unsqueeze(2).to_broadcast([64, 4, 64]), op=ALU.add))(kvh),
                      r=[PB[7], bSNK], w=[bDEN])
                    V(lambda e: e.reciprocal(out=DEN[:], in_=DEN[:]), r=[bDEN], w=[bDEN])
                    V((lambda kvh: lambda e: e.tensor_tensor(
                        out=OATT[:, 4 * kvh:4 * kvh + 4, cs], in0=pbank[7][0:64, 0:256].rearrange("p (g q) -> p g q", g=4),
                        in1=DEN[:].rearrange("p (g q) -> p g q", g=4), op=ALU.mult))(kvh), r=[PB[7], bDEN], w=[bOATT])

                fillers = [f_vtrans, lambda: f_bkt(0), lambda: f_bkt(1), f_rkg, lambda: f_sc(0), lambda: f_pv(0),
                           lambda: f_sc(1), lambda: f_pv(1)]

                def fill():
                    if fillers:
                        fillers.pop(0)()

                ckpt(30)
                for h in range(8):
                    bank = h // 4
                    PE(mm(pbank[bank][0:64, (h % 4) * 128:(h % 4 + 1) * 128], BK[:, h, k, 0, :], KR[:, h, k, :, :], True, True),
                       r=[bBK, bKR], w=[PB[bank]])
                for h in range(8):
                    bank = 2 + h // 4
                    PE(mm(pbank[bank][0:64, (h % 4) * 128:(h % 4 + 1) * 128], BK[:, h, k, 1, :], KR[:, h, k, :, :], True, True),
                       r=[bBK, bKR], w=[PB[bank]])
                for h in range(8):
                    PE(mm(pbank[4][0:64, h * 64:(h + 1) * 64], KR[:, h, k, 0, :], BK[:, h, k, 0, :], True, True),
                       r=[bBK, bKR], w=[PB[4]])
                m1v = M1.rearrange("p (w t) -> p w t", w=2)
                for bank in range(2):
                    pv4 = pbank[bank][0:64, :].rearrange("p (h w t) -> p h w t", h=4, w=2)
                    V((lambda pr, bank, pv4: lambda e: e.tensor_tensor(
                        out=NN[pr][:, 4 * bank:4 * bank + 4, :], in0=pv4[:, :, 0, :],
                        in1=m1v[:, 0:1, :].to_broadcast([64, 4, 64]), op=ALU.mult))(pr, bank, pv4), r=[PB[bank], bCST], w=[bNN[pr]])
                    V((lambda pr, bank, pv4: lambda e: e.tensor_tensor(
                        out=ARB[pr][:, 4 * bank:4 * bank + 4, :], in0=pv4[:, :, 1, :],
                        in1=m1v[:, 1:2, :].to_broadcast([64, 4, 64]), op=ALU.mult))(pr, bank, pv4), r=[PB[bank], bCST], w=[bARB[pr]])
                    pk4 = pbank[2 + bank][0:64, :].rearrange("p (h w t) -> p h w t", h=4, w=2)
                    V((lambda pr, bank, pk4: lambda e: e.tensor_tensor(
                        out=AKT[pr][:, 4 * bank:4 * bank + 4, :], in0=pk4[:, :, 0, :],
                        in1=m1v[:, 0:1, :].to_broadcast([64, 4, 64]), op=ALU.mult))(pr, bank, pk4), r=[PB[2 + bank], bCST], w=[bAKT[pr]])
                    V((lambda pr, bank, pk4: lambda e: e.tensor_tensor(
                        out=ARK[pr][:, 4 * bank:4 * bank + 4, :], in0=pk4[:, :, 1, :],
                        in1=m1v[:, 1:2, :].to_broadcast([64, 4, 64]), op=ALU.mult))(pr, bank, pk4), r=[PB[2 + bank], bCST], w=[bARK[pr]])
                V((lambda pr: lambda e: e.tensor_tensor(
                    out=QQ[pr][:], in0=pbank[4][0:64, :].rearrange("p (h s) -> p h s", h=8),
                    in1=ML.unsqueeze(1).to_broadcast([64, 8, 64]), op=ALU.mult))(pr), r=[PB[4], bCST], w=[bQQ[pr]])
                G((lambda pr: lambda e: e.tensor_tensor(
                    out=XX[pr][:], in0=IDF[0:64, 0:64].unsqueeze(1).to_broadcast([64, 8, 64]), in1=NN[pr][:], op=ALU.subtract))(pr),
                  r=[bCST, bNN[pr]], w=[bXX[pr]])
                Pc, Qc, bPc, bQc = NN[pr], QQ[pr], bNN[pr], bQQ[pr]
                Pn, Qn, bPn, bQn = NN2[pr], QQ2[pr], bNN2[pr], bQQ2[pr]
                for rd in range(1, 6):
                    last = rd == 5
                    if not last:
                        for h in range(8):
                            PE(mm(pbank[0][0:64, h * 64:(h + 1) * 64], Qc[:, h, :], Pc[:, h, :], True, True), r=[bPc, bQc], w=[PB[0]])
                    for h in range(8):
                        PE(mm(pbank[1][0:64, h * 64:(h + 1) * 64], Pc[:, h, :], Qc[:, h, :], True, True), r=[bPc, bQc], w=[PB[1]])
                    if not last:
                        A((lambda Pn: lambda e: e.activation(out=Pn[:], in_=pbank[0][0:64, :].rearrange("p (h s) -> p h s", h=8),
                                                             func=AF.Copy))(Pn), r=[PB[0]], w=[bPn])
                    V((lambda Qn: lambda e: e.tensor_copy(out=Qn[:], in_=pbank[1][0:64, :].rearrange("p (h s) -> p h s", h=8)))(Qn),
                      r=[PB[1]], w=[bQn])
                    fill()
                    for h in range(8):
                        PE(mm(pbank[2][0:64, h * 64:(h + 1) * 64], Qn[:, h, :], XX[pr][:, h, :], True, True), r=[bQn, bXX[pr]], w=[PB[2]])
                    V((lambda pr: lambda e: e.tensor_tensor(out=XX[pr][:], in0=XX[pr][:],
                                                            in1=pbank[2][0:64, :].rearrange("p (h s) -> p h s", h=8), op=ALU.add))(pr),
                      r=[PB[2], bXX[pr]], w=[bXX[pr]])
                    fill()
                    Pc, Qc, bPc, bQc, Pn, Qn, bPn, bQn = Pn, Qn, bPn, bQn, Pc, Qc, bPc, bQc

                while fillers:
                    fill()
                ckpt(31)
                stt = c["state"]
                if stt == "zero":
                    G(lambda e: e.memset(HH[:], 0.0), w=[bHH])
                elif stt != "keep":
                    b = stt[1]
                    DMA((lambda b: lambda e: e.dma_start(out=SIN[:], in_=st_wkv[b].rearrange("h i j -> i h j")))(b), w=[bSIN])
                    for h in range(8):
                        PE((lambda h: lambda e: e.transpose(out=pbank[3][0:64, h * 64:(h + 1) * 64], in_=SIN[:, h, :],
                                                            identity=IDF[0:64, 0:64]))(h), r=[bSIN, bCST], w=[PB[3]])
                    V(lambda e: e.tensor_copy(out=HH[:], in_=pbank[3][0:64, :].rearrange("p (h i) -> p h i", h=8)), r=[PB[3]], w=[bHH])
                for h in range(8):
                    o = pbank[3][0:64, h * 64:(h + 1) * 64]
                    PE(mm(o, KR[:, h, k, 0, :], HH[:, h, :], True, False), r=[bKR, bHH], w=[PB[3]])
                    PE(mm(o, AKT[pr][:, h, :], VR[pr][:, h, :], False, True), r=[bAKT[pr], bVR[pr]], w=[PB[3]])
                A(lambda e: e.activation(out=WN[:], in_=pbank[3][0:64, :].rearrange("p (h i) -> p h i", h=8), func=AF.Copy, scale=-1.0),
                  r=[PB[3]], w=[bWN])
                for h in range(8):
                    PE(mm(pbank[4][0:64, h * 64:(h + 1) * 64], XX[pr][:, h, :], WN[:, h, :], True, True), r=[bXX[pr], bWN], w=[PB[4]])
                V(lambda e: e.tensor_copy(out=UU[:], in_=pbank[4][0:64, :].rearrange("p (h i) -> p h i", h=8)), r=[PB[4]], w=[bUU])
                for h in range(8):
                    o = pbank[3][0:64, h * 64:(h + 1) * 64]
                    PE(mm(o, KR[:, h, k, 1, :], HH[:, h, :], True, False), r=[bKR, bHH], w=[PB[3]])
                    PE(mm(o, ARB[pr][:, h, :], UU[:, h, :], False, False), r=[bARB[pr], bUU], w=[PB[3]])
                    PE(mm(o, ARK[pr][:, h, :], VR[pr][:, h, :], False, True), r=[bARK[pr], bVR[pr]], w=[PB[3]])
                for h in range(8):
                    o = pbank[4][0:64, h * 64:(h + 1) * 64]
                    PE(mm(o, BKT[pr][:, h, 0, :], UU[:, h, :], True, False), r=[bBKT[pr], bUU], w=[PB[4]])
                    PE(mm(o, BKT[pr][:, h, 1, :], VR[pr][:, h, :], False, True), r=[bBKT[pr], bVR[pr]], w=[PB[4]])
                V(lambda e: e.tensor_tensor(out=HTMP[:], in0=pbank[4][0:64, :].rearrange("p (h i) -> p h i", h=8), in1=HH[:], op=ALU.add),
                  r=[PB[4], bHH], w=[bHTMP])
                epl = EP[:].rearrange("p (h c t) -> p h c t", h=8, c=2)[:, :, k, 63:64]
                V(lambda e: e.tensor_tensor(out=HH[:], in0=HTMP[:], in1=epl.to_broadcast([64, 8, 64]), op=ALU.mult),
                  r=[bHTMP, bEP], w=[bHH])
                if c.get("wkv_out") is not None:
                    for h in range(8):
                        PE((lambda h: lambda e: e.transpose(out=pbank[4][0:64, h * 64:(h + 1) * 64], in_=HH[:, h, :],
                                                            identity=IDF[0:64, 0:64]))(h), r=[bHH, bCST], w=[PB[4]])
                    V(lambda e: e.tensor_copy(out=SOUT[:], in_=pbank[4][0:64, :]), r=[PB[4]], w=[bSOUT])
                    DMA((lambda c: lambda e: e.dma_start(out=c["wkv_out"].rearrange("h i j -> i h j"),
                                                         in_=SOUT[:].rearrange("p (h j) -> p h j", h=8)))(c), r=[bSOUT])
                y3 = pbank[3][0:64, :].rearrange("p (h i) -> p h i", h=8)
                A(lambda e: e.activation(out=YS[:], in_=y3, func=AF.Copy), r=[PB[3]], w=[bYS])
                A(lambda e: e.activation(out=YQ[:], in_=y3, func=AF.Square), r=[PB[3]], w=[bYQ])
                V(lambda e: e.tensor_reduce(out=ST8[:, 0, :], in_=YS[:], axis=AX.X, op=ALU.add), r=[bYS], w=[bST8])
                V(lambda e: e.tensor_reduce(out=ST8[:, 1, :], in_=YQ[:], axis=AX.X, op=ALU.add), r=[bYQ], w=[bST8])
                V(lambda e: e.tensor_scalar(out=ST8[:, 0, :], in0=ST8[:, 0, :], scalar1=1.0 / 64, scalar2=None, op0=ALU.mult), r=[bST8], w=[bST8])
                V(lambda e: e.tensor_tensor(out=ST8[:, 2, :], in0=ST8[:, 0, :], in1=ST8[:, 0, :], op=ALU.mult), r=[bST8], w=[bST8])
                V(lambda e: e.scalar_tensor_tensor(out=ST8[:, 3, :], in0=ST8[:, 1, :], scalar=1.0 / 64, in1=ST8[:, 2, :],
                                                   op0=ALU.mult, op1=ALU.subtract), r=[bST8], w=[bST8])
                A(lambda e: e.activation(out=ST8[:, 4, :], in_=ST8[:, 3, :], func=AF.Sqrt, bias=EPS_T[0:64, 1:2]), r=[bST8, bEPS], w=[bST8])
                V(lambda e: e.reciprocal(out=ST8[:, 5, :], in_=ST8[:, 4, :]), r=[bST8], w=[bST8])
                V(lambda e: e.tensor_tensor(out=YS[:], in0=YS[:], in1=ST8[:, 0, :].unsqueeze(2).to_broadcast([64, 8, 64]), op=ALU.subtract),
                  r=[bYS, bST8], w=[bYS])
                V(lambda e: e.tensor_tensor(out=YS[:], in0=YS[:], in1=ST8[:, 5, :].unsqueeze(2).to_broadcast([64, 8, 64]), op=ALU.mult),
                  r=[bYS, bST8], w=[bYS])
                ysf = YS[:].rearrange("p h i -> p (h i)")
                G(lambda e: e.tensor_tensor(out=ysf, in0=ysf, in1=LNW[:], op=ALU.mult), r=[bYS, bLN], w=[bYS])
                G(lambda e: e.tensor_tensor(out=ysf, in0=ysf, in1=LNB[:], op=ALU.add), r=[bYS, bLN], w=[bYS])
                V((lambda pr: lambda e: e.tensor_tensor(out=YQ[:], in0=VR[pr][:], in1=RKB[pr][:].unsqueeze(2).to_broadcast([64, 8, 64]),
                                                        op=ALU.mult))(pr), r=[bVR[pr], bRKB[pr]], w=[bYQ])
                G(lambda e: e.tensor_tensor(out=YS[:], in0=YS[:], in1=YQ[:], op=ALU.add), r=[bYS, bYQ], w=[bYS])
                V(lambda e: e.tensor_tensor(out=ORW[:], in0=ysf, in1=GG[:], op=ALU.mult), r=[bYS, bGG], w=[bORW])
                ob = pbank[0][:].bitcast(BF16)
                for q4 in range(4):
                    PE((lambda q4: lambda e: e.transpose(out=ob[:, q4 * 64:(q4 + 1) * 64], in_=ORW[:, q4 * 128:(q4 + 1) * 128],
                                                         identity=IDB[0:64, 0:64]))(q4), r=[bORW, bIDB], w=[PB[0]])
                V(lambda e: e.tensor_copy(out=ORWT[:, :, cs], in_=ob[:, 0:256].rearrange("p (q t) -> p q t", q=4)), r=[PB[0]], w=[bORWT])

            for k_, c_ in enumerate(chunks):
                do_chunk(k_, c_)
                ckpt(32)

            for dp in range(4):
                sa = wload(wb0_s[:, dp, :], [bWB0])
                bank = dp // 2
                for dl in range(2):
                    dc = dp * 2 + dl
                    o = pbank[bank][:, (dc % 4) * 128:(dc % 4 + 1) * 128]
                    for kc in range(4):
                        PE(mm(o, RING[:, sa, kc * 256 + dl * 128:kc * 256 + dl * 128 + 128], ORWT[:, kc, :], kc == 0, kc == 3),
                           r=[bRING[sa], bORWT], w=[PB[bank]])
            for dc in range(8):
                sa = wload(wb1_s[:, dc, :], [bWB1], npart=64)
                bank = 2 + dc // 4
                o = pbank[bank][:, (dc % 4) * 128:(dc % 4 + 1) * 128]
                for qh in range(8):
                    PE(mm(o, RING[0:64, sa, qh * 128:(qh + 1) * 128], OATT[:, qh, :], qh == 0, qh == 7),
                       r=[bRING[sa], bOATT], w=[PB[bank]])
            for bank in range(2):
                V((lambda bank: lambda e: e.tensor_tensor(out=MT1[:, 4 * bank:4 * bank + 4, :],
                                                          in0=pbank[bank][:, :].rearrange("p (a t) -> p a t", a=4),
                                                          in1=SGT[:, 4 * bank:4 * bank + 4, :], op=ALU.mult))(bank),
                  r=[PB[bank], bSGT], w=[bMT1])
                V((lambda bank: lambda e: e.tensor_tensor(out=MT2[:, 4 * bank:4 * bank + 4, :],
                                                          in0=pbank[2 + bank][:, :].rearrange("p (a t) -> p a t", a=4),
                                                          in1=SGT[:, 8 + 4 * bank:8 + 4 * bank + 4, :], op=ALU.mult))(bank),
                  r=[PB[2 + bank], bSGT], w=[bMT2])
            V(lambda e: e.tensor_tensor(out=MRG[:], in0=MT1[:], in1=MT2[:], op=ALU.add), r=[bMT1, bMT2], w=[bMRG])
            for cs8 in range(8):
                sa = wload(wo_s[:, cs8, :], [bWO])
                bank = 4 + cs8 // 4
                o = pbank[bank][:, (cs8 % 4) * 128:(cs8 % 4 + 1) * 128]
                for kc in range(8):
                    PE(mm(o, MRG[:, kc, :], RING[:, sa, kc * 128:(kc + 1) * 128], kc == 0, kc == 7), r=[bRING[sa], bMRG], w=[PB[bank]])
            for bank in range(2):
                V((lambda bank: lambda e: e.tensor_tensor(out=X[:, 512 * bank:512 * bank + 512], in0=X[:, 512 * bank:512 * bank + 512],
                                                          in1=pbank[4 + bank][:, :], op=ALU.add))(bank), r=[PB[4 + bank], bX], w=[bX])

            ckpt(40)
            if do_peer:
                do_peer_tile()

            rmsnorm(X, bX, GFB, YO, bYO, 2)
            for k, c in enumerate(chunks):
                DMA((lambda k, c: lambda e: e.dma_start(out=c["yout"], in_=YO[64 * k:64 * k + 64, :]))(k, c), r=[bYO])

        if do_peer:
            bigA = [bRF, bKF, bVT, bLD, bCC, bCE, bCUM, bAS]
            bigB = [bKKR, bSQ, bT1, bBB]
            GWT = BIGA[:].bitcast(BF16).rearrange("p (i t) -> p i t", i=128)
            QREP = BIGB[64:128, :].bitcast(BF16).rearrange("p (t m) -> p t m", t=64)
            H2T = sb("H2T", [128, 8, 128], BF16); bH2T = Buf()
            QTP = sb("QTP", [128, 8, 128], BF16); bQTP = Buf()
            SC = sb("SC", [128, 256]); bSC = Buf()
            TOP1 = sb("TOP1", [128, 8, 16]); TOP2 = sb("TOP2", [128, 8, 16]); bTOP = Buf()
            IDXU = sb("IDXU", [128, 8, 16], U32); IDXF = sb("IDXF", [128, 8, 16]); bIDX = Buf()
            BEST = sb("BEST", [128, 8, 16]); bBEST = Buf()
            DD = sb("DD", [128, 8, 16]); bDD = Buf()
            ST = sb("ST", [128, 6, 8]); bST = Buf()
            C1T = VW(AKTf[:, 0:128].rearrange("p (h a) -> p h a", h=8)); C2T = VW(AKTf[:, 128:256].rearrange("p (h a) -> p h a", h=8))
            bCT = bAKT[0]
            CT3 = VW(ARKf[:, 0:384].rearrange("p (c t) -> p c t", c=3)); bCT3 = bARK[0]
            EXB = [VW(NNf[:].rearrange("p (j i) -> p j i", j=4)), VW(QQf[:].rearrange("p (j i) -> p j i", j=4))]
            bEXB = [bNN[0], bQQ[0]]
            RB = [VW(NN2f[:].bitcast(BF16)[:, 0:512].rearrange("p (j i) -> p j i", j=4)),
                  VW(QQ2f[:].bitcast(BF16)[:, 0:512].rearrange("p (j i) -> p j i", j=4))]
            bRB = [bNN2[0], bQQ2[0]]
            E1B = [VW(XXf[:].bitcast(BF16)[:, 0:512].rearrange("p (j i) -> p j i", j=4)),
                   VW(ARBf[:].bitcast(BF16)[:, 0:512].rearrange("p (j i) -> p j i", j=4))]
            bE1B = [bXX[0], bARB[0]]
            GB = [sb("GB%d" % i, [128, 128], BF16) for i in range(2)]; bGB_ = [Buf(), Buf()]
            GWB = [sb("GWB%d" % i, [128, 128], BF16) for i in range(2)]; bGWB = [Buf(), Buf()]
            bXR = [Buf() for _ in range(8)]; dcount = [0]
            bEXj = [[Buf() for _ in range(4)] for _ in range(2)]; bRj = [[Buf() for _ in range(4)] for _ in range(2)]
            bE1j = [[Buf() for _ in range(4)] for _ in range(2)]
            CAND = YO[:].rearrange("p (h x) -> p h x", h=4)
            CW = HF[:].rearrange("p (h x) -> p h x", h=4)

        def do_peer_tile():
            rmsnorm(X, bX, G2B, HF, bHF, 0)
            A(lambda e: e.activation(out=HB[:], in_=HF[:], func=AF.Copy), r=[bHF], w=[bHB])
            hb = pbank[7][:].bitcast(BF16)
            for kc in range(8):
                PE((lambda kc: lambda e: e.transpose(out=hb[:, kc * 128:(kc + 1) * 128], in_=HB[:, kc * 128:(kc + 1) * 128],
                                                     identity=IDB[:]))(kc), r=[bHB, bIDB], w=[PB[7]])
            A(lambda e: e.activation(out=H2T[:].rearrange("p a t -> p (a t)"), in_=hb, func=AF.Copy), r=[PB[7]], w=[bH2T])
            for h in range(8):
                sa = wload(wq_s[:, h, :], [bWQ])
                bank = 2 + h // 4
                o = pbank[bank][:, (h % 4) * 128:(h % 4 + 1) * 128]
                for kc in range(8):
                    PE(mm(o, RING[:, sa, kc * 128:(kc + 1) * 128], H2T[:, kc, :], kc == 0, kc == 7), r=[bRING[sa], bH2T], w=[PB[bank]])
            for bank in range(2):
                A((lambda bank: lambda e: e.activation(out=QTP[:, 4 * bank:4 * bank + 4, :],
                                                       in_=pbank[2 + bank][:, :].rearrange("p (h t) -> p h t", h=4),
                                                       func=AF.Copy))(bank), r=[PB[2 + bank]], w=[bQTP])
            for h in range(8):
                bank = 4 + h // 2
                o = pbank[bank][:, (h % 2) * 256:(h % 2) * 256 + 256]
                PE(mm(o, QTP[:, h, :], KEYB[:], True, True), r=[bQTP, bKEY], w=[PB[bank]])
            for h in range(8):
                bank = 4 + h // 2
                for p, TOP in enumerate((TOP1, TOP2)):
                    sv = pbank[bank][:, (h % 2) * 256 + p * 128:(h % 2) * 256 + p * 128 + 128]
                    V((lambda h, TOP, sv: lambda e: e.max(out=TOP[:, h, 0:8], in_=sv))(h, TOP, sv), r=[PB[bank]], w=[bTOP])
                    V((lambda h, TOP, sv, p: lambda e: e.match_replace(out=SC[:, p * 128:(p + 1) * 128], in_to_replace=TOP[:, h, 0:8],
                                                                       in_values=sv, imm_value=-1e30))(h, TOP, sv, p),
                      r=[PB[bank], bTOP], w=[bSC])
                    V((lambda h, TOP, p: lambda e: e.max(out=TOP[:, h, 8:16], in_=SC[:, p * 128:(p + 1) * 128]))(h, TOP, p),
                      r=[bSC], w=[bTOP])
                    if p == 0:
                        V((lambda h, sv: lambda e: e.max_index(out=IDXU[:, h, 0:8], in_max=TOP1[:, h, 0:8], in_values=sv))(h, sv),
                          r=[PB[bank], bTOP], w=[bIDX])
                        V((lambda h: lambda e: e.max_index(out=IDXU[:, h, 8:16], in_max=TOP1[:, h, 8:16], in_values=SC[:, 0:128]))(h),
                          r=[bSC, bTOP], w=[bIDX])
            V(lambda e: e.tensor_copy(out=IDXF[:], in_=IDXU[:]), r=[bIDX], w=[bIDX])
            for hh in range(2):
                V((lambda hh: lambda e: e.tensor_tensor(
                    out=CAND.rearrange("p h (a b) -> p h a b", a=16),
                    in0=TOP1[:, 4 * hh:4 * hh + 4, :].unsqueeze(3).to_broadcast([128, 4, 16, 16]),
                    in1=TOP2[:, 4 * hh:4 * hh + 4, :].unsqueeze(2).to_broadcast([128, 4, 16, 16]), op=ALU.add))(hh),
                  r=[bTOP], w=[bYO])
                for hl in range(4):
                    h = 4 * hh + hl
                    V((lambda h, hl: lambda e: e.max(out=BEST[:, h, 0:8], in_=CAND[:, hl, :]))(h, hl), r=[bYO], w=[bBEST])
                    V((lambda h, hl: lambda e: e.match_replace(out=CW[:, hl, :], in_to_replace=BEST[:, h, 0:8], in_values=CAND[:, hl, :],
                                                               imm_value=-1e30))(h, hl), r=[bYO, bBEST], w=[bHF])
                    V((lambda h, hl: lambda e: e.max(out=BEST[:, h, 8:16], in_=CW[:, hl, :]))(h, hl), r=[bHF], w=[bBEST])
            mB = BEST[:, :, 0:1]
            V(lambda e: e.tensor_tensor(out=DD[:], in0=BEST[:], in1=mB.to_broadcast([128, 8, 16]), op=ALU.subtract), r=[bBEST], w=[bDD])
            A(lambda e: e.activation(out=DD[:], in_=DD[:], func=AF.Exp), r=[bDD], w=[bDD])
            V(lambda e: e.tensor_reduce(out=ST[:, 0, :], in_=DD[:], axis=AX.X, op=ALU.add), r=[bDD], w=[bST])
            A(lambda e: e.activation(out=ST[:, 1, :], in_=ST[:, 0, :], func=AF.Ln), r=[bST], w=[bST])
            V(lambda e: e.tensor_tensor(out=ST[:, 2, :], in0=ST[:, 1, :], in1=BEST[:, :, 0], op=ALU.add), r=[bST, bBEST], w=[bST])
            V(lambda e: e.tensor_scalar(out=ST[:, 3, :], in0=BEST[:, :, 15], scalar1=-1e-5, scalar2=None, op0=ALU.add), r=[bBEST], w=[bST])
            V(lambda e: e.tensor_tensor(out=C1T[:], in0=TOP1[:], in1=ST[:, 2, :].unsqueeze(2).to_broadcast([128, 8, 16]), op=ALU.subtract),
              r=[bTOP, bST], w=[bCT])
            V(lambda e: e.tensor_tensor(out=ST[:, 4, :], in0=ST[:, 3, :], in1=ST[:, 2, :], op=ALU.subtract), r=[bST], w=[bST])
            A(lambda e: e.activation(out=ST[:, 5, :], in_=ST[:, 4, :], func=AF.Exp), r=[bST], w=[bST])
            V(lambda e: e.tensor_copy(out=C2T[:], in_=ST[:, 5, :].unsqueeze(2).to_broadcast([128, 8, 16])), r=[bST], w=[bCT])
            for q3, SRC in enumerate((C1T, C2T, IDXF)):
                PE((lambda q3, SRC: lambda e: e.transpose(out=pbank[3][:, q3 * 128:(q3 + 1) * 128],
                                                          in_=SRC[:].rearrange("p h a -> p (h a)"), identity=IDF))(q3, SRC),
                   r=[bCT, bIDX, bCST], w=[PB[3]])
            V(lambda e: e.tensor_copy(out=CT3[:].rearrange("p c t -> p (c t)"), in_=pbank[3][:, 0:384]), r=[PB[3]], w=[bCT3])
            def wb_bc(idx):
                half, g = divmod(idx, 16)
                pp = idx % 2
                bb = 4 + pp
                if g == 0:
                    A((lambda half: lambda e: e.activation(
                        out=QREP.rearrange("p t (h a) -> p t h a", h=8),
                        in_=QTP[64:128, :, 64 * half:64 * half + 64].rearrange("p h t -> p t h").unsqueeze(3).to_broadcast([64, 64, 8, 16]),
                        func=AF.Copy))(half), r=[bQTP] + bigB, w=bigB)
                for j in range(4):
                    tl = g * 4 + j
                    PE(mm(pbank[bb][:, j * 128:(j + 1) * 128], QREP[:, tl, :], KEYB[64:128, 128:256], True, True), r=bigB + [bKEY], w=[PB[bb]])

            def wb_ew(idx):
                half, g = divmod(idx, 16)
                pp = idx % 2
                bb = 4 + pp
                for j in range(4):
                    t = 64 * half + g * 4 + j
                    bcj = pbank[bb][:, j * 128:(j + 1) * 128]
                    A((lambda pp, j, t, bcj: lambda e: e.activation(out=EXB[pp][:, j, :], in_=bcj, func=AF.Exp,
                                                                    bias=CT3[:, 0, t:t + 1]))(pp, j, t, bcj),
                      r=[PB[bb], bCT3, bEXB[pp]], w=[bEXj[pp][j]])
                    V((lambda pp, j, t: lambda e: e.scalar_tensor_tensor(
                        out=RB[pp][:, j, :], in0=EXB[pp][:, j, :], scalar=CT3[:, 1, t:t + 1], in1=EXB[pp][:, j, :],
                        op0=ALU.is_ge, op1=ALU.mult))(pp, j, t), r=[bCT3, bEXj[pp][j], bRB[pp]], w=[bRj[pp][j]])
                    V((lambda pp, j, t: lambda e: e.tensor_scalar(out=E1B[pp][:, j, :], in0=IOTA, scalar1=CT3[:, 2, t:t + 1],
                                                                  scalar2=None, op0=ALU.is_equal))(pp, j, t),
                      r=[bCST, bCT3, bE1B[pp]], w=[bE1j[pp][j]])

            def wb_wt(idx):
                half, g = divmod(idx, 16)
                pp = idx % 2
                wb = 6 + pp
                for j in range(4):
                    PE(mm(pbank[wb][:, j * 128:(j + 1) * 128], RB[pp][:, j, :], E1B[pp][:, j, :], True, True),
                       r=[bRj[pp][j], bE1j[pp][j]], w=[PB[wb]])
                t0 = 64 * half + 4 * g
                A((lambda wb, t0: lambda e: e.activation(out=GWT[:, :, t0:t0 + 4],
                                                         in_=pbank[wb][:, :].rearrange("p (t i) -> p i t", t=4),
                                                         func=AF.Copy))(wb, t0), r=[PB[wb]], w=bigA)
            wb_bc(0)
            wb_bc(1)
            wb_ew(0)
            for idx in range(32):
                if idx + 1 < 32:
                    wb_ew(idx + 1)
                wb_wt(idx)
                if idx + 2 < 32:
                    wb_bc(idx + 2)

            slots = {}
            KRb = KRf[:].rearrange("p a b c d -> p (a b c d)").bitcast(BF16)
            BKb = BKf[:].rearrange("p a b c d -> p (a b c d)").bitcast(BF16)
            SLOTAP = [RING[:, k, :] for k in range(NS)] + [KRb[:, j * 1024:(j + 1) * 1024] for j in range(4)] \
                + [BKb[:, j * 1024:(j + 1) * 1024] for j in range(4)]
            bSLOT = list(bRING) + bXR
            lease = [[] for _ in range(NS)] + [[bKR]] * 4 + [[bBK]] * 4

            def dload(src_ap, deps):
                k = dcount[0] % len(SLOTAP)
                dcount[0] += 1
                DMA(lambda e: e.dma_start(out=SLOTAP[k], in_=src_ap), r=list(deps) + lease[k], w=[bSLOT[k]])
                return k

            def de_at(ec):
                su = dload(ut_s[:, ec, :], [bUT[ec // 16]])
                sv_ = dload(pv_s[:, ec, :], [bPV[ec // 16]])
                slots[ec] = sv_
                ab = 2 + ec % 2
                for kc in range(8):
                    PE(mm(pbank[ab][:, 0:128], SLOTAP[su][:, kc * 128:(kc + 1) * 128], H2T[:, kc, :], kc == 0, kc == 7),
                       r=[bSLOT[su], bH2T], w=[PB[ab]])

            def de_ew(ec):
                pp = ec % 2
                ab = 2 + pp
                A((lambda pp, ab: lambda e: e.activation(out=GB[pp][:], in_=pbank[ab][:, 0:128], func=AF.Gelu))(pp, ab),
                  r=[PB[ab]], w=[bGB_[pp]])
                V((lambda pp, ec: lambda e: e.tensor_tensor(out=GWB[pp][:], in0=GB[pp][:], in1=GWT[:, ec, :], op=ALU.mult))(pp, ec),
                  r=[bGB_[pp]] + bigA, w=[bGWB[pp]])

            def de_mm2(ec):
                pp = ec % 2
                sv_ = slots.pop(ec)
                for hf in range(2):
                    PE(mm(pbank[hf][:, :], GWB[pp][:], SLOTAP[sv_][:, hf * 512:(hf + 1) * 512], ec == 0, ec == 127),
                       r=[bGWB[pp], bSLOT[sv_]], w=[PB[hf]])
            de_at(0)
            for ec in range(128):
                if ec + 1 < 128:
                    de_at(ec + 1)
                de_ew(ec)
                de_mm2(ec)
            for hf in range(2):
                V((lambda hf: lambda e: e.tensor_tensor(out=X[:, 512 * hf:512 * hf + 512], in0=X[:, 512 * hf:512 * hf + 512],
                                                        in1=pbank[hf][:, :], op=ALU.add))(hf), r=[PB[hf], bX], w=[bX])

        nchunk = [0]
        try:
          ckpt(0)
          for sq in DBG_SEQS:
            for t in range(NTP):
                chunks = []
                for k in range(2):
                    ci = 2 * t + k
                    g = nchunk[0]
                    nchunk[0] += 1
                    c = dict(xsrc=xp[sq, 64 * ci:64 * ci + 64, :], yout=y_p[sq, 64 * ci:64 * ci + 64, :])
                    c["prev"] = "zero" if ci == 0 else ("carry" if k == 0 else "own")
                    c["state"] = "zero" if ci == 0 else "keep"
                    c["kslots"] = [((g - 2) % 6) if ci >= 2 else None, ((g - 1) % 6) if ci >= 1 else None, g % 6]
                    if t == NTP - 1:
                        c["k_out"] = k_p[sq, 64 * k:64 * k + 64, :]
                        c["v_out"] = v_p[sq, 64 * k:64 * k + 64, :]
                        if k == 1:
                            c["shift_out"] = shift_p[sq:sq + 1, :]
                            c["wkv_out"] = wkv_p[sq]
                    chunks.append(c)
                do_tile(chunks)
          g0 = nchunk[0]
          slots = [(g0 + i) % 6 for i in range(6)]
          CKF = sb("CKF", [128, 128]); bCKF = Buf()
          CVF = VW(SOUT[:, 0:256].rearrange("p (c f) -> p c f", c=2)); bCVF = bSOUT
          chunks = []
          for k in range(2):
              sA, sB, sC = slots[3 * k], slots[3 * k + 1], slots[3 * k + 2]
              DMA((lambda k: lambda e: e.dma_start(out=CKF[:], in_=ck[k]))(k), w=[bCKF])
              for kvh in range(2):
                  PE((lambda kvh: lambda e: e.transpose(out=pbank[7][0:64, kvh * 128:(kvh + 1) * 128], in_=CKF[:, kvh * 64:(kvh + 1) * 64],
                                                        identity=IDF))(kvh), r=[bCKF, bCST], w=[PB[7]])
              for j, sl in enumerate((sA, sB)):
                  V((lambda j, sl: lambda e: e.tensor_copy(
                      out=KTR[:, :, sl, :], in_=pbank[7][0:64, 0:256].rearrange("p (v c t) -> p v c t", v=2, c=2)[:, :, j, :]))(j, sl),
                    r=[PB[7]], w=[bKTR[sl]])
              DMA((lambda k: lambda e: e.dma_start(out=CVF[:], in_=cv[k].rearrange("(c p) f -> p c f", p=64)))(k), w=[bCVF])
              for j, sl in enumerate((sA, sB)):
                  V((lambda j, sl: lambda e: e.tensor_copy(out=VAR[:, sl, :], in_=CVF[:, j, :]))(j, sl), r=[bCVF], w=[bVAR[sl]])
              c = dict(xsrc=xs[k], yout=y_s[k], prev=("state", k), state=("load", k), kslots=[sA, sB, sC],
                       k_out=k_s[k], v_out=v_s[k], shift_out=shift_s[k:k + 1, :], wkv_out=wkv_s[k])
              chunks.append(c)
          do_tile(chunks)
        except _Stop:
            pass
        S.emit()
    return nc, S


def host_layout(inp, do_peer=True):
    f = lambda a: np.ascontiguousarray(a, dtype=np.float32)
    w_in = inp["w_in"][0]
    sh = {}
    sh["win_f"] = f(w_in.reshape(8, 128, 36, 128).transpose(1, 2, 0, 3).reshape(128, 36, 1024))
    mu = inp["mu_shift"][0]
    sh["mu_row"] = f(mu)
    par = np.zeros((128, NPAR), np.float32)
    hj = lambda v: np.asarray(v).reshape(8, 64).T
    par[0:64, 27:35] = hj(inp["k_k"][0]); par[0:64, 35:43] = hj(inp["k_a"][0])
    par[0:64, 59:67] = hj(inp["r_k"][0].reshape(512))
    sh["par"] = par
    sh["cst"] = make_consts()
    sh["n1g"] = f(inp["norm1_g"][0]); sh["n2g"] = f(inp["norm2_g"][0]); sh["fg"] = f(inp["final_g"])
    sh["w0row"] = f(inp["w_decay0"][0].reshape(1, 512)); sh["a0row"] = f(inp["a_icl0"][0].reshape(1, 512))
    sh["lw"] = f(inp["w_decay_lora"][0]); sh["la"] = f(inp["a_icl_lora"][0]); sh["gl"] = f(inp["g_lora"][0])
    sh["lnw"] = f(inp["lnx_w"][0]); sh["lnb"] = f(inp["lnx_b"][0]); sh["sinks"] = f(inp["attn_sinks"][0])
    wb = inp["w_branch"][0]
    sh["wb0_f"] = f(wb[0].reshape(4, 128, 4, 2, 128).transpose(1, 2, 0, 3, 4).reshape(128, 4, 1024))
    sh["wb1_f"] = f(wb[1].reshape(8, 64, 8, 128).transpose(1, 2, 0, 3).reshape(64, 8, 1024))
    sh["wo_f"] = f(inp["w_out"][0].reshape(8, 128, 8, 128).transpose(1, 2, 0, 3).reshape(128, 8, 1024))
    if do_peer:
        sh["wq_f"] = f(inp["peer_wq"][0].reshape(8, 128, 8, 128).transpose(1, 2, 0, 3).reshape(128, 8, 1024))
        sk = inp["peer_sub_keys"][0]
        keys = np.zeros((128, 256), np.float32)
        keys[0:64, 0:128] = sk[0].T
        keys[64:128, 128:256] = sk[1].T
        sh["keys_f"] = keys
        sh["ut_f"] = f(inp["peer_u"][0].reshape(128, 128, 8, 128).transpose(3, 0, 2, 1).reshape(128, 128, 1024))
        sh["pv_f"] = f(inp["peer_v"][0].reshape(128, 128, 1024).transpose(1, 0, 2))
    return sh


def core_inputs(inp, c, sh):
    f = lambda a: np.ascontiguousarray(a, dtype=np.float32)
    m = dict(sh)
    m["xp"] = f(inp["x_prompt"][2 * c:2 * c + 2]); m["xs"] = f(inp["x_sample"][2 * c:2 * c + 2])
    m["st_shift"] = f(inp["state_shift"][0, 2 * c:2 * c + 2]); m["st_wkv"] = f(inp["state_wkv"][0, 2 * c:2 * c + 2])
    m["ck"] = f(inp["cache_k"][0, 2 * c:2 * c + 2].reshape(2, 128, 128)); m["cv"] = f(inp["cache_v"][0, 2 * c:2 * c + 2].reshape(2, 128, 128))
    return m


DO_PEER = True
_CACHE = {}


def kernel(**inp):
    B, SEQ, _ = inp["x_prompt"].shape
    NTP = SEQ // 128
    key = (NTP, DO_PEER)
    if key not in _CACHE:
        _CACHE[key] = build(NTP, DO_PEER)[0]
    nc = _CACHE[key]
    sh = host_layout(inp, DO_PEER)
    in_maps = [core_inputs(inp, c, sh) for c in range(8)]
    res = run_bass_kernel_spmd(nc, in_maps, core_ids=list(range(8)))
    R = res.results
    cat = lambda k: np.concatenate([np.asarray(r[k]) for r in R], axis=0)
    y_p = cat("y_p"); y_s = cat("y_s")
    return (y_p, y_s, cat("shift_p")[None], cat("wkv_p")[None],
            cat("k_p").reshape(1, 16, 128, 2, 64), cat("v_p").reshape(1, 16, 128, 2, 64),
            cat("shift_s")[None], cat("wkv_s")[None],
            cat("k_s").reshape(1, 16, 64, 2, 64), cat("v_s").reshape(1, 16, 64, 2, 64))
```
